# Optimizing a Trainium2 kernel written in Bass

```python
import math
import jax, jax.numpy as jnp
from jax import lax
import numpy as np

D_MODEL = 2048
BATCH = 1
SEQ = 8192
DEPTH = 4

EPS = 1e-6
N_BRANCH = 4
BRANCH_WIDTH = 512
A_GROUPS = ((128, 1), (512, 4), (2048, 16))
A_HEADS_PER_GROUP = 4
A_HEAD_DIM = 128
A_HEADS = 12
A_WIDTH = A_HEADS_PER_GROUP * A_HEAD_DIM
B_HEADS = 4
B_NOPE = 128
B_ROPE = 64
B_VDIM = 128
B_Q_LORA = 448
B_KV_LORA = 128
B_Q_BLOCK = 128
B_WIDTH = B_HEADS * B_VDIM
ROPE_THETA = 10000.0
C_HEADS = 4
C_QK_DIM = 64
C_V_DIM = 128
C_WIDTH = C_HEADS * C_V_DIM
C_CONV = 4
C_CHUNK = 64
D_WIDTH = 512
D_GROUP = 16
D_STATE = 64
D_NGROUPS = D_WIDTH // D_GROUP
IN_SPLITS = (
    A_HEADS * A_HEAD_DIM, A_HEADS * A_HEAD_DIM, A_HEADS * A_HEAD_DIM, A_WIDTH,
    B_Q_LORA, B_KV_LORA, B_ROPE, B_WIDTH,
    2 * C_HEADS * C_QK_DIM, C_WIDTH, C_HEADS, C_HEADS, C_WIDTH, C_WIDTH,
    D_WIDTH, D_WIDTH,
    N_BRANCH * D_MODEL,
)
IN_WIDTH = sum(IN_SPLITS)

kernel_name = "hybrid_parallel_gated_mixers"


def _rmsnorm(x, g):
    x32 = x.astype(jnp.float32)
    y = x32 * lax.rsqrt(jnp.mean(x32 * x32, axis=-1, keepdims=True) + EPS)
    return (y * g.astype(jnp.float32)).astype(x.dtype)


def _alibi_slopes(n):
    return 2.0 ** (-8.0 * jnp.arange(1, n + 1, dtype=jnp.float32) / n)


def _dilated_window_attn(q, k, v, window, dil, slopes):
    Bsz, S, H, hd = q.shape
    L = S // dil
    blk = window // dil
    def to_strided(t):
        return t.astype(jnp.float32).reshape(Bsz, L, dil, H, hd).transpose(0, 3, 2, 1, 4)
    qs, ks, vs = to_strided(q), to_strided(k), to_strided(v)
    nb = -(-L // blk)
    pad = nb * blk - L
    qs = jnp.pad(qs, ((0, 0), (0, 0), (0, 0), (0, pad), (0, 0)))
    ks = jnp.pad(ks, ((0, 0), (0, 0), (0, 0), (blk, pad), (0, 0)))
    vs = jnp.pad(vs, ((0, 0), (0, 0), (0, 0), (blk, pad), (0, 0)))
    qb = qs.reshape(Bsz, H, dil, nb, blk, hd)
    kb = ks.reshape(Bsz, H, dil, nb + 1, blk, hd)
    vb = vs.reshape(Bsz, H, dil, nb + 1, blk, hd)
    kw = jnp.concatenate([kb[:, :, :, :-1], kb[:, :, :, 1:]], axis=-2)
    vw = jnp.concatenate([vb[:, :, :, :-1], vb[:, :, :, 1:]], axis=-2)
    s = jnp.einsum('bhrnqd,bhrnkd->bhrnqk', qb, kw) * (hd ** -0.5)
    qi = jnp.arange(blk)[:, None]
    ki = jnp.arange(2 * blk)[None, :]
    delta = qi - ki + blk
    band = (delta >= 0) & (delta <= blk)
    key_abs = jnp.arange(nb)[:, None, None] * blk + ki[None] - blk
    mask = band[None] & (key_abs >= 0)
    bias = -slopes.astype(jnp.float32)[:, None, None] * (delta * dil).astype(jnp.float32)[None]
    s = s + bias[None, :, None, None]
    s = jnp.where(mask, s, -jnp.inf)
    m = jnp.max(s, axis=-1, keepdims=True)
    p = jnp.exp(s - m)
    den = jnp.sum(p, axis=-1, keepdims=True)
    o = jnp.einsum('bhrnqk,bhrnkd->bhrnqd', p, vw) / den
    lse = (m + jnp.log(den))[..., 0]
    o = o.reshape(Bsz, H, dil, nb * blk, hd)[:, :, :, :L].transpose(0, 3, 2, 1, 4).reshape(Bsz, S, H, hd)
    lse = lse.reshape(Bsz, H, dil, nb * blk)[:, :, :, :L].transpose(0, 3, 2, 1).reshape(Bsz, S, H)
    return o, lse


def _branch_dilated(q, k, v, qn_g, kn_g):
    Bsz, S, _ = q.shape
    ng = len(A_GROUPS)
    shp = (Bsz, S, ng, A_HEADS_PER_GROUP, A_HEAD_DIM)
    qh = _rmsnorm(q.reshape(shp), qn_g)
    kh = _rmsnorm(k.reshape(shp), kn_g)
    vh = v.reshape(shp)
    slopes = _alibi_slopes(A_HEADS).reshape(ng, A_HEADS_PER_GROUP)
    outs, lses = [], []
    for g, (window, dil) in enumerate(A_GROUPS):
        o, l = _dilated_window_attn(qh[:, :, g], kh[:, :, g], vh[:, :, g], window, dil, slopes[g])
        outs.append(o)
        lses.append(l)
    o_all = jnp.stack(outs)
    w = jax.nn.softmax(jnp.stack(lses), axis=0)
    out = jnp.sum(w[..., None] * o_all, axis=0)
    return out.reshape(Bsz, S, A_WIDTH).astype(q.dtype)


def _rope(x, cos, sin):
    half = x.shape[-1] // 2
    x1, x2 = x[..., :half], x[..., half:]
    c, s = cos[:, :, None, :], sin[:, :, None, :]
    return jnp.concatenate([x1 * c - x2 * s, x2 * c + x1 * s], axis=-1)


def _branch_mla(c_q, c_kv, k_r, positions, cq_g, ckv_g, w_uq, w_ukv, qn_g, kn_g):
    Bsz, S, _ = c_q.shape
    H = B_HEADS
    q = (_rmsnorm(c_q, cq_g) @ w_uq).reshape(Bsz, S, H, B_NOPE + B_ROPE)
    kv = (_rmsnorm(c_kv, ckv_g) @ w_ukv).reshape(Bsz, S, H, B_NOPE + B_VDIM)
    k_nope, v = kv[..., :B_NOPE], kv[..., B_NOPE:]
    k = jnp.concatenate([k_nope, jnp.broadcast_to(k_r[:, :, None, :], (Bsz, S, H, B_ROPE))], axis=-1)
    q = _rmsnorm(q, qn_g)
    k = _rmsnorm(k, kn_g)
    inv = ROPE_THETA ** (-jnp.arange(0, B_ROPE, 2, dtype=jnp.float32) / B_ROPE)
    ang = positions.astype(jnp.float32)[..., None] * inv
    cos, sin = jnp.cos(ang), jnp.sin(ang)
    q = jnp.concatenate([q[..., :B_NOPE], _rope(q[..., B_NOPE:].astype(jnp.float32), cos, sin).astype(q.dtype)], axis=-1)
    k = jnp.concatenate([k[..., :B_NOPE], _rope(k[..., B_NOPE:].astype(jnp.float32), cos, sin).astype(k.dtype)], axis=-1)
    dq = B_NOPE + B_ROPE
    nb = S // B_Q_BLOCK
    qb = q.reshape(Bsz, nb, B_Q_BLOCK, H, dq).transpose(1, 0, 3, 2, 4)
    kt = k.transpose(0, 2, 1, 3)
    vt = v.transpose(0, 2, 1, 3)
    kpos = jnp.arange(S)
    scale = dq ** -0.5
    def blk_fn(args):
        qblk, i = args
        s = jnp.einsum('bhqd,bhkd->bhqk', qblk, kt).astype(jnp.float32) * scale
        qpos = i * B_Q_BLOCK + jnp.arange(B_Q_BLOCK)
        s = jnp.where(kpos[None, :] <= qpos[:, None], s, -jnp.inf)
        p = jax.nn.softmax(s, axis=-1)
        return jnp.einsum('bhqk,bhkd->bhqd', p.astype(vt.dtype), vt)
    o = lax.map(blk_fn, (qb, jnp.arange(nb)))
    return o.transpose(1, 0, 3, 2, 4).reshape(Bsz, S, B_WIDTH).astype(c_q.dtype)


def _causal_conv(x, w, b):
    K, C = w.shape
    y = lax.conv_general_dilated(x, w[:, None, :], window_strides=(1,), padding=[(K - 1, 0)],
                                 dimension_numbers=('NWC', 'WIO', 'NWC'), feature_group_count=C)
    return y + b


def _branch_mlstm(qk, v, i_pre, f_pre, o_pre, conv_w, conv_b, i_b, f_b):
    Bsz, S, _ = qk.shape
    H, dk, dv, L = C_HEADS, C_QK_DIM, C_V_DIM, C_CHUNK
    qk = jax.nn.silu(_causal_conv(qk, conv_w, conv_b)).astype(jnp.float32)
    q = qk[..., :H * dk].reshape(Bsz, S, H, dk)
    k = qk[..., H * dk:].reshape(Bsz, S, H, dk) * (dk ** -0.5)
    vv = v.astype(jnp.float32).reshape(Bsz, S, H, dv)
    ig = (i_pre + i_b).astype(jnp.float32)
    lf = jax.nn.log_sigmoid((f_pre + f_b).astype(jnp.float32))
    nc = S // L
    def chunks(t):
        t = t.reshape((Bsz, nc, L) + t.shape[2:])
        return jnp.moveaxis(jnp.moveaxis(t, 1, 0), 3, 2)
    qc, kc, vc, igc, lfc = chunks(q), chunks(k), chunks(vv), chunks(ig), chunks(lf)
    tril = jnp.tril(jnp.ones((L, L), dtype=bool))
    def step(carry, xs):
        C, n, m = carry
        qt, kt, vt, it, ft = xs
        b = jnp.cumsum(ft, axis=-1)
        Dm = jnp.where(tril, b[..., :, None] - b[..., None, :] + it[..., None, :], -jnp.inf)
        inter = b + m[..., None]
        mt = jnp.maximum(inter, jnp.max(Dm, axis=-1))
        wD = jnp.exp(Dm - mt[..., None])
        wi = jnp.exp(inter - mt)
        sqk = wD * jnp.einsum('bhtd,bhsd->bhts', qt, kt)
        num = wi[..., None] * jnp.einsum('bhvd,bhtd->bhtv', C, qt) + jnp.einsum('bhts,bhsv->bhtv', sqk, vt)
        den = wi * jnp.einsum('bhd,bhtd->bht', n, qt) + jnp.sum(sqk, axis=-1)
        h = num / jnp.maximum(jnp.abs(den), jnp.exp(-mt))[..., None]
        bL = b[..., -1]
        gs = bL[..., None] - b + it
        m_new = jnp.maximum(bL + m, jnp.max(gs, axis=-1))
        decay = jnp.exp(bL + m - m_new)
        ws = jnp.exp(gs - m_new[..., None])
        C_new = decay[..., None, None] * C + jnp.einsum('bhs,bhsv,bhsd->bhvd', ws, vt, kt)
        n_new = decay[..., None] * n + jnp.einsum('bhs,bhsd->bhd', ws, kt)
        return (C_new, n_new, m_new), h
    init = (jnp.zeros((Bsz, H, dv, dk), jnp.float32), jnp.zeros((Bsz, H, dk), jnp.float32),
            jnp.zeros((Bsz, H), jnp.float32))
    _, hs = lax.scan(step, init, (qc, kc, vc, igc, lfc))
    h = hs.transpose(1, 0, 3, 2, 4).reshape(Bsz, S, C_WIDTH)
    return (jax.nn.sigmoid(o_pre.astype(jnp.float32)) * h).astype(v.dtype)


def _complex_affine_combine(e1, e2):
    a1r, a1i, b1r, b1i = e1
    a2r, a2i, b2r, b2i = e2
    return (a2r * a1r - a2i * a1i, a2r * a1i + a2i * a1r,
            a2r * b1r - a2i * b1i + b2r, a2r * b1i + a2i * b1r + b2i)


def _branch_s5(u, lam_re, lam_im, log_dt, b_re, b_im, c_re, c_im, d_skip, glu_w, glu_b):
    Bsz, S, _ = u.shape
    u32 = u.astype(jnp.float32).reshape(Bsz, S, D_NGROUPS, D_GROUP)
    lr, li = lam_re.astype(jnp.float32), lam_im.astype(jnp.float32)
    dt = jnp.exp(log_dt.astype(jnp.float32))[:, None]
    mag = jnp.exp(lr * dt)
    a_re, a_im = mag * jnp.cos(li * dt), mag * jnp.sin(li * dt)
    den = lr * lr + li * li
    f_re = ((a_re - 1.0) * lr + a_im * li) / den
    f_im = (a_im * lr - (a_re - 1.0) * li) / den
    br, bi = b_re.astype(jnp.float32), b_im.astype(jnp.float32)
    bb_re = f_re[..., None] * br - f_im[..., None] * bi
    bb_im = f_re[..., None] * bi + f_im[..., None] * br
    bu_re = jnp.einsum('bsgc,gpc->bsgp', u32, bb_re)
    bu_im = jnp.einsum('bsgc,gpc->bsgp', u32, bb_im)
    shp = bu_re.shape
    elems = (jnp.broadcast_to(a_re, shp), jnp.broadcast_to(a_im, shp), bu_re, bu_im)
    _, _, x_re, x_im = lax.associative_scan(_complex_affine_combine, elems, axis=1)
    y = (jnp.einsum('bsgp,gcp->bsgc', x_re, c_re.astype(jnp.float32))
         - jnp.einsum('bsgp,gcp->bsgc', x_im, c_im.astype(jnp.float32))
         + d_skip.astype(jnp.float32).reshape(D_NGROUPS, D_GROUP) * u32)
    y = jax.nn.gelu(y.reshape(Bsz, S, D_WIDTH))
    y = y * jax.nn.sigmoid(y @ glu_w.astype(jnp.float32) + glu_b.astype(jnp.float32))
    return y.astype(u.dtype)


def _layer(x, positions, norm_g, w_in, a_qn_g, a_kn_g, b_cq_g, b_ckv_g, b_w_uq, b_w_ukv, b_qn_g, b_kn_g,
           c_conv_w, c_conv_b, c_i_b, c_f_b, d_lam_re, d_lam_im, d_log_dt, d_b_re, d_b_im, d_c_re, d_c_im,
           d_skip, d_glu_w, d_glu_b, w_up, merge_b, w_out):
    Bsz, S, _ = x.shape
    h = _rmsnorm(x, norm_g)
    proj = h @ w_in
    idx = [int(i) for i in np.cumsum(IN_SPLITS)[:-1]]
    (a_q, a_k, a_v, a_z, b_cq, b_ckv, b_kr, b_z, c_qk, c_v, c_i, c_f, c_o, c_z,
     d_u, d_z, gate) = jnp.split(proj, idx, axis=-1)
    ya = _branch_dilated(a_q, a_k, a_v, a_qn_g, a_kn_g) * jax.nn.silu(a_z)
    yb = _branch_mla(b_cq, b_ckv, b_kr, positions, b_cq_g, b_ckv_g, b_w_uq, b_w_ukv, b_qn_g, b_kn_g) * jax.nn.silu(b_z)
    yc = _branch_mlstm(c_qk, c_v, c_i, c_f, c_o, c_conv_w, c_conv_b, c_i_b, c_f_b) * jax.nn.silu(c_z)
    yd = _branch_s5(d_u, d_lam_re, d_lam_im, d_log_dt, d_b_re, d_b_im, d_c_re, d_c_im,
                    d_skip, d_glu_w, d_glu_b) * jax.nn.silu(d_z)
    ys = jnp.stack([ya, yb, yc, yd], axis=2)
    up = jnp.einsum('bsnw,nwd->bsnd', ys, w_up)
    gates = jax.nn.sigmoid((gate + merge_b).astype(jnp.float32)).reshape(Bsz, S, N_BRANCH, D_MODEL)
    merged = jnp.sum(gates * up.astype(jnp.float32), axis=2).astype(x.dtype)
    return x + merged @ w_out


def setup_inputs(seed: int = 0) -> dict:
    key = jax.random.key(seed)
    ks = jax.random.split(key, 32)
    nrm = jax.random.normal
    G, P = D_NGROUPS, D_STATE
    f32 = jnp.float32
    x = nrm(ks[0], (BATCH, SEQ, D_MODEL), f32)
    positions = jnp.broadcast_to(jnp.arange(SEQ, dtype=jnp.int32), (BATCH, SEQ))
    norm_g = 1.0 + 0.02 * nrm(ks[1], (DEPTH, D_MODEL), f32)
    w_in = nrm(ks[2], (DEPTH, D_MODEL, IN_WIDTH), f32) * D_MODEL ** -0.5
    a_qn_g = 1.0 + 0.02 * nrm(ks[3], (DEPTH, A_HEAD_DIM), f32)
    a_kn_g = 1.0 + 0.02 * nrm(ks[4], (DEPTH, A_HEAD_DIM), f32)
    b_cq_g = 1.0 + 0.02 * nrm(ks[5], (DEPTH, B_Q_LORA), f32)
    b_ckv_g = 1.0 + 0.02 * nrm(ks[6], (DEPTH, B_KV_LORA), f32)
    b_w_uq = nrm(ks[7], (DEPTH, B_Q_LORA, B_HEADS * (B_NOPE + B_ROPE)), f32) * B_Q_LORA ** -0.5
    b_w_ukv = nrm(ks[8], (DEPTH, B_KV_LORA, B_HEADS * (B_NOPE + B_VDIM)), f32) * B_KV_LORA ** -0.5
    b_qn_g = 1.0 + 0.02 * nrm(ks[9], (DEPTH, B_NOPE + B_ROPE), f32)
    b_kn_g = 1.0 + 0.02 * nrm(ks[10], (DEPTH, B_NOPE + B_ROPE), f32)
    c_conv_w = nrm(ks[11], (DEPTH, C_CONV, 2 * C_HEADS * C_QK_DIM), f32) * C_CONV ** -0.5
    c_conv_b = 0.01 * nrm(ks[12], (DEPTH, 2 * C_HEADS * C_QK_DIM), f32)
    c_i_b = -1.0 + 0.1 * nrm(ks[13], (DEPTH, C_HEADS), f32)
    c_f_b = jnp.linspace(3.0, 6.0, C_HEADS, dtype=f32)[None] + 0.1 * nrm(ks[14], (DEPTH, C_HEADS), f32)
    d_lam_re = -0.5 + 0.01 * nrm(ks[15], (DEPTH, G, P), f32)
    d_lam_im = math.pi * jnp.arange(P, dtype=f32)[None, None] + 0.01 * nrm(ks[16], (DEPTH, G, P), f32)
    d_log_dt = jax.random.uniform(ks[17], (DEPTH, G), f32, math.log(1e-3), math.log(1e-1))
    b_scale = (2.0 * D_GROUP) ** -0.5
    d_b_re = nrm(ks[18], (DEPTH, G, P, D_GROUP), f32) * b_scale
    d_b_im = nrm(ks[19], (DEPTH, G, P, D_GROUP), f32) * b_scale
    c_scale = (2.0 * P) ** -0.5
    d_c_re = nrm(ks[20], (DEPTH, G, D_GROUP, P), f32) * c_scale
    d_c_im = nrm(ks[21], (DEPTH, G, D_GROUP, P), f32) * c_scale
    d_skip = 1.0 + 0.1 * nrm(ks[22], (DEPTH, D_WIDTH), f32)
    d_glu_w = nrm(ks[23], (DEPTH, D_WIDTH, D_WIDTH), f32) * D_WIDTH ** -0.5
    d_glu_b = 0.01 * nrm(ks[24], (DEPTH, D_WIDTH), f32)
    w_up = nrm(ks[25], (DEPTH, N_BRANCH, BRANCH_WIDTH, D_MODEL), f32) * BRANCH_WIDTH ** -0.5
    merge_b = 0.01 * nrm(ks[26], (DEPTH, N_BRANCH * D_MODEL), f32)
    w_out = nrm(ks[27], (DEPTH, D_MODEL, D_MODEL), f32) * D_MODEL ** -0.5
    return {"x": x, "positions": positions, "norm_g": norm_g, "w_in": w_in,
            "a_qn_g": a_qn_g, "a_kn_g": a_kn_g, "b_cq_g": b_cq_g, "b_ckv_g": b_ckv_g,
            "b_w_uq": b_w_uq, "b_w_ukv": b_w_ukv, "b_qn_g": b_qn_g, "b_kn_g": b_kn_g,
            "c_conv_w": c_conv_w, "c_conv_b": c_conv_b, "c_i_b": c_i_b, "c_f_b": c_f_b,
            "d_lam_re": d_lam_re, "d_lam_im": d_lam_im, "d_log_dt": d_log_dt,
            "d_b_re": d_b_re, "d_b_im": d_b_im, "d_c_re": d_c_re, "d_c_im": d_c_im,
            "d_skip": d_skip, "d_glu_w": d_glu_w, "d_glu_b": d_glu_b,
            "w_up": w_up, "merge_b": merge_b, "w_out": w_out}


def reference(x, positions, norm_g, w_in, a_qn_g, a_kn_g, b_cq_g, b_ckv_g, b_w_uq, b_w_ukv, b_qn_g, b_kn_g,
              c_conv_w, c_conv_b, c_i_b, c_f_b, d_lam_re, d_lam_im, d_log_dt, d_b_re, d_b_im, d_c_re, d_c_im,
              d_skip, d_glu_w, d_glu_b, w_up, merge_b, w_out):
    for l in range(DEPTH):
        x = _layer(x, positions, norm_g[l], w_in[l], a_qn_g[l], a_kn_g[l], b_cq_g[l], b_ckv_g[l],
                   b_w_uq[l], b_w_ukv[l], b_qn_g[l], b_kn_g[l], c_conv_w[l], c_conv_b[l], c_i_b[l], c_f_b[l],
                   d_lam_re[l], d_lam_im[l], d_log_dt[l], d_b_re[l], d_b_im[l], d_c_re[l], d_c_im[l],
                   d_skip[l], d_glu_w[l], d_glu_b[l], w_up[l], merge_b[l], w_out[l])
    return x
```

```python
import contextlib
import numpy as np
import concourse.bass as bass
import concourse.mybir as mybir
from concourse.bass_utils import run_bass_kernel_spmd

F32 = mybir.dt.float32
BF16 = mybir.dt.bfloat16
I32 = mybir.dt.int32
AF = mybir.ActivationFunctionType
ALU = mybir.AluOpType
AX = mybir.AxisListType
EPS = 1e-6
NCORES = 8


class Sched:
    CE = ('pe', 'act', 'dve', 'pool', 'sp')
    KDMA = 8

    def __init__(self):
        self.ops = {e: [] for e in self.CE}
        self.last_w = {}
        self.readers = {}
        self.waited = {e: {} for e in self.CE}
        self.ndma = {e: 0 for e in self.CE}
        self.fragile = set()

    def _deps(self, E, reads, writes):
        toks = []
        for b in reads:
            t = self.last_w.get(b)
            if t is not None:
                toks.append(t)
            if b.startswith("ps"):
                toks.extend(v for kk, v in self.readers.get(b, {}).items() if kk != E)
        for b in writes:
            t = self.last_w.get(b)
            if t is not None:
                toks.append(t)
            toks.extend(self.readers.get(b, {}).values())
        out = []
        for (key, val) in toks:
            if key == E and E == 'pe':
                continue
            if self.waited[E].get(key, -1) >= val:
                continue
            self.waited[E][key] = val
            out.append((key, val))
        return out

    def _commit(self, tok, reads, writes):
        for b in reads:
            d = self.readers.setdefault(b, {})
            k = tok[0]
            if k not in d or d[k][1] < tok[1]:
                d[k] = tok
        for b in writes:
            self.last_w[b] = tok
            self.readers[b] = {}

    def op(self, E, fn, reads=(), writes=(), fragile=False):
        waits = self._deps(E, reads, writes)
        idx = len(self.ops[E])
        self.ops[E].append(dict(fn=fn, waits=waits, dma=None, signal=False))
        if fragile and E != 'pe':
            self.fragile.add((E, idx))
        self._commit((E, idx), reads, writes)

    def dma(self, Q, fn, reads=(), writes=()):
        n = self.ndma[Q]
        self.ndma[Q] += 1
        slot = n % self.KDMA
        key = ('dma', Q, slot)
        waits = self._deps(Q, reads, writes)
        if n >= self.KDMA:
            v = 16 * (n // self.KDMA)
            if self.waited[Q].get(key, -1) < v:
                self.waited[Q][key] = v
                waits.append((key, v))
        self.ops[Q].append(dict(fn=fn, waits=waits, dma=key, signal=False))
        self._commit((key, 16 * (n // self.KDMA + 1)), reads, writes)

    def final_wait(self, E, bufs=()):
        waits = []
        for q in self.CE:
            n = self.ndma[q]
            for s in range(min(n, self.KDMA)):
                cnt = (n - s + self.KDMA - 1) // self.KDMA
                waits.append((('dma', q, s), 16 * cnt))
        self.ops[E].append(dict(fn=None, waits=waits, dma=None, signal=False))

    def emit(self, nc, st):
        for e in self.CE:
            for o in self.ops[e]:
                for (key, val) in o['waits']:
                    if isinstance(key, str):
                        self.ops[key][val]['signal'] = True
        rank = {}
        for e in self.CE:
            c = 0
            r = []
            for o in self.ops[e]:
                if o['signal'] and o['dma'] is None:
                    c += 1
                r.append(c)
            rank[e] = r
        sems = {e: st.enter_context(nc.semaphore("s_" + e)) for e in self.CE}
        for q in self.CE:
            if self.ndma[q]:
                for s in range(self.KDMA):
                    sems[('dma', q, s)] = st.enter_context(nc.semaphore("d_%s%d" % (q, s)))
        block = st.enter_context(nc.Block())

        def run(e):
            def body(eng):
                for o in self.ops[e]:
                    for (key, val) in o['waits']:
                        if isinstance(key, str):
                            eng.wait_ge(sems[key], rank[key][val])
                        else:
                            eng.wait_ge(sems[key], val)
                    if o['fn'] is None:
                        continue
                    inst = o['fn'](eng)
                    if o['dma'] is not None:
                        inst.then_inc(sems[o['dma']], 16)
                    elif o['signal']:
                        inst.then_inc(sems[e], 1)
            return body
        if self.ops['sp']:
            block.sync(run('sp'))
        if self.ops['pe']:
            block.tensor(run('pe'))
        if self.ops['act']:
            block.scalar(run('act'))
        if self.ops['dve']:
            block.vector(run('dve'))
        if self.ops['pool']:
            block.gpsimd(run('pool'))


class V:
    def __init__(self, ap, key):
        self.ap = ap
        self.key = key


def _ap(x):
    return x.ap if isinstance(x, V) else x


def _key(x):
    if isinstance(x, V):
        return list(x.key) if isinstance(x.key, (tuple, list)) else [x.key]
    return [x.tensor.name]


def _frag(out, accum=None):
    try:
        n = _ap(out).free_size()
    except Exception:
        n = 0
    return (n < 256) or (accum is not None)


class K:
    def __init__(self):
        self.nc = bass.Bass("TRN2", target_bir_lowering=False)
        self.S = Sched()
        self.st = contextlib.ExitStack()
        self.pools = {}
        self.outs = []
        self.psb = None
        self.psi = 0

    def din(self, name, shape, dt=F32):
        return self.nc.dram_tensor(name, list(shape), dt, kind="ExternalInput").ap()

    def dout(self, name, shape, dt=F32):
        ap = self.nc.dram_tensor(name, list(shape), dt, kind="ExternalOutput").ap()
        self.outs.append(name)
        return ap

    def sb(self, name, shape, dt=F32):
        return self.st.enter_context(self.nc.sbuf_tensor(name, list(shape), dt))

    def pool(self, name, shape, dt, n):
        if name not in self.pools:
            self.pools[name] = [[self.sb("%s_%d" % (name, i), shape, dt) for i in range(n)], 0]
        p = self.pools[name]
        t = p[0][p[1] % len(p[0])]
        p[1] += 1
        return t

    def psum(self, lo=0, hi=8):
        if self.psb is None:
            self.psb = [self.st.enter_context(self.nc.psum_tensor("ps%d" % i, [128, 512], F32)) for i in range(8)]
        t = self.psb[lo + self.psi % (hi - lo)]
        self.psi += 1
        return t

    def bank(self, i):
        self.psum(0, 8) if self.psb is None else None
        return self.psb[i]

    def dma(self, out, in_, q='sp', **kw):
        o, i = _ap(out), _ap(in_)
        self.S.dma(q, lambda e: e.dma_start(out=o, in_=i, **kw), reads=_key(in_), writes=_key(out))

    def mm(self, out, lhsT, rhs, start=True, stop=True, **kw):
        o, l, r = _ap(out), _ap(lhsT), _ap(rhs)
        self.S.op('pe', lambda e: e.matmul(o, lhsT=l, rhs=r, start=start, stop=stop, **kw),
                  reads=_key(lhsT) + _key(rhs) + ([] if start else _key(out)), writes=_key(out))

    def tr(self, out, in_, ident):
        o, i, d = _ap(out), _ap(in_), _ap(ident)
        self.S.op('pe', lambda e: e.transpose(o, i, d), reads=_key(in_) + _key(ident), writes=_key(out))

    def act(self, out, in_, func, bias=0.0, scale=1.0, accum_out=None, eng='act'):
        o, i = _ap(out), _ap(in_)
        rd = _key(in_)
        wr = _key(out)
        b = bias
        s = scale
        if not isinstance(bias, (int, float)):
            rd += _key(bias)
            b = _ap(bias)
        if not isinstance(scale, (int, float)):
            rd += _key(scale)
            s = _ap(scale)
        kw = {}
        if accum_out is not None:
            kw['accum_out'] = _ap(accum_out)
            wr += _key(accum_out)
        if func == AF.Copy:
            self.S.op(eng, lambda e: e.activation(out=o, in_=i, func=func, scale=s, **kw), reads=rd, writes=wr, fragile=_frag(out, accum_out))
        else:
            self.S.op(eng, lambda e: e.activation(out=o, in_=i, func=func, bias=b, scale=s, **kw), reads=rd, writes=wr, fragile=_frag(out, accum_out))

    def tt(self, out, in0, in1, op, eng='dve'):
        o, a, b = _ap(out), _ap(in0), _ap(in1)
        self.S.op(eng, lambda e: e.tensor_tensor(out=o, in0=a, in1=b, op=op),
                  reads=_key(in0) + _key(in1), writes=_key(out), fragile=_frag(out))

    def ts(self, out, in0, s1, op0, s2=None, op1=None, eng='dve', accum_out=None):
        o, a = _ap(out), _ap(in0)
        rd = _key(in0)
        wr = _key(out)
        x1, x2 = s1, s2
        if not isinstance(s1, (int, float)):
            rd += _key(s1)
            x1 = _ap(s1)
        if s2 is not None and not isinstance(s2, (int, float)):
            rd += _key(s2)
            x2 = _ap(s2)
        kw = {}
        if op1 is not None:
            kw['op1'] = op1
        if accum_out is not None:
            kw['accum_out'] = _ap(accum_out)
            wr += _key(accum_out)
        self.S.op(eng, lambda e: e.tensor_scalar(out=o, in0=a, scalar1=x1, scalar2=x2, op0=op0, **kw),
                  reads=rd, writes=wr, fragile=_frag(out, accum_out))

    def stt(self, out, in0, scalar, in1, op0, op1, eng='dve'):
        eng = 'dve'
        o, a, b = _ap(out), _ap(in0), _ap(in1)
        rd = _key(in0) + _key(in1)
        s = scalar
        if not isinstance(scalar, (int, float)):
            rd += _key(scalar)
            s = _ap(scalar)
        self.S.op(eng, lambda e: e.scalar_tensor_tensor(out=o, in0=a, scalar=s, in1=b, op0=op0, op1=op1),
                  reads=rd, writes=_key(out), fragile=_frag(out))

    def copy(self, out, in_, eng='dve'):
        o, i = _ap(out), _ap(in_)
        if eng == 'act':
            self.S.op('act', lambda e: e.copy(out=o, in_=i), reads=_key(in_), writes=_key(out), fragile=_frag(out))
        else:
            self.S.op(eng, lambda e: e.tensor_copy(out=o, in_=i), reads=_key(in_), writes=_key(out), fragile=_frag(out))

    def memset(self, out, val, eng='dve'):
        o = _ap(out)
        self.S.op(eng, lambda e: e.memset(o, val), reads=[], writes=_key(out), fragile=_frag(out))

    def recip(self, out, in_):
        o, i = _ap(out), _ap(in_)
        self.S.op('dve', lambda e: e.reciprocal(out=o, in_=i), reads=_key(in_), writes=_key(out), fragile=_frag(out))

    def scan(self, out, d0, d1, initial, op0, op1):
        o, a, b = _ap(out), _ap(d0), _ap(d1)
        rd = _key(d0) + _key(d1)
        ini = initial
        if not isinstance(initial, (int, float)):
            rd += _key(initial)
            ini = _ap(initial)
        self.S.op('dve', lambda e: e.tensor_tensor_scan(out=o, data0=a, data1=b, initial=ini, op0=op0, op1=op1),
                  reads=rd, writes=_key(out), fragile=_frag(out))

    def finish(self):
        rem = self.nc.sbuf_bytes_remaining
        rem = rem() if callable(rem) else rem
        used = 229376 - rem
        print("SBUF used per partition: %.1f KB" % (used / 1024.0))
        assert used <= 190 * 1024, "SBUF over budget (top of SBUF is reserved: DMA rings)"
        self.S.final_wait('sp')
        self.S.emit(self.nc, self.st)
        self.st.close()
        return self.nc


def run(nc, in_maps, trace=False):
    return run_bass_kernel_spmd(nc, in_maps, core_ids=list(range(len(in_maps))), trace=trace)


import math

T = 1024
NH = T // 512
OFF = dict(a_q=0, a_k=1536, a_v=3072, a_z=4608, b_cq=5120, b_ckv=5568, b_kr=5696, b_z=5760,
           c_qk=6272, c_v=6784, c_i=7296, c_f=7300, c_o=7304, c_z=7816, d_u=8328, d_z=8840, gate=9352)
IN_W = 17544
PGROUPS = [("a_q", 0, 1536), ("a_k", 1536, 1536), ("a_v", 3072, 1536), ("b_cq", 5120, 448), ("b_ckv", 5568, 128),
           ("b_kr", 5696, 64), ("c_qk", 6272, 512), ("c_v", 6784, 512), ("c_i", 7296, 4), ("c_f", 7300, 4), ("d_u", 8328, 512)]
PCOLS = []
POFF = {}
for (_n, _o, _w) in PGROUPS:
    POFF[_n] = len(PCOLS)
    PCOLS.extend(range(_o, _o + _w))
PW = len(PCOLS)
_uid = [0]


def uk(p):
    _uid[0] += 1
    return "%s#%d" % (p, _uid[0])


def load_w(k, w_ap, c0, ncols, nk=16, rows=None):
    wt = k.pool("wt", [128, 16, 512], BF16, 2)
    if rows is None:
        src = w_ap[:, c0:c0 + ncols].rearrange("(kt p) c -> p kt c", p=128)
        k.dma(wt[:, 0:nk, 0:ncols], src, q='pool')
    return wt


def hk(half):
    return tuple("hT%d" % tt for tt in range(half * 4, half * 4 + 4))


def norm_hT(k, x_ap, norm_g, ident, keep_x=None, ext=None):
    NT = T // 128
    eb = epsb(k)
    if ext is None:
        grep = k.sb("grep", [128, 2048], F32)[:]
    else:
        grep = ext[2]
    k.dma(grep, norm_g.partition_broadcast(128))
    hT = k.sb("hT", [128, 16, T], BF16)
    for tt in range(NT):
        if keep_x is not None:
            xt = V(keep_x[:, tt, :], "xs%d" % tt)
        elif ext is not None:
            xt = ext[0]
        else:
            xt = k.pool("xt", [128, 2048], F32, 1)[:]
        k.dma(xt, x_ap[tt * 128:(tt + 1) * 128, :])
        ss = k.pool("ss", [128, 1], F32, 2)
        xn = k.pool("xn", [128, 2048], BF16, 1)[:] if ext is None else ext[1]
        k.act(xn, xt, AF.Square, accum_out=ss[:])
        rstd = k.pool("rstd", [128, 1], F32, 2)
        ms = k.pool("ms", [128, 1], F32, 2)
        k.ts(ms[:], ss[:], 1.0 / 2048, ALU.mult, EPS, ALU.add)
        k.act(rstd[:], ms[:], AF.Ln)
        k.act(rstd[:], rstd[:], AF.Exp, scale=-0.5)
        if getattr(k, "dbg", None) is not None:
            k.dma(V(k.dbg[:, tt:tt + 1], uk("o")), ss[:], allow_slow_non_contiguous=True)
            k.dma(V(k.dbg[:, 8 + tt:9 + tt], uk("o")), rstd[:], allow_slow_non_contiguous=True)
        k.stt(xn, xt, rstd[:, 0:1], grep, ALU.mult, ALU.mult)
        xnk = _key(xn)
        for f4 in range(4):
            ps = k.psum()
            pb = ps[:].bitcast(BF16)
            for j in range(4):
                ft = f4 * 4 + j
                k.tr(V(pb[:, j * 128:(j + 1) * 128], ps.name), V(_ap(xn)[:, ft * 128:(ft + 1) * 128], xnk), ident[:])
            src = V(pb[:, 0:512].rearrange("p (j t) -> p j t", j=4), ps.name)
            dst = V(hT[:, f4 * 4:(f4 + 1) * 4, tt * 128:(tt + 1) * 128], "hT%d" % tt)
            k.copy(dst, src, eng=('act' if f4 % 2 == 0 else 'dve'))
    return hT


def epsb(k):
    if not hasattr(k, "_epsb"):
        k._epsb = k.sb("epsb", [128, 1], F32)
        k.memset(k._epsb[:], EPS)
    return k._epsb


def consts(k):
    idf = k.din("c_ident", [128, 128], F32)
    ident = k.sb("ident", [128, 128], BF16)
    k.dma(ident[:], idf, q='pool')
    ones = k.sb("ones", [128, 128], BF16)
    k.memset(ones[:], 1.0)
    return ident, ones


def rstd_from_ss(k, ps_ss, n, rows=128):
    r = k.pool("rs", [128, 512], F32, 2)
    k.act(r[0:rows, :], ps_ss[0:rows, :], AF.Ln, bias=epsb(k)[0:rows, 0:1], scale=1.0 / n)
    k.act(r[0:rows, :], r[0:rows, :], AF.Exp, scale=-0.5)
    return r


def proj_block(k, wt, wc0, nb, hT, half, nk=16):
    ps = k.psum()
    for kt in range(nk):
        k.mm(ps[0:nb, :], wt[:, kt, wc0:wc0 + nb], V(hT[:, kt, half * 512:(half + 1) * 512], hk(half)),
             start=(kt == 0), stop=(kt == nk - 1))
    return ps


def colvec(k, name, dram_row, n, col0=0):
    t = k.sb(name, [128, 1], F32)
    k.dma(t[0:n, :], dram_row[:, col0:col0 + n].rearrange("o f -> f o"))
    return t


def build_P(upto=99.0):
    k = K()
    x = k.din("x", [T, 2048])
    w_in = k.din("w_p", [2048, PW])
    norm_g = k.din("norm_g", [1, 2048])
    a_qn_g = k.din("a_qn_g", [1, 128])
    a_kn_g = k.din("a_kn_g", [1, 128])
    b_cq_g = k.din("b_cq_g", [1, 448])
    b_ckv_g = k.din("b_ckv_g", [1, 128])
    b_w_uq = k.din("b_w_uq", [448, 768])
    b_w_ukv = k.din("b_w_ukv", [128, 1024])
    b_qn_g = k.din("b_qn_g", [1, 192])
    b_kn_g = k.din("b_kn_g", [1, 192])
    pos = k.din("pos", [1, T], I32)
    invf = k.din("c_invf", [64, 1])
    sgn = k.din("c_sgn", [64, 1])

    qa = k.dout("qa", [12, 128, T], BF16)
    ka = k.dout("ka", [12, 128, T], BF16)
    va = k.dout("va", [T, 1536], BF16)
    qb = k.dout("qb", [4, 192, T], BF16)
    kb = k.dout("kb", [4, 192, T], BF16)
    vb = k.dout("vb", [T, 512], BF16)
    cqk = k.dout("cqk", [512, T], F32)
    cv = k.dout("cv", [T, 512], BF16)
    cif = k.dout("cif", [8, T], F32)
    du = k.dout("du", [512, T], BF16)

    ident, ones = consts(k)
    if upto <= 1:
        k.dbg = k.dout("dbg", [128, 16])
        k.dbg2 = k.dout("dbg2", [128, 2048])
    hT = norm_hT(k, x, norm_g, ident)
    if upto <= 1:
        hf = k.sb("hf", [128, 2048], F32)
        k.copy(hf[:].rearrange("p (a b) -> p a b", a=16), V(hT[:, :, 0:128], "hT0"))
        k.dma(k.dbg2, hf[:])

    if upto <= 1:
        return k.finish()
    posi = k.sb("posi", [64, T], I32)
    k.dma(posi[:], pos.partition_broadcast(64))
    posf = k.sb("posf", [64, T], F32)
    k.copy(posf[:], posi[:])
    invf_sb = k.sb("invf_sb", [64, 1], F32)
    k.dma(invf_sb[:], invf)
    sgn_sb = k.sb("sgn_sb", [64, 1], F32)
    k.dma(sgn_sb[:], sgn)
    ang = k.sb("ang", [64, T], F32)
    k.ts(ang[:], posf[:], invf_sb[:, 0:1], ALU.mult)
    cosT = k.sb("cosT", [64, T], F32)
    sinS = k.sb("sinS", [64, T], F32)
    trig(k, cosT, ang, 64, T, math.pi / 2, posf, posi)
    trig(k, sinS, ang, 64, T, 0.0, posf, posi)
    k.ts(sinS[:], sinS[:], sgn_sb[:, 0:1], ALU.mult)

    if upto <= 2:
        return k.finish()
    gq = colvec(k, "gq", a_qn_g, 128)
    gk = colvec(k, "gk", a_kn_g, 128)
    for (nm, outd, g) in (("a_q", qa, gq), ("a_k", ka, gk)):
        for cg in range(3):
            wt = load_w(k, w_in, POFF[nm] + cg * 512, 512)
            for b in range(4):
                hd = cg * 4 + b
                for half in range(NH):
                    ps = proj_block(k, wt, b * 128, 128, hT, half)
                    sq = k.pool("sq", [128, 512], BF16, 3)
                    k.act(sq[:], ps[:], AF.Square)
                    ps2 = k.psum()
                    k.mm(ps2[:], ones[:], sq[:])
                    rs = rstd_from_ss(k, ps2, 128)
                    ob = k.pool("ob", [128, 512], BF16, 4)
                    k.stt(ob[:], ps[:], g[:, 0:1], rs[:], ALU.mult, ALU.mult)
                    k.dma(V(outd[hd, :, half * 512:(half + 1) * 512], uk("o")), ob[:])

    if upto <= 3:
        return k.finish()
    for (nm, outd, ncg) in (("a_v", va, 3), ("c_v", cv, 1)):
        for cg in range(ncg):
            wt = load_w(k, w_in, POFF[nm] + cg * 512, 512)
            for tt in range(T // 128):
                ps = k.psum()
                for kt in range(16):
                    k.mm(ps[:], V(hT[:, kt, tt * 128:(tt + 1) * 128], "hT%d" % tt), wt[:, kt, 0:512], start=(kt == 0), stop=(kt == 15))
                ob = k.pool("ob", [128, 512], BF16, 4)
                k.copy(ob[:], ps[:], eng=('act' if tt % 2 else 'dve'))
                k.dma(V(outd[tt * 128:(tt + 1) * 128, cg * 512:(cg + 1) * 512], uk("o")), ob[:])

    if upto <= 4:
        return k.finish()
    for (nm, outd, dt_) in (("c_qk", cqk, F32), ("d_u", du, BF16)):
        wt = load_w(k, w_in, POFF[nm], 512)
        for b in range(4):
            for half in range(NH):
                ps = proj_block(k, wt, b * 128, 128, hT, half)
                if dt_ == F32:
                    ob = k.pool("obf", [128, 512], F32, 2)
                else:
                    ob = k.pool("ob", [128, 512], BF16, 4)
                k.copy(ob[:], ps[:], eng=('act' if half else 'dve'))
                k.dma(V(outd[b * 128:(b + 1) * 128, half * 512:(half + 1) * 512], uk("o")), ob[:])
    wt = load_w(k, w_in, POFF["c_i"], 8)
    for half in range(NH):
        ps = proj_block(k, wt, 0, 8, hT, half)
        ob = k.pool("obf", [128, 512], F32, 2)
        k.copy(ob[0:8, :], ps[0:8, :])
        k.dma(V(cif[:, half * 512:(half + 1) * 512], uk("o")), ob[0:8, :])

    if upto <= 5:
        return k.finish()
    wtq = load_w(k, w_in, POFF["b_cq"], 448)
    wtk = load_w(k, w_in, POFF["b_ckv"], 192)
    wsw = k.sb("wsw", [128, 16, 64], BF16)
    k.dma(V(wsw[:, :, 0:32], "wsw_a"), w_in[:, POFF["b_kr"] + 32:POFF["b_kr"] + 64].rearrange("(kt p) c -> p kt c", p=128), q='pool')
    k.dma(V(wsw[:, :, 32:64], "wsw_b"), w_in[:, POFF["b_kr"]:POFF["b_kr"] + 32].rearrange("(kt p) c -> p kt c", p=128), q='pool')
    WSW = [V(wsw[:, kt, :], "wsw_a") for kt in range(16)]
    wuq = k.sb("wuq", [128, 4, 768], BF16)
    for blk in range(4):
        r = 128 if blk < 3 else 64
        k.dma(V(wuq[0:r, blk, :], "wuq%d" % blk), b_w_uq[blk * 128:blk * 128 + r, :], q='pool')
    wuqs = k.sb("wuqs", [128, 4, 4, 64], BF16)
    b_w_uq_h = b_w_uq.rearrange("k (h d) -> k h d", h=4)
    for blk in range(4):
        r = 128 if blk < 3 else 64
        k.dma(V(wuqs[0:r, blk, :, 0:32], "wuqs%da" % blk), b_w_uq_h[blk * 128:blk * 128 + r, :, 160:192], q='pool')
        k.dma(V(wuqs[0:r, blk, :, 32:64], "wuqs%db" % blk), b_w_uq_h[blk * 128:blk * 128 + r, :, 128:160], q='pool')
    wukv = k.sb("wukv", [128, 1024], BF16)
    k.dma(wukv[:], b_w_ukv, q='pool')
    gcq = k.sb("gcq", [128, 4], F32)
    for blk in range(4):
        r = 128 if blk < 3 else 64
        k.dma(V(gcq[0:r, blk:blk + 1], "gcq%d" % blk), b_cq_g[:, blk * 128:blk * 128 + r].rearrange("o f -> f o"))
    gckv = colvec(k, "gckv", b_ckv_g, 128)
    gqn = colvec(k, "gqn", b_qn_g, 128)
    gkn = colvec(k, "gkn", b_kn_g, 128)
    gqr = colvec(k, "gqr", b_qn_g, 64, 128)
    gkr = colvec(k, "gkr", b_kn_g, 64, 128)
    gqrs = k.sb("gqrs", [128, 1], F32)
    k.dma(V(gqrs[0:32, :], "gqrs_a"), b_qn_g[:, 160:192].rearrange("o f -> f o"))
    k.dma(V(gqrs[32:64, :], "gqrs_b"), b_qn_g[:, 128:160].rearrange("o f -> f o"))
    gkrs = k.sb("gkrs", [128, 1], F32)
    k.dma(V(gkrs[0:32, :], "gkrs_a"), b_kn_g[:, 160:192].rearrange("o f -> f o"))
    k.dma(V(gkrs[32:64, :], "gkrs_b"), b_kn_g[:, 128:160].rearrange("o f -> f o"))
    GQRS = V(gqrs[0:64, 0:1], ("gqrs_a", "gqrs_b"))
    GKRS = V(gkrs[0:64, 0:1], ("gkrs_a", "gkrs_b"))

    if upto <= 6:
        return k.finish()
    for half in range(NH):
        hs = slice(half * 512, (half + 1) * 512)
        cqraw = k.pool("cqraw", [128, 4, 512], F32, 1)
        sqa = k.pool("sqa", [128, 512], F32, 1)
        if not hasattr(k, "_sq3"):
            k._sq3 = k.sb("sq3", [128, 512], F32)
            k.memset(k._sq3[:], 0.0)
        for blk in range(4):
            r = 128 if blk < 3 else 64
            ps = proj_block(k, wtq, blk * 128, r, hT, half)
            k.copy(V(cqraw[0:r, blk, :], "cqraw%d" % blk), ps[0:r, :], eng='dve')
            craw = V(cqraw[0:r, blk, :], "cqraw%d" % blk)
            if blk == 0:
                k.act(sqa[:], craw, AF.Square)
            elif blk < 3:
                sqt = k.pool("sqt", [128, 512], F32, 2)
                k.act(sqt[:], craw, AF.Square)
                k.tt(sqa[:], sqa[:], sqt[:], ALU.add)
            else:
                k.act(k._sq3[0:64, :], craw, AF.Square)
                sq = k.pool("sq", [128, 512], BF16, 3)
                k.tt(sq[:], sqa[:], k._sq3[:], ALU.add)
        ps_ss = k.psum()
        k.mm(ps_ss[:], ones[:], sq[:])
        if upto <= 6.6:
            continue
        rs = rstd_from_ss(k, ps_ss, 448)
        cqn = k.pool("cqn", [128, 4, 512], BF16, 1)
        for blk in range(4):
            r = 128 if blk < 3 else 64
            k.stt(V(cqn[0:r, blk, :], "cqn%d" % blk), V(cqraw[0:r, blk, :], "cqraw%d" % blk),
                  V(gcq[0:r, blk:blk + 1], "gcq%d" % blk), rs[0:r, :], ALU.mult, ALU.mult)
        if upto <= 7:
            continue
        for hh in range(4):
            psn = k.psum()
            psr = k.psum()
            pss = k.psum()
            for blk in range(4):
                r = 128 if blk < 3 else 64
                rhs = V(cqn[0:r, blk, :], "cqn%d" % blk)
                k.mm(psn[:], V(wuq[0:r, blk, hh * 192:hh * 192 + 128], "wuq%d" % blk), rhs, start=(blk == 0), stop=(blk == 3))
            for blk in range(4):
                r = 128 if blk < 3 else 64
                rhs = V(cqn[0:r, blk, :], "cqn%d" % blk)
                k.mm(psr[0:64, :], V(wuq[0:r, blk, hh * 192 + 128:hh * 192 + 192], "wuq%d" % blk), rhs, start=(blk == 0), stop=(blk == 3))
            for blk in range(4):
                r = 128 if blk < 3 else 64
                rhs = V(cqn[0:r, blk, :], "cqn%d" % blk)
                k.S.op('pe', (lambda e, o=pss[0:64, :], l=wuqs[0:r, blk, hh, :], rr=_ap(rhs), s=(blk == 0), t=(blk == 3):
                              e.matmul(o, lhsT=l, rhs=rr, start=s, stop=t)),
                       reads=["wuqs%da" % blk, "wuqs%db" % blk, rhs.key] + ([] if blk == 0 else [pss.name]), writes=[pss.name])
            sq = k.pool("sq", [128, 512], BF16, 3)
            k.act(sq[:], psn[:], AF.Square)
            sq2 = k.pool("sq", [128, 512], BF16, 3)
            k.act(sq2[0:64, :], psr[0:64, :], AF.Square)
            ps2 = k.psum()
            k.mm(ps2[:], ones[:], sq[:], start=True, stop=False)
            k.mm(ps2[:], ones[0:64, :], sq2[0:64, :], start=False, stop=True)
            rq = rstd_from_ss(k, ps2, 192)
            ob = k.pool("ob", [128, 512], BF16, 4)
            k.stt(ob[:], psn[:], gqn[:, 0:1], rq[:], ALU.mult, ALU.mult)
            k.dma(V(qb[hh, 0:128, hs], uk("o")), ob[:])
            t1 = k.pool("t1", [64, 512], F32, 1)
            k.stt(t1[:], psr[0:64, :], gqr[0:64, 0:1], cosT[:, hs], ALU.mult, ALU.mult)
            t2 = k.pool("t2", [64, 512], F32, 1)
            k.stt(t2[:], pss[0:64, :], GQRS, sinS[:, hs], ALU.mult, ALU.mult)
            k.tt(t1[:], t1[:], t2[:], ALU.add)
            ob2 = k.pool("ob", [128, 512], BF16, 4)
            k.tt(ob2[0:64, :], t1[:], rq[0:64, :], ALU.mult)
            k.dma(V(qb[hh, 128:192, hs], uk("o")), ob2[0:64, :])
        if upto <= 8:
            continue
        ps = proj_block(k, wtk, 0, 128, hT, half)
        ckraw = k.pool("ckraw", [128, 512], F32, 1)
        k.copy(ckraw[:], ps[:], eng='dve')
        sq = k.pool("sq", [128, 512], BF16, 3)
        k.act(sq[:], ps[:], AF.Square)
        ps2 = k.psum()
        k.mm(ps2[:], ones[:], sq[:])
        rs = rstd_from_ss(k, ps2, 128)
        ckvn = k.pool("ckvn", [128, 512], BF16, 1)
        k.stt(ckvn[:], ckraw[:], gckv[:, 0:1], rs[:], ALU.mult, ALU.mult)
        if upto <= 9:
            continue
        wv = wukv[:].rearrange("p (h d) -> p h d", h=4)[:, :, 128:256]
        for t4 in range(4):
            psv = k.psum()
            k.mm(psv[:].rearrange("p (h d) -> p h d", h=4), ckvn[:, t4 * 128:(t4 + 1) * 128], wv)
            ob = k.pool("ob", [128, 512], BF16, 4)
            k.copy(ob[:], psv[:], eng='act')
            tok0 = half * 512 + t4 * 128
            k.dma(V(vb[tok0:tok0 + 128, :], uk("o")), ob[:])
        if upto <= 10:
            continue
        pkr = proj_block(k, wtk, 128, 64, hT, half)
        pks = k.psum()
        for kt in range(16):
            k.S.op('pe', (lambda e, o=pks[0:64, :], l=wsw[:, kt, :], rr=hT[:, kt, hs], s=(kt == 0), t=(kt == 15):
                          e.matmul(o, lhsT=l, rhs=rr, start=s, stop=t)),
                   reads=["wsw_a", "wsw_b"] + list(hk(half)) + ([] if kt == 0 else [pks.name]), writes=[pks.name])
        sqr = k.pool("sqr", [64, 512], BF16, 2)
        k.act(sqr[:], pkr[0:64, :], AF.Square)
        kro = k.pool("kro", [64, 512], F32, 1)
        k.stt(kro[:], pkr[0:64, :], gkr[0:64, 0:1], cosT[:, hs], ALU.mult, ALU.mult)
        t2 = k.pool("t2", [64, 512], F32, 1)
        k.stt(t2[:], pks[0:64, :], GKRS, sinS[:, hs], ALU.mult, ALU.mult)
        k.tt(kro[:], kro[:], t2[:], ALU.add)
        if upto <= 11:
            continue
        for hh in range(4):
            pkn = k.psum()
            k.mm(pkn[:], wukv[:, hh * 256:hh * 256 + 128], ckvn[:])
            sq = k.pool("sq", [128, 512], BF16, 3)
            k.act(sq[:], pkn[:], AF.Square)
            ps2 = k.psum()
            k.mm(ps2[:], ones[:], sq[:], start=True, stop=False)
            k.mm(ps2[:], ones[0:64, :], sqr[:], start=False, stop=True)
            rk = rstd_from_ss(k, ps2, 192)
            ob = k.pool("ob", [128, 512], BF16, 4)
            k.stt(ob[:], pkn[:], gkn[:, 0:1], rk[:], ALU.mult, ALU.mult)
            k.dma(V(kb[hh, 0:128, hs], uk("o")), ob[:])
            ob2 = k.pool("ob", [128, 512], BF16, 4)
            k.tt(ob2[0:64, :], kro[:], rk[0:64, :], ALU.mult)
            k.dma(V(kb[hh, 128:192, hs], uk("o")), ob2[0:64, :])
    return k.finish()


def trig(k, out, ang, P, N, shift, a, ki):
    TWO_PI = 2.0 * math.pi
    HI = 6.28125
    LO = TWO_PI - HI
    k.ts(a[:], ang[:], shift, ALU.add)
    kf = k.pool("tg_k", [P, N], F32, 1)
    k.ts(kf[:], a[:], 1.0 / TWO_PI, ALU.mult)
    k.copy(ki[:], kf[:])
    k.copy(kf[:], ki[:])
    k.stt(a[:], kf[:], -HI, a[:], ALU.mult, ALU.add)
    k.stt(a[:], kf[:], -LO, a[:], ALU.mult, ALU.add)
    m = k.pool("tg_m", [P, N], F32, 1)
    k.ts(m[:], a[:], math.pi, ALU.is_gt, -TWO_PI, ALU.mult)
    k.tt(a[:], a[:], m[:], ALU.add)
    k.ts(m[:], a[:], -math.pi, ALU.is_lt, TWO_PI, ALU.mult)
    k.tt(a[:], a[:], m[:], ALU.add)
    k.ts(a[:], a[:], math.pi, ALU.min, -math.pi, ALU.max)
    k.act(out[:], a[:], AF.Sin)


import math

S = 8192
A_DIL = (1, 4, 16)
A_SCALE = 128 ** -0.5
B_SCALE = 192 ** -0.5
NEG = -30000.0


def sst(start, count, step):
    return slice(start, start + step * (count - 1) + 1, step)


def host_biasA(core):
    slopes = 2.0 ** (-8.0 * np.arange(1, 13, dtype=np.float64) / 12)
    kk = np.arange(128)[:, None]
    qq = np.arange(128)[None, :]
    out = np.zeros((12, 3, 128, 128), np.float32)
    for gh in range(12):
        d = A_DIL[gh // 4]
        sl = slopes[gh]
        delta0 = qq - kk + 128
        b0 = np.where(kk >= qq, -sl * d * delta0, NEG * A_SCALE)
        delta1 = qq - kk
        b1 = np.where(kk <= qq, -sl * d * delta1, NEG * A_SCALE)
        out[gh, 0] = b0 / A_SCALE
        out[gh, 1] = (b0 / A_SCALE) if core > 0 else NEG
        out[gh, 2] = b1 / A_SCALE
    return np.maximum(out, NEG).astype(np.float32)


def host_maskB(par):
    out = np.zeros((8, 128, 512), np.float32)
    kk = np.arange(128)[:, None]
    qq = np.arange(512)[None, :]
    for i in range(8):
        o = i - 4 * par
        if o < 0:
            out[i] = 0.0
        elif o > 3:
            out[i] = NEG
        else:
            out[i] = np.where(128 * o + kk <= qq, 0.0, NEG)
    return out


def mixer_A(k, ident, ones):
    qa = k.din("qa", [12, 128, T], BF16)
    kax = k.din("ka_ext", [12, 128, 2048 + T], BF16)
    vax = k.din("va_ext", [2048 + T, 1536], BF16)
    biasd = k.din("c_biasA", [12, 3, 128, 128], F32)
    ya = k.dout("ya", [512, T], F32)

    bias = k.sb("biasA", [128, 36, 128], BF16)
    for gh in range(12):
        k.dma(V(bias[:, gh * 3:(gh + 1) * 3, :], "biasA%d" % gh), biasd[gh].rearrange("t k q -> k t q"), q='pool')
    accN = [k.sb("accN%d" % h, [128, T], F32) for h in range(4)]
    accD = [k.sb("accD%d" % h, [128, T], F32) for h in range(4)]
    pend = []

    def part2(ctx):
        vt, vkeys, t0, t1, P, Bq, g, h, c0, d = ctx
        psO = k.psum(4, 8)
        k.mm(psO[:, 0:Bq], V(vt[:, t0, :], vkeys), P[:, 0:Bq], start=True, stop=False)
        k.mm(psO[:, 0:Bq], V(vt[0:Bq, t1, :], vkeys), P[0:Bq, 128:128 + Bq], start=False, stop=True)
        k.mm(psO[:, 128:128 + Bq], ones[:], P[:, 0:Bq], start=True, stop=False)
        k.mm(psO[:, 128:128 + Bq], ones[0:Bq, :], P[0:Bq, 128:128 + Bq], start=False, stop=True)
        cs = sst(c0, Bq, d)
        if g == 0:
            k.copy(accN[h][:, cs], psO[:, 0:Bq], eng='dve')
            k.copy(accD[h][:, cs], psO[:, 128:128 + Bq], eng='dve')
        else:
            k.tt(accN[h][:, cs], accN[h][:, cs], psO[:, 0:Bq], ALU.add)
            k.tt(accD[h][:, cs], accD[h][:, cs], psO[:, 128:128 + Bq], ALU.add)
    for gh in range(12):
        g, h = gh // 4, gh % 4
        d = A_DIL[g]
        Bq = 128 if d < 16 else 64
        nq = T // d
        nblk = nq // Bq
        qs = k.pool("qA", [128, T], BF16, 2)
        k.dma(qs[:], qa[gh])
        kx = k.pool("kA", [128, 2048 + T], BF16, 2)
        k.dma(kx[:], kax[gh])
        for r in range(d):
            nkeys = 128 + nq
            nt_full = nkeys // 128
            rem = nkeys - nt_full * 128
            vt = k.pool("vA", [128, 9, 128], BF16, 4)
            e0 = 2048 + r - 128 * d
            src = vax[sst(e0, 128 * nt_full, d), gh * 128:(gh + 1) * 128].rearrange("(t k) c -> k t c", k=128)
            vkeys = ["%s_a" % vt.name]
            k.dma(V(vt[:, 0:nt_full, :], vkeys[0]), src)
            if rem:
                e1 = e0 + d * 128 * nt_full
                vkeys.append("%s_b" % vt.name)
                k.dma(V(vt[0:rem, nt_full, :], vkeys[1]), vax[sst(e1, rem, d), gh * 128:(gh + 1) * 128])
            for blk in range(nblk):
                i0 = blk * Bq
                q_ap = qs[:, sst(r + d * i0, Bq, d)]
                ek0 = 2048 + r + d * (i0 - 128)
                k0 = kx[:, sst(ek0, 128, d)]
                ek1 = 2048 + r + d * i0
                k1 = kx[:, sst(ek1, Bq, d)]
                btype = 1 if i0 == 0 else 0
                bkey = "biasA%d" % gh
                psS = k.psum(0, 4)
                k.mm(psS[:, 0:Bq], k0, q_ap, start=True, stop=False)
                k.mm(psS[:, 0:Bq], ident[:], V(bias[:, gh * 3 + btype, 0:Bq], bkey), start=False, stop=True)
                k.mm(psS[0:Bq, 128:128 + Bq], k1, q_ap, start=True, stop=False)
                k.mm(psS[0:Bq, 128:128 + Bq], ident[0:Bq, 0:Bq], V(bias[0:Bq, gh * 3 + 2, 0:Bq], bkey), start=False, stop=True)
                P = k.pool("PA", [128, 256], BF16, 4)
                if Bq == 128:
                    k.act(P[:], psS[:, 0:256], AF.Exp, scale=A_SCALE)
                else:
                    k.act(P[:, 0:Bq], psS[:, 0:Bq], AF.Exp, scale=A_SCALE)
                    k.act(P[0:Bq, 128:128 + Bq], psS[0:Bq, 128:128 + Bq], AF.Exp, scale=A_SCALE)
                t0, t1 = blk, blk + 1
                pend.append((vt, vkeys, t0, t1, P, Bq, g, h, r + d * i0, d))
                if len(pend) > 1:
                    part2(pend.pop(0))
    while pend:
        part2(pend.pop(0))
    for h in range(4):
        k.recip(accD[h][:], accD[h][:])
        k.tt(accN[h][:], accN[h][:], accD[h][:], ALU.mult)
        k.dma(V(ya[h * 128:(h + 1) * 128, :], uk("o")), accN[h][:])


def mixer_B(k, ident, ones):
    qb = k.din("qb_h", [192, 4096], BF16)
    kb = k.din("kb_h", [192, S], BF16)
    vb = k.din("vb_h", [S, 128], BF16)
    maskd = k.din("c_maskB", [8, 128, 512], F32)
    yb = k.dout("yb", [128, 4096], F32)

    mask = k.sb("maskB", [128, 8, 512], BF16)
    k.dma(mask[:], maskd.rearrange("i k q -> k i q"), q='pool')
    kn = k.sb("kbn", [128, S], BF16)
    kr = k.sb("kbr", [64, S], BF16)
    for c4 in range(4):
        cs = slice(c4 * 2048, (c4 + 1) * 2048)
        k.dma(V(kn[:, cs], "kbn%d" % c4), kb[0:128, cs])
        k.dma(V(kr[:, cs], "kbr%d" % c4), kb[128:192, cs])
    vt = k.sb("vbt", [128, 64, 128], BF16)
    for c4 in range(4):
        k.dma(V(vt[:, c4 * 16:(c4 + 1) * 16, :], "vbt%d" % c4),
              vb[c4 * 2048:(c4 + 1) * 2048, :].rearrange("(t k) c -> k t c", k=128))
    for j in range(8):
        qn = k.pool("qbn", [128, 512], BF16, 2)
        qr = k.pool("qbr", [64, 512], BF16, 2)
        k.dma(qn[:], qb[0:128, j * 512:(j + 1) * 512])
        k.dma(qr[:], qb[128:192, j * 512:(j + 1) * 512])
        nkb = 8 * j + 8
        psO = k.bank(j % 2)
        psD = k.bank(2 + j % 2)

        def pv(b, P):
            c4_ = b // 16
            k.mm(psO[:], V(vt[:, b, :], "vbt%d" % c4_), P[:], start=(b == 0), stop=(b == nkb - 1))
            k.mm(psD[:], ones[:], P[:], start=(b == 0), stop=(b == nkb - 1))
        prev = None
        for b in range(nkb):
            c4 = b // 16
            psS = k.psum(4, 8)
            last8 = b >= nkb - 8
            k.mm(psS[:], V(kn[:, b * 128:(b + 1) * 128], "kbn%d" % c4), qn[:], start=True, stop=False)
            k.mm(psS[:], V(kr[:, b * 128:(b + 1) * 128], "kbr%d" % c4), qr[:], start=False, stop=(not last8))
            if last8:
                k.mm(psS[:], ident[:], mask[:, b - (nkb - 8), :], start=False, stop=True)
            P = k.pool("PB", [128, 512], BF16, 4)
            k.act(P[:], psS[:], AF.Exp, scale=B_SCALE)
            if prev is not None:
                pv(*prev)
            prev = (b, P)
        pv(*prev)
        rd = k.pool("rdB", [128, 512], F32, 2)
        k.recip(rd[:], psD[:])
        ob = k.pool("obB", [128, 512], F32, 2)
        k.tt(ob[:], psO[:], rd[:], ALU.mult)
        k.dma(V(yb[:, j * 512:(j + 1) * 512], uk("o")), ob[:])


def build_M(parts="ABCD"):
    k = K()
    idf = k.din("c_ident", [128, 128], F32)
    ident = k.sb("ident", [128, 128], BF16)
    k.dma(ident[:], idf, q='pool')
    ones = k.sb("ones", [128, 128], BF16)
    k.memset(ones[:], 1.0)
    if "A" in parts:
        mixer_A(k, ident, ones)
    if "B" in parts:
        mixer_B(k, ident, ones)
    if "C" in parts:
        mixer_C(k, ident, ones)
    if "D" in parts:
        mixer_D(k, ident, ones)
    return k.finish()


S = 8192
NCH = 128
L = 64


def host_constsC():
    tri = (np.arange(128)[:, None] < np.arange(128)[None, :]).astype(np.float32)
    mask = (np.arange(64)[:, None] <= np.arange(64)[None, :]).astype(np.float32)
    return {"c_tri": tri, "c_maskC": mask, "c_identf": np.eye(128, dtype=np.float32)}


def mixer_C(k, ident, ones):
    cq = k.din("cq_h", [64, 3 + S], F32)
    ck = k.din("ck_h", [64, 3 + S], F32)
    cwq = k.din("cw_q", [64, 4], F32)
    cbq = k.din("cb_q", [64, 1], F32)
    cwk = k.din("cw_k", [64, 4], F32)
    cbk = k.din("cb_k", [64, 1], F32)
    cv = k.din("cv_h", [S, 64], BF16)
    gi = k.din("gi", [128, 64], F32)
    gf = k.din("gf", [128, 64], F32)
    gb = k.din("gb", [1, 2], F32)
    trid = k.din("c_tri", [128, 128], F32)
    maskd = k.din("c_maskC", [64, 64], F32)
    identd = k.din("c_identf", [128, 128], F32)
    hc = k.dout("hc", [S, 64], F32)

    identf = k.sb("identf", [128, 128], F32)
    k.dma(identf[:], identd)
    tri = k.sb("tri", [128, 128], F32)
    k.dma(tri[:], trid)
    maskC = k.sb("maskC", [64, 64], F32)
    k.dma(maskC[:], maskd)
    onesf = k.sb("onesf", [128, 64], F32)
    k.memset(onesf[:], 1.0)

    qT = k.sb("qT", [64, S], BF16)
    kT = k.sb("kT", [64, S], BF16)
    CH = 2048
    for (src, wd, bd, dst, scl, nm) in ((cq, cwq, cbq, qT, 1.0, "q"), (ck, cwk, cbk, kT, 0.125, "k")):
        w = k.sb("cw" + nm, [64, 4], F32)
        k.dma(w[:], wd)
        b = k.sb("cb" + nm, [64, 1], F32)
        k.dma(b[:], bd)
        for ci in range(S // CH):
            xt = k.pool("cx", [64, CH + 3], F32, 2)
            k.dma(xt[:], src[:, ci * CH:ci * CH + CH + 3])
            acc = k.pool("cacc", [64, CH], F32, 2)
            k.ts(acc[:], xt[:, 0:CH], w[:, 0:1], ALU.mult)
            for j in range(1, 4):
                k.stt(acc[:], xt[:, j:j + CH], w[:, j:j + 1], acc[:], ALU.mult, ALU.add)
            dkey = V(dst[:, ci * CH:(ci + 1) * CH], "%sT%d" % (nm, ci))
            if scl == 1.0:
                k.act(dkey, acc[:], AF.Silu, bias=b[:, 0:1])
            else:
                k.act(acc[:], acc[:], AF.Silu, bias=b[:, 0:1])
                k.ts(dkey, acc[:], scl, ALU.mult, eng='pool')

    def tk(nm, t0):
        return "%sT%d" % (nm, t0 // CH)

    vx = k.sb("vx", [64, NCH, 65], BF16)
    for c4 in range(4):
        k.dma(V(vx[:, c4 * 32:(c4 + 1) * 32, 0:64], "vx%d" % c4),
              cv[c4 * 2048:(c4 + 1) * 2048, :].rearrange("(c s) d -> s c d", s=64))
    k.memset(V(vx[:, :, 64:65], "vx1"), 1.0, eng='pool')

    def vxk(c):
        return ("vx%d" % (c // 32), "vx1")

    def g(name, cols=64):
        return k.sb(name, [128, cols], F32)
    gbb = g("gbb", 2)
    k.dma(gbb[:], gb.partition_broadcast(128))
    ngb = g("ngb", 2)
    k.ts(ngb[:], gbb[:], -1.0, ALU.mult)
    gi_s = g("gi_s")
    gf_s = g("gf_s")
    k.dma(gi_s[:], gi)
    k.dma(gf_s[:], gf)
    e1 = g("e1")
    k.act(e1[:], gf_s[:], AF.Exp, bias=ngb[:, 1:2], scale=-1.0)
    k.act(e1[:], e1[:], AF.Ln, bias=1.0)
    lf = g("lf")
    k.ts(lf[:], e1[:], -1.0, ALU.mult)
    bloc = g("bloc")
    k.scan(bloc[:], onesf[:], lf[:], 0.0, ALU.mult, ALU.add)
    psb = k.psum()
    k.mm(psb[:, 0:1], tri[:], bloc[:, 63:64])
    bst = g("bst", 1)
    k.copy(bst[:], psb[:, 0:1])
    Bg = g("Bg")
    k.ts(Bg[:], bloc[:], bst[:, 0:1], ALU.add)
    a = g("a")
    k.ts(a[:], gi_s[:], gbb[:, 0:1], ALU.add)
    k.tt(a[:], a[:], Bg[:], ALU.subtract)
    cm = g("cm")
    k.scan(cm[:], a[:], a[:], -1.0e30, ALU.max, ALU.max)
    pst = k.psum()
    k.tr(pst[0:1, 0:128], cm[:, 63:64], identf[:])
    crow = k.sb("crow", [1, 128], F32)
    k.copy(crow[:], pst[0:1, 0:128])
    zrow = k.sb("zrow", [1, 128], F32)
    k.memset(zrow[:], 0.0)
    rinc = k.sb("rinc", [1, 128], F32)
    k.scan(rinc[:], crow[:], zrow[:], 0.0, ALU.max, ALU.max)
    rexc = k.sb("rexc", [1, 128], F32)
    k.memset(V(rexc[:, 0:1], "rexc"), 0.0)
    k.copy(V(rexc[:, 1:128], "rexc"), rinc[:, 0:127])
    Rc = g("Rc", 1)
    Rn = g("Rn", 1)
    for (row, col) in ((rexc, Rc), (rinc, Rn)):
        p2 = k.psum()
        k.tr(p2[:, 0:1], V(row[0:1, :], row.name), identf[0:1, 0:1])
        k.copy(col[:], p2[:, 0:1])
    nRn = g("nRn", 1)
    k.ts(nRn[:], Rn[:], -1.0, ALU.mult)
    M = g("M")
    k.ts(M[:], cm[:], Rc[:, 0:1], ALU.max)
    fI = g("fI")
    k.act(fI[:], M[:], AF.Exp, bias=Rn[:, 0:1], scale=-1.0)
    fE = g("fE")
    k.act(fE[:], M[:], AF.Exp, bias=Rc[:, 0:1], scale=-1.0)
    wk = g("wk")
    k.act(wk[:], a[:], AF.Exp, bias=nRn[:, 0:1], scale=1.0)
    thr = g("thr")
    k.tt(thr[:], Bg[:], M[:], ALU.add)
    k.act(thr[:], thr[:], AF.Exp, scale=-1.0)
    dec = g("dec", 1)
    k.tt(dec[:], Rc[:], Rn[:], ALU.subtract)
    k.act(dec[:], dec[:], AF.Exp)
    tabs = {}
    for nm, src in (("fI", fI), ("fE", fE), ("wk", wk), ("thr", thr)):
        p2 = k.psum()
        k.tr(p2[0:64, 0:128], src[:], identf[:])
        t = k.sb(nm + "_T", [64, 128], F32)
        k.copy(t[:], p2[0:64, 0:128])
        tabs[nm] = t
    p2 = k.psum()
    k.tr(p2[0:1, 0:128], dec[:], identf[:])
    drow = k.sb("drow", [1, 128], F32)
    k.copy(drow[:], p2[0:1, 0:128])
    p3 = k.psum()
    k.mm(p3[0:64, 0:128], onesf[0:1, 0:64], drow[:])
    dec_rep = k.sb("dec_rep", [64, 128], F32)
    k.copy(dec_rep[:], p3[0:64, 0:128])

    Sf = k.sb("Sf", [64, 65], F32)
    k.memset(Sf[:], 0.0)
    Sb = k.sb("Sb", [64, 65], BF16)
    k.memset(Sb[:], 0.0)
    hout = None
    for c in range(NCH):
        t0 = c * L
        ts_ = slice(t0, t0 + L)
        kTc = V(kT[:, ts_], tk("k", t0))
        qTc = V(qT[:, ts_], tk("q", t0))
        wcol = tabs["wk"][:, c:c + 1]
        bA = k.psum(0, 3)
        bB = k.psum(3, 6)
        bC = k.psum(6, 8)
        pkb = bC[:].bitcast(BF16)
        k.tr(V(pkb[0:64, 0:64], bC.name), kTc, ident[0:64, 0:64])
        Kt = k.pool("Kt", [64, 64], BF16, 3)
        k.ts(Kt[:], V(pkb[0:64, 0:64], bC.name), wcol, ALU.mult)
        k.mm(bA[0:64, 0:64], kTc, qTc)
        Pt = k.pool("Pt", [64, 64], BF16, 3)
        k.stt(Pt[:], bA[0:64, 0:64], wcol, maskC[:], ALU.mult, ALU.mult)
        vxc = V(vx[:, c, :], vxk(c))
        k.mm(bB[0:64, 0:65], Pt[:], vxc)
        k.mm(bB[0:64, 128:193], qTc, Sb[:])
        k.mm(bA[0:64, 128:193], Kt[:], vxc)
        k.stt(Sf[:], Sf[:], dec_rep[:, c:c + 1], bA[0:64, 128:193], ALU.mult, ALU.add)
        k.copy(Sb[:], Sf[:], eng='pool')
        tmp = k.pool("ctmp", [64, 65], F32, 3)
        k.ts(tmp[:], bB[0:64, 128:193], tabs["fE"][:, c:c + 1], ALU.mult)
        tot = k.pool("ctot", [64, 65], F32, 3)
        k.stt(tot[:], bB[0:64, 0:65], tabs["fI"][:, c:c + 1], tmp[:], ALU.mult, ALU.add)
        dm = k.pool("cdm", [64, 1], F32, 3)
        k.stt(dm[:], tot[:, 64:65], -1.0, tot[:, 64:65], ALU.mult, ALU.max)
        k.ts(dm[:], dm[:], tabs["thr"][:, c:c + 1], ALU.max)
        k.recip(dm[:], dm[:])
        if c % 16 == 0:
            hout = k.pool("hout", [64, 16, 64], F32, 2)
        k.ts(V(hout[:, c % 16, :], hout.name), tot[:, 0:64], dm[:, 0:1], ALU.mult)
        if c % 16 == 15:
            c0 = c - 15
            k.dma(V(hc[c0 * 64:(c0 + 16) * 64, :].rearrange("(c s) d -> s c d", s=64), uk("o")), hout[:])


import math

S = 8192
TC = 64
NC_ = S // TC


def host_constsD():
    seg = np.ones((128, TC), np.float32)
    seg[:, 0] = 0.0
    return {"c_jvec": np.broadcast_to(np.arange(65, dtype=np.float32)[None, :], (128, 65)).copy(),
            "c_seg": seg, "c_identf": np.eye(128, dtype=np.float32)}


def bc(ap, shape, axis):
    return ap.unsqueeze(axis).broadcast_to(list(shape))


def mixer_D(k, ident, ones):
    du = k.din("du_g", [2, 32, S], BF16)
    lam = k.din("d_lam", [128, 2, 3], F32)
    bmat = k.din("d_b", [128, 2, 2, 16], F32)
    cmat = k.din("d_c", [128, 2, 2, 16], F32)
    dsk = k.din("d_sk", [32, 2], F32)
    jvd = k.din("c_jvec", [128, 65], F32)
    segd = k.din("c_seg", [128, TC], F32)
    identd = k.din("c_identf", [128, 128], F32)
    ys = k.dout("ys5", [2, 32, S], F32)

    identf = k.sb("identf", [128, 128], F32)
    k.dma(identf[:], identd)
    jv = k.sb("jv", [128, 65], F32)
    k.dma(jv[:], jvd)
    seg1 = k.sb("seg1", [128, TC], BF16)
    k.dma(seg1[:], segd, q='pool')
    seg = k.sb("seg", [128, NC_, TC], BF16)
    k.copy(seg[:], bc(seg1[:], [128, NC_, TC], 1), eng='pool')
    lam_s = k.sb("lam_s", [128, 2, 3], F32)
    k.dma(lam_s[:], lam)
    b_s = k.sb("b_s", [128, 2, 2, 16], F32)
    k.dma(b_s[:], bmat)
    c_s = k.sb("c_s", [128, 2, 2, 16], F32)
    k.dma(c_s[:], cmat)
    dsk_s = k.sb("dsk_s", [32, 2], F32)
    k.dma(dsk_s[:], dsk)

    arena = k.sb("arena", [128, 2 * S], BF16)
    Zre = V(arena[:, 0:S], "arena")
    Zim = V(arena[:, S:2 * S], "arena")
    ybuf = V(arena[0:32, :].bitcast(F32), "arena")
    Cre = k.sb("Cre", [128, S], BF16)
    Cim = k.sb("Cim", [128, S], BF16)
    Wtab = k.sb("Wtab", [128, TC, 2, 32], F32)
    WT = k.sb("WT", [32, TC, 2, 128], BF16)
    G = k.sb("G", [128, TC + 1, 2, 32], BF16)
    u = k.sb("u", [32, S], BF16)
    dskd = k.sb("dskd", [32, 32], BF16)

    _cols = {}

    def col(name):
        if name not in _cols:
            _cols[name] = k.sb(name, [128, 1], F32)
        return _cols[name]

    for t in range(2):
        k.dma(u[:], du[t])
        lr = lam_s[:, t, 0:1]
        li = lam_s[:, t, 1:2]
        dt = col("dt")
        k.act(dt[:], lam_s[:, t, 2:3], AF.Exp)
        ldt = col("ldt")
        k.tt(ldt[:], lr, dt[:], ALU.mult)
        nldt = col("nldt")
        k.ts(nldt[:], ldt[:], -1.0, ALU.mult)
        wdt = col("wdt")
        k.tt(wdt[:], li, dt[:], ALU.mult)
        magp = k.sb("magp", [128, 65], F32) if t == 0 else magp
        magn = k.sb("magn", [128, 65], F32) if t == 0 else magn
        k.act(magp[:], jv[:], AF.Exp, scale=ldt[:, 0:1])
        k.act(magn[:], jv[:], AF.Exp, scale=nldt[:, 0:1])
        ang = k.sb("angd", [128, 65], F32) if t == 0 else ang
        k.ts(ang[:], jv[:], wdt[:, 0:1], ALU.mult)
        cosj = k.sb("cosj", [128, 65], F32) if t == 0 else cosj
        sinj = k.sb("sinj", [128, 65], F32) if t == 0 else sinj
        sa = k.sb("tga", [128, 65], F32) if t == 0 else sa
        si = k.sb("tgi", [128, 65], I32) if t == 0 else si
        trig(k, cosj, ang, 128, 65, math.pi / 2, sa, si)
        trig(k, sinj, ang, 128, 65, 0.0, sa, si)
        are = k.sb("are", [128, 65], F32) if t == 0 else are
        aim = k.sb("aim", [128, 65], F32) if t == 0 else aim
        nre = k.sb("nre", [128, 65], F32) if t == 0 else nre
        nim = k.sb("nim", [128, 65], F32) if t == 0 else nim
        k.tt(are[:], magp[:], cosj[:], ALU.mult)
        k.tt(aim[:], magp[:], sinj[:], ALU.mult)
        k.tt(nre[:], magn[:], cosj[:], ALU.mult)
        k.stt(nim[:], magn[:], -1.0, sinj[:], ALU.mult, ALU.mult)
        den = col("den")
        k.tt(den[:], lr, lr, ALU.mult)
        k.stt(den[:], li, li, den[:], ALU.mult, ALU.add)
        k.recip(den[:], den[:])
        ar1 = col("ar1")
        k.ts(ar1[:], are[:, 1:2], -1.0, ALU.add)
        fre = col("fre")
        k.tt(fre[:], ar1[:], lr, ALU.mult)
        k.stt(fre[:], aim[:, 1:2], li, fre[:], ALU.mult, ALU.add)
        k.tt(fre[:], fre[:], den[:], ALU.mult)
        fim = col("fim")
        k.tt(fim[:], aim[:, 1:2], lr, ALU.mult)
        tmpc = col("tmpc")
        k.tt(tmpc[:], ar1[:], li, ALU.mult)
        k.tt(fim[:], fim[:], tmpc[:], ALU.subtract)
        k.tt(fim[:], fim[:], den[:], ALU.mult)
        nfim = col("nfim")
        k.ts(nfim[:], fim[:], -1.0, ALU.mult)
        Bb = k.sb("Bb", [128, 2, 16], F32) if t == 0 else Bb
        bre, bim = b_s[:, t, 0, :], b_s[:, t, 1, :]
        k.ts(V(Bb[:, 0, :], "Bb"), bre, fre[:, 0:1], ALU.mult)
        k.stt(V(Bb[:, 0, :], "Bb"), bim, nfim[:, 0:1], V(Bb[:, 0, :], "Bb"), ALU.mult, ALU.add)
        k.ts(V(Bb[:, 1, :], "Bb"), bim, fre[:, 0:1], ALU.mult)
        k.stt(V(Bb[:, 1, :], "Bb"), bre, fim[:, 0:1], V(Bb[:, 1, :], "Bb"), ALU.mult, ALU.add)
        k.memset(Wtab[:], 0.0, eng='pool')
        k.memset(G[:], 0.0, eng='pool')
        tA = k.sb("tA", [128, TC + 1, 16], F32) if t == 0 else tA
        tB = k.sb("tB", [128, TC + 1, 16], F32) if t == 0 else tB
        for gi in range(2):
            ps_ = slice(gi * 64, (gi + 1) * 64)
            cs_ = slice(gi * 16, (gi + 1) * 16)
            shp = [64, TC, 16]
            n_re = bc(nre[ps_, 0:TC], shp, 2)
            n_im = bc(nim[ps_, 0:TC], shp, 2)
            B_re = bc(V(Bb[ps_, 0, :], "Bb").ap, shp, 1)
            B_im = bc(V(Bb[ps_, 1, :], "Bb").ap, shp, 1)
            tAv = V(tA[ps_, 0:TC, :], "tA")
            tBv = V(tB[ps_, 0:TC, :], "tB")
            k.tt(tAv, V(n_re, "nre"), V(B_re, "Bb"), ALU.mult)
            k.tt(tBv, V(n_im, "nim"), V(B_im, "Bb"), ALU.mult)
            k.tt(V(Wtab[ps_, :, 0, cs_], "Wtab"), tAv, tBv, ALU.subtract)
            k.tt(tAv, V(n_re, "nre"), V(B_im, "Bb"), ALU.mult)
            k.tt(tBv, V(n_im, "nim"), V(B_re, "Bb"), ALU.mult)
            k.tt(V(Wtab[ps_, :, 1, cs_], "Wtab"), tAv, tBv, ALU.add)
            shp2 = [64, TC + 1, 16]
            a_re = bc(are[ps_, :], shp2, 2)
            a_im = bc(aim[ps_, :], shp2, 2)
            C_re = bc(c_s[ps_, t, 0, :], shp2, 1)
            C_im = bc(c_s[ps_, t, 1, :], shp2, 1)
            tAv2 = V(tA[ps_, :, :], "tA")
            tBv2 = V(tB[ps_, :, :], "tB")
            k.tt(tAv2, V(a_re, "are"), V(C_re, "c_s"), ALU.mult)
            k.tt(tBv2, V(a_im, "aim"), V(C_im, "c_s"), ALU.mult)
            k.tt(V(G[ps_, :, 0, cs_], "G"), tAv2, tBv2, ALU.subtract)
            k.tt(tAv2, V(a_re, "are"), V(C_im, "c_s"), ALU.mult)
            k.tt(tBv2, V(a_im, "aim"), V(C_re, "c_s"), ALU.mult)
            k.stt(V(G[ps_, :, 1, cs_], "G"), tAv2, -1.0, tBv2, ALU.mult, ALU.subtract)
        for i4 in range(TC // 2):
            pt = k.psum()
            for q_ in range(2):
                i = i4 * 2 + q_
                for ri in range(2):
                    k.tr(pt[0:32, (q_ * 2 + ri) * 128:(q_ * 2 + ri + 1) * 128], V(Wtab[:, i, ri, :], "Wtab"), identf[:])
            k.copy(V(WT[:, i4 * 2:i4 * 2 + 2, :, :], "WT"), pt[0:32, :].rearrange("p (i r m) -> p i r m", i=2, r=2),
                   eng=('act' if i4 % 2 else 'dve'))
        idb = k.sb("idb", [32, 32], F32) if t == 0 else idb
        k.ts(idb[:], identf[0:32, 0:32], dsk_s[:, t:t + 1], ALU.mult)
        k.copy(dskd[:], idb[:])
        for i4 in range(TC // 4):
            pzr = k.psum()
            pzi = k.psum()
            for q_ in range(4):
                i = i4 * 4 + q_
                rhs = u[:, i:i + TC * (NC_ - 1) + 1:TC]
                k.mm(pzr[:, q_ * 128:(q_ + 1) * 128], V(WT[:, i, 0, :], "WT"), rhs)
                k.mm(pzi[:, q_ * 128:(q_ + 1) * 128], V(WT[:, i, 1, :], "WT"), rhs)
            for (pz, Z, e_) in ((pzr, Zre, 'dve'), (pzi, Zim, 'act')):
                dst = V(Z.ap.rearrange("p (c i) -> p c i", i=TC)[:, :, i4 * 4:i4 * 4 + 4], "arena")
                k.copy(dst, pz[:].rearrange("p (i c) -> p c i", i=4), eng=e_)
        segf = V(seg[:].rearrange("p c i -> p (c i)"), "seg")
        k.scan(Cre[:], segf, Zre, 0.0, ALU.mult, ALU.add)
        k.scan(Cim[:], segf, Zim, 0.0, ALU.mult, ALU.add)
        def pl(name):
            return k.sb(name, [128, NC_], F32) if t == 0 else getattr(k, "_pl_" + name)
        Ere, Eim = pl("Ere"), pl("Eim")
        X0r, X0i, X1r, X1i = pl("X0r"), pl("X0i"), pl("X1r"), pl("X1i")
        for nm_, o_ in (("Ere", Ere), ("Eim", Eim), ("X0r", X0r), ("X0i", X0i), ("X1r", X1r), ("X1i", X1i)):
            setattr(k, "_pl_" + nm_, o_)
        k.copy(Ere[:], Cre[:, TC - 1:TC - 1 + TC * (NC_ - 1) + 1:TC])
        k.copy(Eim[:], Cim[:, TC - 1:TC - 1 + TC * (NC_ - 1) + 1:TC])
        a63r, a63i = are[:, 63:64], aim[:, 63:64]
        na63i = col("na63i")
        k.ts(na63i[:], a63i, -1.0, ALU.mult)
        k.ts(X0r[:], Ere[:], a63r, ALU.mult)
        k.stt(X0r[:], Eim[:], na63i[:, 0:1], X0r[:], ALU.mult, ALU.add)
        k.ts(X0i[:], Ere[:], a63i, ALU.mult)
        k.stt(X0i[:], Eim[:], a63r, X0i[:], ALU.mult, ALU.add)
        Akr, Aki, nAki, Ak2 = col("Akr"), col("Aki"), col("nAki"), col("Ak2")
        k.copy(Akr[:], are[:, 64:65])
        k.copy(Aki[:], aim[:, 64:65])
        cur = (X0r, X0i)
        nxt = (X1r, X1i)
        for st_ in range(7):
            s = 1 << st_
            k.ts(nAki[:], Aki[:], -1.0, ALU.mult)
            cr, ci_ = cur
            nr, ni = nxt
            k.copy(nr[:, 0:s], cr[:, 0:s])
            k.copy(ni[:, 0:s], ci_[:, 0:s])
            n_ = NC_ - s
            k.stt(nr[:, s:], cr[:, 0:n_], Akr[:, 0:1], cr[:, s:], ALU.mult, ALU.add)
            k.stt(nr[:, s:], ci_[:, 0:n_], nAki[:, 0:1], nr[:, s:], ALU.mult, ALU.add)
            k.stt(ni[:, s:], ci_[:, 0:n_], Akr[:, 0:1], ci_[:, s:], ALU.mult, ALU.add)
            k.stt(ni[:, s:], cr[:, 0:n_], Aki[:, 0:1], ni[:, s:], ALU.mult, ALU.add)
            cur, nxt = nxt, cur
            if st_ < 6:
                k.tt(Ak2[:], Aki[:], Aki[:], ALU.mult)
                k.tt(Aki[:], Akr[:], Aki[:], ALU.mult)
                k.ts(Aki[:], Aki[:], 2.0, ALU.mult)
                k.stt(Akr[:], Akr[:], Akr[:, 0:1], Ak2[:], ALU.mult, ALU.subtract)
        Xpr = k.sb("Xpr", [128, NC_], BF16) if t == 0 else Xpr
        Xpi = k.sb("Xpi", [128, NC_], BF16) if t == 0 else Xpi
        k.memset(V(Xpr[:, 0:1], "Xpr"), 0.0)
        k.memset(V(Xpi[:, 0:1], "Xpi"), 0.0)
        k.copy(V(Xpr[:, 1:NC_], "Xpr"), cur[0][:, 0:NC_ - 1])
        k.copy(V(Xpi[:, 1:NC_], "Xpi"), cur[1][:, 0:NC_ - 1])
        for j4 in range(TC // 4):
            py = k.psum()
            for q_ in range(4):
                j = j4 * 4 + q_
                o_ = py[0:32, q_ * 128:(q_ + 1) * 128]
                sl = slice(j, j + TC * (NC_ - 1) + 1, TC)
                k.mm(o_, V(G[:, j, 0, :], "G"), Cre[:, sl], start=True, stop=False)
                k.mm(o_, V(G[:, j, 1, :], "G"), Cim[:, sl], start=False, stop=False)
                k.mm(o_, V(G[:, j + 1, 0, :], "G"), Xpr[:], start=False, stop=False)
                k.mm(o_, V(G[:, j + 1, 1, :], "G"), Xpi[:], start=False, stop=False)
                k.mm(o_, dskd[:], u[:, sl], start=False, stop=True)
            dst = V(ybuf.ap.rearrange("p (c i) -> p c i", i=TC)[:, :, j4 * 4:j4 * 4 + 4], "arena")
            k.copy(dst, py[0:32, :].rearrange("p (i c) -> p c i", i=4), eng=('act' if j4 % 2 else 'dve'))
        k.dma(V(ys[t], uk("o")), ybuf)


FGROUPS = [("a_z", 4608, 512), ("b_z", 5760, 512), ("c_o", 7304, 512), ("c_z", 7816, 512), ("d_z", 8840, 512), ("gate", 9352, 8192)]
FCOLS = []
FOFF = {}
for (_n, _o, _w) in FGROUPS:
    FOFF[_n] = len(FCOLS)
    FCOLS.extend(range(_o, _o + _w))
FW = len(FCOLS)
GELU_C = 1.5957691216057308


def build_F():
    k = K()
    x = k.din("x", [T, 2048])
    w_f = k.din("w_f", [2048, FW])
    norm_g = k.din("norm_g", [1, 2048])
    mbd = k.din("merge_b", [128, 64])
    gbd = k.din("glu_b", [128, 4])
    glu_w = k.din("glu_w", [512, 512])
    w_up = k.din("w_up", [4, 512, 2048])
    w_out = k.din("w_out", [2048, 2048])
    yaT = k.din("yaT", [512, T])
    ybT = k.din("ybT", [512, T])
    ycT = k.din("ycT", [512, T])
    y5T = k.din("y5T", [512, T])
    xo = k.dout("xo", [T, 2048])

    ident, ones = consts(k)
    mT = k.sb("mT", [128, 16, T], BF16)
    mk = ("mT_0", "mT_1")
    ext = (V(mT[:, 0:4, :].rearrange("p a b -> p (a b)").bitcast(F32), mk),
           V(mT[:, 8:10, :].rearrange("p a b -> p (a b)"), mk),
           V(mT[:, 4:8, :].rearrange("p a b -> p (a b)").bitcast(F32), mk))
    hT = norm_hT(k, x, norm_g, ident, ext=ext)
    mb = k.sb("mb", [128, 64], F32)
    k.dma(mb[:], mbd)
    gb = k.sb("gb", [128, 4], F32)
    k.dma(gb[:], gbd)
    ysT = k.sb("ysT", [128, 16, T], BF16)

    def wload(c0, ncols):
        wt = k.pool("wt", [128, 16, 512], BF16, 2)
        k.dma(wt[:, :, 0:ncols], w_f[:, c0:c0 + ncols].rearrange("(kt p) c -> p kt c", p=128), q='pool')
        return wt

    def proj(wt, wc0, half):
        ps = k.psum()
        for kt in range(16):
            k.mm(ps[:], wt[:, kt, wc0:wc0 + 128], V(hT[:, kt, half * 512:(half + 1) * 512], hk(half)),
                 start=(kt == 0), stop=(kt == 15))
        return ps

    def ytile(src, wt_, half):
        t = k.pool("yt", [128, 512], F32, 2)
        k.dma(t[:], src[wt_ * 128:(wt_ + 1) * 128, half * 512:(half + 1) * 512])
        return t

    def yk(n, wt_, half):
        return "ysT_%d_%d" % (n * 4 + wt_, half)

    wz = {nm: wload(FOFF[nm], 512) for nm in ("a_z", "b_z")}
    for n, (nm, src) in enumerate((("a_z", yaT), ("b_z", ybT))):
        for wt_ in range(4):
            for half in range(NH):
                ps = proj(wz[nm], wt_ * 128, half)
                s = k.pool("sz", [128, 512], F32, 3)
                k.act(s[:], ps[:], AF.Silu)
                y = ytile(src, wt_, half)
                k.tt(V(ysT[:, n * 4 + wt_, half * 512:(half + 1) * 512], yk(n, wt_, half)), s[:], y[:], ALU.mult)
    wo = wload(FOFF["c_o"], 512)
    wc = wload(FOFF["c_z"], 512)
    for wt_ in range(4):
        for half in range(NH):
            ps = proj(wo, wt_ * 128, half)
            so = k.pool("sz", [128, 512], F32, 3)
            k.act(so[:], ps[:], AF.Sigmoid)
            ps2 = proj(wc, wt_ * 128, half)
            s = k.pool("sz", [128, 512], F32, 3)
            k.act(s[:], ps2[:], AF.Silu)
            y = ytile(ycT, wt_, half)
            k.tt(s[:], s[:], so[:], ALU.mult)
            k.tt(V(ysT[:, 8 + wt_, half * 512:(half + 1) * 512], yk(2, wt_, half)), s[:], y[:], ALU.mult)
    gdb = k.sb("gdb", [128, 4, T], BF16)
    for wt_ in range(4):
        for half in range(NH):
            y = ytile(y5T, wt_, half)
            t1 = k.pool("sz", [128, 512], F32, 3)
            k.tt(t1[:], y[:], y[:], ALU.mult)
            k.ts(t1[:], t1[:], 0.044715, ALU.mult, 1.0, ALU.add)
            k.tt(t1[:], t1[:], y[:], ALU.mult)
            k.act(t1[:], t1[:], AF.Sigmoid, scale=GELU_C)
            k.tt(V(gdb[:, wt_, half * 512:(half + 1) * 512], "gdb%d_%d" % (wt_, half)), t1[:], y[:], ALU.mult)
    wg = k.sb("wglu", [128, 4, 512], BF16)
    k.dma(wg[:], glu_w.rearrange("(kt p) c -> p kt c", p=128), q='pool')
    wd = wload(FOFF["d_z"], 512)
    for ob in range(4):
        for half in range(NH):
            ps = k.psum()
            for kt in range(4):
                k.mm(ps[:], wg[:, kt, ob * 128:(ob + 1) * 128],
                     V(gdb[:, kt, half * 512:(half + 1) * 512], "gdb%d_%d" % (kt, half)), start=(kt == 0), stop=(kt == 3))
            sg = k.pool("sz", [128, 512], F32, 3)
            k.act(sg[:], ps[:], AF.Sigmoid, bias=gb[:, ob:ob + 1])
            ps2 = proj(wd, ob * 128, half)
            s = k.pool("sz", [128, 512], F32, 3)
            k.act(s[:], ps2[:], AF.Silu)
            k.tt(s[:], s[:], sg[:], ALU.mult)
            k.tt(V(ysT[:, 12 + ob, half * 512:(half + 1) * 512], yk(3, ob, half)), s[:],
                 V(gdb[:, ob, half * 512:(half + 1) * 512], "gdb%d_%d" % (ob, half)), ALU.mult)

    g0 = FOFF["gate"]
    w_up_r = w_up.rearrange("n (wt p) d -> n p wt d", p=128)
    for db in range(16):
        wt = k.pool("wt", [128, 16, 512], BF16, 2)
        wu = k.pool("wup", [128, 4, 4, 128], BF16, 2)
        for n in range(4):
            c0 = g0 + n * 2048 + db * 128
            k.dma(wt[:, :, n * 128:(n + 1) * 128],
                  w_f[:, c0:c0 + 128].rearrange("(kt p) c -> p kt c", p=128), q='pool')
            k.dma(wu[:, n, :, :], w_up_r[n, :, :, db * 128:(db + 1) * 128], q='pool')
        for half in range(NH):
            acc = k.pool("acc", [128, 512], F32, 2)
            for n in range(4):
                pg = k.psum()
                for kt in range(16):
                    k.mm(pg[:], wt[:, kt, n * 128:(n + 1) * 128],
                         V(hT[:, kt, half * 512:(half + 1) * 512], hk(half)), start=(kt == 0), stop=(kt == 15))
                pu = k.psum()
                for wt_ in range(4):
                    k.mm(pu[:], wu[:, n, wt_, :],
                         V(ysT[:, n * 4 + wt_, half * 512:(half + 1) * 512], yk(n, wt_, half)), start=(wt_ == 0), stop=(wt_ == 3))
                sg = k.pool("sz", [128, 512], F32, 3)
                k.act(sg[:], pg[:], AF.Sigmoid, bias=mb[:, n * 16 + db:n * 16 + db + 1])
                if n == 0:
                    k.tt(acc[:], sg[:], pu[:], ALU.mult)
                elif n < 3:
                    k.tt(sg[:], sg[:], pu[:], ALU.mult)
                    k.tt(acc[:], acc[:], sg[:], ALU.add)
                else:
                    k.tt(sg[:], sg[:], pu[:], ALU.mult)
                    k.tt(V(mT[:, db, half * 512:(half + 1) * 512], "mT_%d" % half), acc[:], sg[:], ALU.add)

    for cg in range(4):
        wt = k.pool("wt", [128, 16, 512], BF16, 2)
        k.dma(wt[:], w_out[:, cg * 512:(cg + 1) * 512].rearrange("(kt p) c -> p kt c", p=128), q='pool')
        for tt in range(T // 128):
            ps = k.psum()
            for kt in range(16):
                k.mm(ps[:], V(mT[:, kt, tt * 128:(tt + 1) * 128], "mT_%d" % (tt // 4)), wt[:, kt, :], start=(kt == 0), stop=(kt == 15))
            xt = k.pool("yt", [128, 512], F32, 2)
            k.dma(xt[:], x[tt * 128:(tt + 1) * 128, cg * 512:(cg + 1) * 512])
            o = k.pool("sz", [128, 512], F32, 3)
            k.tt(o[:], xt[:], ps[:], ALU.add)
            k.dma(V(xo[tt * 128:(tt + 1) * 128, cg * 512:(cg + 1) * 512], uk("o")), o[:])
    return k.finish()


_PROGS = {}
import sys as _sys
import time as _time
_T0 = [_time.time()]


def _log(msg):
    print("[kernel %.1fs] %s" % (_time.time() - _T0[0], msg), file=_sys.stderr, flush=True)


def _prog(name):
    if name not in _PROGS:
        if name == "P":
            _PROGS[name] = build_P()
        elif name == "AB":
            _PROGS[name] = build_M("AB")
        elif name == "C":
            _PROGS[name] = build_M("C")
        elif name == "D":
            _PROGS[name] = build_M("D")
        elif name == "F":
            _PROGS[name] = build_F()
    return _PROGS[name]


def _c(a, dt=None):
    return np.ascontiguousarray(a) if dt is None else np.ascontiguousarray(a, dtype=dt)


def _prep_C(core, cqk_full, cv_full, cif_full, conv_w, conv_b, i_b, f_b):
    hc, dh = core // 2, core % 2
    qch = slice(hc * 64, hc * 64 + 64)
    kch = slice(256 + hc * 64, 256 + hc * 64 + 64)
    z3 = np.zeros((64, 3), np.float32)
    m = dict(cq_h=_c(np.concatenate([z3, cqk_full[qch]], 1)), ck_h=_c(np.concatenate([z3, cqk_full[kch]], 1)),
             cw_q=_c(conv_w[:, qch].T), cb_q=_c(conv_b[qch][:, None]),
             cw_k=_c(conv_w[:, kch].T), cb_k=_c(conv_b[kch][:, None]),
             cv_h=_c(cv_full[:, hc * 128 + dh * 64: hc * 128 + dh * 64 + 64]),
             gi=_c(cif_full[hc].reshape(128, 64)), gf=_c(cif_full[4 + hc].reshape(128, 64)),
             gb=_c(np.stack([i_b[hc:hc + 1], f_b[hc:hc + 1]], axis=1), np.float32))
    m.update(host_constsC())
    return m


def _prep_D(core, du_full, W):
    g0 = 4 * core
    lam = np.zeros((128, 2, 3), np.float32)
    bm = np.zeros((128, 2, 2, 16), np.float32)
    cm = np.zeros((128, 2, 2, 16), np.float32)
    dsk = np.zeros((32, 2), np.float32)
    du = np.zeros((2, 32, 8192), du_full.dtype)
    for t in range(2):
        for gi in range(2):
            g = g0 + 2 * t + gi
            ps = slice(gi * 64, gi * 64 + 64)
            lam[ps, t, 0] = W['d_lam_re'][g]
            lam[ps, t, 1] = W['d_lam_im'][g]
            lam[ps, t, 2] = W['d_log_dt'][g]
            bm[ps, t, 0] = W['d_b_re'][g]
            bm[ps, t, 1] = W['d_b_im'][g]
            cm[ps, t, 0] = W['d_c_re'][g].T
            cm[ps, t, 1] = W['d_c_im'][g].T
            dsk[gi * 16:(gi + 1) * 16, t] = W['d_skip'][g * 16:(g + 1) * 16]
            du[t, gi * 16:(gi + 1) * 16] = du_full[g * 16:(g + 1) * 16]
    m = dict(du_g=du, d_lam=lam, d_b=bm, d_c=cm, d_sk=dsk)
    m.update(host_constsD())
    return m


def kernel(x, positions, norm_g, w_in, a_qn_g, a_kn_g, b_cq_g, b_ckv_g, b_w_uq, b_w_ukv, b_qn_g, b_kn_g,
           c_conv_w, c_conv_b, c_i_b, c_f_b, d_lam_re, d_lam_im, d_log_dt, d_b_re, d_b_im, d_c_re, d_c_im,
           d_skip, d_glu_w, d_glu_b, w_up, merge_b, w_out):
    f32 = np.float32
    cur = np.asarray(x, f32)[0]
    pos = np.asarray(positions).astype(np.int32)
    eye = np.eye(128, dtype=f32)
    invf = (10000.0 ** (-(np.arange(0, 64, 2, dtype=f32)) / 64)).astype(f32)
    c_invf = np.concatenate([invf, invf])[:, None].astype(f32)
    c_sgn = np.concatenate([-np.ones(32), np.ones(32)])[:, None].astype(f32)
    biasA = [host_biasA(c) for c in range(NCORES)]
    maskB = [host_maskB(p) for p in range(2)]
    TS = [slice(c * T, (c + 1) * T) for c in range(NCORES)]
    for l in range(4):
        g = lambda a: np.asarray(a[l], f32)
        w_p = _c(g(w_in)[:, PCOLS])
        maps = []
        for c in range(NCORES):
            m = dict(c_ident=eye, c_invf=c_invf, c_sgn=c_sgn, x=_c(cur[TS[c]]), w_p=w_p, pos=_c(pos[:, TS[c]]),
                     norm_g=_c(g(norm_g)[None]), a_qn_g=_c(g(a_qn_g)[None]), a_kn_g=_c(g(a_kn_g)[None]),
                     b_cq_g=_c(g(b_cq_g)[None]), b_ckv_g=_c(g(b_ckv_g)[None]), b_qn_g=_c(g(b_qn_g)[None]),
                     b_kn_g=_c(g(b_kn_g)[None]), b_w_uq=_c(g(b_w_uq)), b_w_ukv=_c(g(b_w_ukv)))
            maps.append(m)
        _log("layer %d P" % l)
        rp = run(_prog("P"), maps).results
        del w_p, maps
        qfull = np.concatenate([rp[c]["qb"] for c in range(NCORES)], axis=2)
        kfull = np.concatenate([rp[c]["kb"] for c in range(NCORES)], axis=2)
        vfull = np.concatenate([rp[c]["vb"] for c in range(NCORES)], axis=0)
        maps = []
        toks = []
        for c in range(NCORES):
            def ext(name, axis):
                parts = []
                for cc in (c - 2, c - 1, c):
                    a = rp[max(cc, 0)][name]
                    parts.append(a if cc >= 0 else np.zeros_like(a))
                return _c(np.concatenate(parts, axis=axis))
            h, par = c // 2, c % 2
            tk_ = np.concatenate([np.arange(512 * (2 * j + par), 512 * (2 * j + par) + 512) for j in range(8)])
            toks.append(tk_)
            m = dict(c_ident=eye, qa=_c(rp[c]["qa"]), ka_ext=ext("ka", 2), va_ext=ext("va", 0), c_biasA=biasA[c],
                     qb_h=_c(qfull[h][:, tk_]), kb_h=_c(kfull[h]), vb_h=_c(vfull[:, h * 128:(h + 1) * 128]),
                     c_maskB=maskB[par])
            maps.append(m)
        _log("layer %d AB" % l)
        rab = run(_prog("AB"), maps).results
        del maps, qfull, kfull, vfull
        yaT = [rab[c]["ya"] for c in range(NCORES)]
        ybT = np.zeros((512, 8192), f32)
        for c in range(NCORES):
            h = c // 2
            ybT[h * 128:(h + 1) * 128, toks[c]] = rab[c]["yb"]
        cqk_full = np.concatenate([rp[c]["cqk"] for c in range(NCORES)], 1)
        cv_full = np.concatenate([rp[c]["cv"] for c in range(NCORES)], 0)
        cif_full = np.concatenate([rp[c]["cif"] for c in range(NCORES)], 1)
        maps = []
        for c in range(NCORES):
            m = _prep_C(c, cqk_full, cv_full, cif_full, g(c_conv_w), g(c_conv_b), g(c_i_b), g(c_f_b))
            m["c_ident"] = eye
            maps.append(m)
        _log("layer %d C" % l)
        rc = run(_prog("C"), maps).results
        ycT = np.zeros((512, 8192), f32)
        for c in range(NCORES):
            hc_, dh = c // 2, c % 2
            ycT[hc_ * 128 + dh * 64: hc_ * 128 + dh * 64 + 64, :] = rc[c]["hc"].T
        du_full = np.concatenate([rp[c]["du"] for c in range(NCORES)], 1)
        Wd = dict(d_lam_re=g(d_lam_re), d_lam_im=g(d_lam_im), d_log_dt=g(d_log_dt), d_b_re=g(d_b_re), d_b_im=g(d_b_im),
                  d_c_re=g(d_c_re), d_c_im=g(d_c_im), d_skip=g(d_skip))
        maps = []
        for c in range(NCORES):
            m = _prep_D(c, du_full, Wd)
            m["c_ident"] = eye
            maps.append(m)
        _log("layer %d D" % l)
        rd = run(_prog("D"), maps).results
        y5T = np.zeros((512, 8192), f32)
        for c in range(NCORES):
            for t in range(2):
                r0 = (4 * c + 2 * t) * 16
                y5T[r0:r0 + 32, :] = rd[c]["ys5"][t]
        del rp, maps
        w_f = _c(g(w_in)[:, FCOLS])
        mb = _c(g(merge_b).reshape(64, 128).T)
        gbv = _c(g(d_glu_b).reshape(4, 128).T)
        maps = []
        for c in range(NCORES):
            m = dict(c_ident=eye, x=_c(cur[TS[c]]), w_f=w_f, norm_g=_c(g(norm_g)[None]), merge_b=mb, glu_b=gbv,
                     glu_w=_c(g(d_glu_w)), w_up=_c(g(w_up)), w_out=_c(g(w_out)),
                     yaT=_c(yaT[c]), ybT=_c(ybT[:, TS[c]]), ycT=_c(ycT[:, TS[c]]), y5T=_c(y5T[:, TS[c]]))
            maps.append(m)
        _log("layer %d F" % l)
        rf = run(_prog("F"), maps).results
        cur = np.concatenate([rf[c]["xo"] for c in range(NCORES)], axis=0).astype(f32)
        del w_f, maps, rf
    return cur[None].astype(np.float32)
```

```python
import contextlib
import numpy as np
import concourse.bass as bass
import concourse.mybir as mybir
from concourse.bass_utils import run_bass_kernel_spmd

F32 = mybir.dt.float32
BF16 = mybir.dt.bfloat16
I32 = mybir.dt.int32
AF = mybir.ActivationFunctionType
ALU = mybir.AluOpType
AX = mybir.AxisListType
EPS = 1e-6
NCORES = 8


class Sched:
    CE = ('pe', 'act', 'dve', 'pool', 'sp')
    KDMA = 8

    def __init__(self):
        self.ops = {e: [] for e in self.CE}
        self.last_w = {}
        self.readers = {}
        self.waited = {e: {} for e in self.CE}
        self.ndma = {e: 0 for e in self.CE}
        self.fragile = set()

    def _deps(self, E, reads, writes):
        toks = []
        for b in reads:
            t = self.last_w.get(b)
            if t is not None:
                toks.append(t)
            if b.startswith("ps"):
                toks.extend(v for kk, v in self.readers.get(b, {}).items() if kk != E)
        for b in writes:
            t = self.last_w.get(b)
            if t is not None:
                toks.append(t)
            toks.extend(self.readers.get(b, {}).values())
        out = []
        for (key, val) in toks:
            if key == E and E == 'pe':
                continue
            if self.waited[E].get(key, -1) >= val:
                continue
            self.waited[E][key] = val
            out.append((key, val))
        return out

    def _commit(self, tok, reads, writes):
        for b in reads:
            d = self.readers.setdefault(b, {})
            k = tok[0]
            if k not in d or d[k][1] < tok[1]:
                d[k] = tok
        for b in writes:
            self.last_w[b] = tok
            self.readers[b] = {}

    def op(self, E, fn, reads=(), writes=(), fragile=False):
        waits = self._deps(E, reads, writes)
        idx = len(self.ops[E])
        self.ops[E].append(dict(fn=fn, waits=waits, dma=None, signal=False))
        if fragile and E != 'pe':
            self.fragile.add((E, idx))
        self._commit((E, idx), reads, writes)

    def dma(self, Q, fn, reads=(), writes=()):
        n = self.ndma[Q]
        self.ndma[Q] += 1
        slot = n % self.KDMA
        key = ('dma', Q, slot)
        waits = self._deps(Q, reads, writes)
        if n >= self.KDMA:
            v = 16 * (n // self.KDMA)
            if self.waited[Q].get(key, -1) < v:
                self.waited[Q][key] = v
                waits.append((key, v))
        self.ops[Q].append(dict(fn=fn, waits=waits, dma=key, signal=False))
        self._commit((key, 16 * (n // self.KDMA + 1)), reads, writes)

    def final_wait(self, E, bufs=()):
        waits = []
        for q in self.CE:
            n = self.ndma[q]
            for s in range(min(n, self.KDMA)):
                cnt = (n - s + self.KDMA - 1) // self.KDMA
                waits.append((('dma', q, s), 16 * cnt))
        self.ops[E].append(dict(fn=None, waits=waits, dma=None, signal=False))

    def emit(self, nc, st):
        for e in self.CE:
            for o in self.ops[e]:
                for (key, val) in o['waits']:
                    if isinstance(key, str):
                        self.ops[key][val]['signal'] = True
        rank = {}
        for e in self.CE:
            c = 0
            r = []
            for o in self.ops[e]:
                if o['signal'] and o['dma'] is None:
                    c += 1
                r.append(c)
            rank[e] = r
        sems = {e: st.enter_context(nc.semaphore("s_" + e)) for e in self.CE}
        for q in self.CE:
            if self.ndma[q]:
                for s in range(self.KDMA):
                    sems[('dma', q, s)] = st.enter_context(nc.semaphore("d_%s%d" % (q, s)))
        block = st.enter_context(nc.Block())

        def run(e):
            def body(eng):
                for o in self.ops[e]:
                    for (key, val) in o['waits']:
                        if isinstance(key, str):
                            eng.wait_ge(sems[key], rank[key][val])
                        else:
                            eng.wait_ge(sems[key], val)
                    if o['fn'] is None:
                        continue
                    inst = o['fn'](eng)
                    if o['dma'] is not None:
                        inst.then_inc(sems[o['dma']], 16)
                    elif o['signal']:
                        inst.then_inc(sems[e], 1)
            return body
        if self.ops['sp']:
            block.sync(run('sp'))
        if self.ops['pe']:
            block.tensor(run('pe'))
        if self.ops['act']:
            block.scalar(run('act'))
        if self.ops['dve']:
            block.vector(run('dve'))
        if self.ops['pool']:
            block.gpsimd(run('pool'))


class V:
    def __init__(self, ap, key):
        self.ap = ap
        self.key = key


def _ap(x):
    return x.ap if isinstance(x, V) else x


def _key(x):
    if isinstance(x, V):
        return list(x.key) if isinstance(x.key, (tuple, list)) else [x.key]
    return [x.tensor.name]


def _frag(out, accum=None):
    try:
        n = _ap(out).free_size()
    except Exception:
        n = 0
    return (n < 256) or (accum is not None)


class K:
    def __init__(self):
        self.nc = bass.Bass("TRN2", target_bir_lowering=False)
        self.S = Sched()
        self.st = contextlib.ExitStack()
        self.pools = {}
        self.outs = []
        self.psb = None
        self.psi = 0

    def din(self, name, shape, dt=F32):
        return self.nc.dram_tensor(name, list(shape), dt, kind="ExternalInput").ap()

    def dout(self, name, shape, dt=F32):
        ap = self.nc.dram_tensor(name, list(shape), dt, kind="ExternalOutput").ap()
        self.outs.append(name)
        return ap

    def sb(self, name, shape, dt=F32):
        return self.st.enter_context(self.nc.sbuf_tensor(name, list(shape), dt))

    def pool(self, name, shape, dt, n):
        if name not in self.pools:
            self.pools[name] = [[self.sb("%s_%d" % (name, i), shape, dt) for i in range(n)], 0]
        p = self.pools[name]
        t = p[0][p[1] % len(p[0])]
        p[1] += 1
        return t

    def psum(self, lo=0, hi=8):
        if self.psb is None:
            self.psb = [self.st.enter_context(self.nc.psum_tensor("ps%d" % i, [128, 512], F32)) for i in range(8)]
        t = self.psb[lo + self.psi % (hi - lo)]
        self.psi += 1
        return t

    def bank(self, i):
        self.psum(0, 8) if self.psb is None else None
        return self.psb[i]

    def dma(self, out, in_, q='sp', **kw):
        o, i = _ap(out), _ap(in_)
        self.S.dma(q, lambda e: e.dma_start(out=o, in_=i, **kw), reads=_key(in_), writes=_key(out))

    def mm(self, out, lhsT, rhs, start=True, stop=True, **kw):
        o, l, r = _ap(out), _ap(lhsT), _ap(rhs)
        self.S.op('pe', lambda e: e.matmul(o, lhsT=l, rhs=r, start=start, stop=stop, **kw),
                  reads=_key(lhsT) + _key(rhs) + ([] if start else _key(out)), writes=_key(out))

    def tr(self, out, in_, ident):
        o, i, d = _ap(out), _ap(in_), _ap(ident)
        self.S.op('pe', lambda e: e.transpose(o, i, d), reads=_key(in_) + _key(ident), writes=_key(out))

    def act(self, out, in_, func, bias=0.0, scale=1.0, accum_out=None, eng='act'):
        o, i = _ap(out), _ap(in_)
        rd = _key(in_)
        wr = _key(out)
        b = bias
        s = scale
        if not isinstance(bias, (int, float)):
            rd += _key(bias)
            b = _ap(bias)
        if not isinstance(scale, (int, float)):
            rd += _key(scale)
            s = _ap(scale)
        kw = {}
        if accum_out is not None:
            kw['accum_out'] = _ap(accum_out)
            wr += _key(accum_out)
        if func == AF.Copy:
            self.S.op(eng, lambda e: e.activation(out=o, in_=i, func=func, scale=s, **kw), reads=rd, writes=wr, fragile=_frag(out, accum_out))
        else:
            self.S.op(eng, lambda e: e.activation(out=o, in_=i, func=func, bias=b, scale=s, **kw), reads=rd, writes=wr, fragile=_frag(out, accum_out))

    def tt(self, out, in0, in1, op, eng='dve'):
        o, a, b = _ap(out), _ap(in0), _ap(in1)
        self.S.op(eng, lambda e: e.tensor_tensor(out=o, in0=a, in1=b, op=op),
                  reads=_key(in0) + _key(in1), writes=_key(out), fragile=_frag(out))

    def ts(self, out, in0, s1, op0, s2=None, op1=None, eng='dve', accum_out=None):
        o, a = _ap(out), _ap(in0)
        rd = _key(in0)
        wr = _key(out)
        x1, x2 = s1, s2
        if not isinstance(s1, (int, float)):
            rd += _key(s1)
            x1 = _ap(s1)
        if s2 is not None and not isinstance(s2, (int, float)):
            rd += _key(s2)
            x2 = _ap(s2)
        kw = {}
        if op1 is not None:
            kw['op1'] = op1
        if accum_out is not None:
            kw['accum_out'] = _ap(accum_out)
            wr += _key(accum_out)
        self.S.op(eng, lambda e: e.tensor_scalar(out=o, in0=a, scalar1=x1, scalar2=x2, op0=op0, **kw),
                  reads=rd, writes=wr, fragile=_frag(out, accum_out))

    def stt(self, out, in0, scalar, in1, op0, op1, eng='dve'):
        eng = 'dve'
        o, a, b = _ap(out), _ap(in0), _ap(in1)
        rd = _key(in0) + _key(in1)
        s = scalar
        if not isinstance(scalar, (int, float)):
            rd += _key(scalar)
            s = _ap(scalar)
        self.S.op(eng, lambda e: e.scalar_tensor_tensor(out=o, in0=a, scalar=s, in1=b, op0=op0, op1=op1),
                  reads=rd, writes=_key(out), fragile=_frag(out))

    def copy(self, out, in_, eng='dve'):
        o, i = _ap(out), _ap(in_)
        if eng == 'act':
            self.S.op('act', lambda e: e.copy(out=o, in_=i), reads=_key(in_), writes=_key(out), fragile=_frag(out))
        else:
            self.S.op(eng, lambda e: e.tensor_copy(out=o, in_=i), reads=_key(in_), writes=_key(out), fragile=_frag(out))

    def memset(self, out, val, eng='dve'):
        o = _ap(out)
        self.S.op(eng, lambda e: e.memset(o, val), reads=[], writes=_key(out), fragile=_frag(out))

    def recip(self, out, in_):
        o, i = _ap(out), _ap(in_)
        self.S.op('dve', lambda e: e.reciprocal(out=o, in_=i), reads=_key(in_), writes=_key(out), fragile=_frag(out))

    def scan(self, out, d0, d1, initial, op0, op1):
        o, a, b = _ap(out), _ap(d0), _ap(d1)
        rd = _key(d0) + _key(d1)
        ini = initial
        if not isinstance(initial, (int, float)):
            rd += _key(initial)
            ini = _ap(initial)
        self.S.op('dve', lambda e: e.tensor_tensor_scan(out=o, data0=a, data1=b, initial=ini, op0=op0, op1=op1),
                  reads=rd, writes=_key(out), fragile=_frag(out))

    def finish(self):
        rem = self.nc.sbuf_bytes_remaining
        rem = rem() if callable(rem) else rem
        used = 229376 - rem
        print("SBUF used per partition: %.1f KB" % (used / 1024.0))
        assert used <= 190 * 1024, "SBUF over budget (top of SBUF is reserved: DMA rings)"
        self.S.final_wait('sp')
        self.S.emit(self.nc, self.st)
        self.st.close()
        return self.nc


def run(nc, in_maps, trace=False):
    return run_bass_kernel_spmd(nc, in_maps, core_ids=list(range(len(in_maps))), trace=trace)


import math

T = 1024
NH = T // 512
OFF = dict(a_q=0, a_k=1536, a_v=3072, a_z=4608, b_cq=5120, b_ckv=5568, b_kr=5696, b_z=5760,
           c_qk=6272, c_v=6784, c_i=7296, c_f=7300, c_o=7304, c_z=7816, d_u=8328, d_z=8840, gate=9352)
IN_W = 17544
PGROUPS = [("a_q", 0, 1536), ("a_k", 1536, 1536), ("a_v", 3072, 1536), ("b_cq", 5120, 448), ("b_ckv", 5568, 128),
           ("b_kr", 5696, 64), ("c_qk", 6272, 512), ("c_v", 6784, 512), ("c_i", 7296, 4), ("c_f", 7300, 4), ("d_u", 8328, 512)]
PCOLS = []
POFF = {}
for (_n, _o, _w) in PGROUPS:
    POFF[_n] = len(PCOLS)
    PCOLS.extend(range(_o, _o + _w))
PW = len(PCOLS)
_uid = [0]


def uk(p):
    _uid[0] += 1
    return "%s#%d" % (p, _uid[0])


def load_w(k, w_ap, c0, ncols, nk=16, rows=None):
    wt = k.pool("wt", [128, 16, 512], BF16, 2)
    if rows is None:
        src = w_ap[:, c0:c0 + ncols].rearrange("(kt p) c -> p kt c", p=128)
        k.dma(wt[:, 0:nk, 0:ncols], src, q='pool')
    return wt


def hk(half):
    return tuple("hT%d" % tt for tt in range(half * 4, half * 4 + 4))


def norm_hT(k, x_ap, norm_g, ident, keep_x=None, ext=None):
    NT = T // 128
    eb = epsb(k)
    if ext is None:
        grep = k.sb("grep", [128, 2048], F32)[:]
    else:
        grep = ext[2]
    k.dma(grep, norm_g.partition_broadcast(128))
    hT = k.sb("hT", [128, 16, T], BF16)
    for tt in range(NT):
        if keep_x is not None:
            xt = V(keep_x[:, tt, :], "xs%d" % tt)
        elif ext is not None:
            xt = ext[0]
        else:
            xt = k.pool("xt", [128, 2048], F32, 1)[:]
        k.dma(xt, x_ap[tt * 128:(tt + 1) * 128, :])
        ss = k.pool("ss", [128, 1], F32, 2)
        xn = k.pool("xn", [128, 2048], BF16, 1)[:] if ext is None else ext[1]
        k.act(xn, xt, AF.Square, accum_out=ss[:])
        rstd = k.pool("rstd", [128, 1], F32, 2)
        ms = k.pool("ms", [128, 1], F32, 2)
        k.ts(ms[:], ss[:], 1.0 / 2048, ALU.mult, EPS, ALU.add)
        k.act(rstd[:], ms[:], AF.Ln)
        k.act(rstd[:], rstd[:], AF.Exp, scale=-0.5)
        if getattr(k, "dbg", None) is not None:
            k.dma(V(k.dbg[:, tt:tt + 1], uk("o")), ss[:], allow_slow_non_contiguous=True)
            k.dma(V(k.dbg[:, 8 + tt:9 + tt], uk("o")), rstd[:], allow_slow_non_contiguous=True)
        k.stt(xn, xt, rstd[:, 0:1], grep, ALU.mult, ALU.mult)
        xnk = _key(xn)
        for f4 in range(4):
            ps = k.psum()
            pb = ps[:].bitcast(BF16)
            for j in range(4):
                ft = f4 * 4 + j
                k.tr(V(pb[:, j * 128:(j + 1) * 128], ps.name), V(_ap(xn)[:, ft * 128:(ft + 1) * 128], xnk), ident[:])
            src = V(pb[:, 0:512].rearrange("p (j t) -> p j t", j=4), ps.name)
            dst = V(hT[:, f4 * 4:(f4 + 1) * 4, tt * 128:(tt + 1) * 128], "hT%d" % tt)
            k.copy(dst, src, eng=('act' if f4 % 2 == 0 else 'dve'))
    return hT


def epsb(k):
    if not hasattr(k, "_epsb"):
        k._epsb = k.sb("epsb", [128, 1], F32)
        k.memset(k._epsb[:], EPS)
    return k._epsb


def consts(k):
    idf = k.din("c_ident", [128, 128], F32)
    ident = k.sb("ident", [128, 128], BF16)
    k.dma(ident[:], idf, q='pool')
    ones = k.sb("ones", [128, 128], BF16)
    k.memset(ones[:], 1.0)
    return ident, ones


def rstd_from_ss(k, ps_ss, n, rows=128):
    r = k.pool("rs", [128, 512], F32, 2)
    k.act(r[0:rows, :], ps_ss[0:rows, :], AF.Ln, bias=epsb(k)[0:rows, 0:1], scale=1.0 / n)
    k.act(r[0:rows, :], r[0:rows, :], AF.Exp, scale=-0.5)
    return r


def proj_block(k, wt, wc0, nb, hT, half, nk=16):
    ps = k.psum()
    for kt in range(nk):
        k.mm(ps[0:nb, :], wt[:, kt, wc0:wc0 + nb], V(hT[:, kt, half * 512:(half + 1) * 512], hk(half)),
             start=(kt == 0), stop=(kt == nk - 1))
    return ps


def colvec(k, name, dram_row, n, col0=0):
    t = k.sb(name, [128, 1], F32)
    k.dma(t[0:n, :], dram_row[:, col0:col0 + n].rearrange("o f -> f o"))
    return t


def build_P(upto=99.0):
    k = K()
    x = k.din("x", [T, 2048])
    w_in = k.din("w_p", [2048, PW])
    norm_g = k.din("norm_g", [1, 2048])
    a_qn_g = k.din("a_qn_g", [1, 128])
    a_kn_g = k.din("a_kn_g", [1, 128])
    b_cq_g = k.din("b_cq_g", [1, 448])
    b_ckv_g = k.din("b_ckv_g", [1, 128])
    b_w_uq = k.din("b_w_uq", [448, 768])
    b_w_ukv = k.din("b_w_ukv", [128, 1024])
    b_qn_g = k.din("b_qn_g", [1, 192])
    b_kn_g = k.din("b_kn_g", [1, 192])
    pos = k.din("pos", [1, T], I32)
    invf = k.din("c_invf", [64, 1])
    sgn = k.din("c_sgn", [64, 1])

    qa = k.dout("qa", [12, 128, T], BF16)
    ka = k.dout("ka", [12, 128, T], BF16)
    va = k.dout("va", [T, 1536], BF16)
    qb = k.dout("qb", [4, 192, T], BF16)
    kb = k.dout("kb", [4, 192, T], BF16)
    vb = k.dout("vb", [T, 512], BF16)
    cqk = k.dout("cqk", [512, T], F32)
    cv = k.dout("cv", [T, 512], BF16)
    cif = k.dout("cif", [8, T], F32)
    du = k.dout("du", [512, T], BF16)

    ident, ones = consts(k)
    if upto <= 1:
        k.dbg = k.dout("dbg", [128, 16])
        k.dbg2 = k.dout("dbg2", [128, 2048])
    hT = norm_hT(k, x, norm_g, ident)
    if upto <= 1:
        hf = k.sb("hf", [128, 2048], F32)
        k.copy(hf[:].rearrange("p (a b) -> p a b", a=16), V(hT[:, :, 0:128], "hT0"))
        k.dma(k.dbg2, hf[:])

    if upto <= 1:
        return k.finish()
    posi = k.sb("posi", [64, T], I32)
    k.dma(posi[:], pos.partition_broadcast(64))
    posf = k.sb("posf", [64, T], F32)
    k.copy(posf[:], posi[:])
    invf_sb = k.sb("invf_sb", [64, 1], F32)
    k.dma(invf_sb[:], invf)
    sgn_sb = k.sb("sgn_sb", [64, 1], F32)
    k.dma(sgn_sb[:], sgn)
    ang = k.sb("ang", [64, T], F32)
    k.ts(ang[:], posf[:], invf_sb[:, 0:1], ALU.mult)
    cosT = k.sb("cosT", [64, T], F32)
    sinS = k.sb("sinS", [64, T], F32)
    trig(k, cosT, ang, 64, T, math.pi / 2, posf, posi)
    trig(k, sinS, ang, 64, T, 0.0, posf, posi)
    k.ts(sinS[:], sinS[:], sgn_sb[:, 0:1], ALU.mult)

    if upto <= 2:
        return k.finish()
    gq = colvec(k, "gq", a_qn_g, 128)
    gk = colvec(k, "gk", a_kn_g, 128)
    for (nm, outd, g) in (("a_q", qa, gq), ("a_k", ka, gk)):
        for cg in range(3):
            wt = load_w(k, w_in, POFF[nm] + cg * 512, 512)
            for b in range(4):
                hd = cg * 4 + b
                for half in range(NH):
                    ps = proj_block(k, wt, b * 128, 128, hT, half)
                    sq = k.pool("sq", [128, 512], BF16, 3)
                    k.act(sq[:], ps[:], AF.Square)
                    ps2 = k.psum()
                    k.mm(ps2[:], ones[:], sq[:])
                    rs = rstd_from_ss(k, ps2, 128)
                    ob = k.pool("ob", [128, 512], BF16, 4)
                    k.stt(ob[:], ps[:], g[:, 0:1], rs[:], ALU.mult, ALU.mult)
                    k.dma(V(outd[hd, :, half * 512:(half + 1) * 512], uk("o")), ob[:])

    if upto <= 3:
        return k.finish()
    for (nm, outd, ncg) in (("a_v", va, 3), ("c_v", cv, 1)):
        for cg in range(ncg):
            wt = load_w(k, w_in, POFF[nm] + cg * 512, 512)
            for tt in range(T // 128):
                ps = k.psum()
                for kt in range(16):
                    k.mm(ps[:], V(hT[:, kt, tt * 128:(tt + 1) * 128], "hT%d" % tt), wt[:, kt, 0:512], start=(kt == 0), stop=(kt == 15))
                ob = k.pool("ob", [128, 512], BF16, 4)
                k.copy(ob[:], ps[:], eng=('act' if tt % 2 else 'dve'))
                k.dma(V(outd[tt * 128:(tt + 1) * 128, cg * 512:(cg + 1) * 512], uk("o")), ob[:])

    if upto <= 4:
        return k.finish()
    for (nm, outd, dt_) in (("c_qk", cqk, F32), ("d_u", du, BF16)):
        wt = load_w(k, w_in, POFF[nm], 512)
        for b in range(4):
            for half in range(NH):
                ps = proj_block(k, wt, b * 128, 128, hT, half)
                if dt_ == F32:
                    ob = k.pool("obf", [128, 512], F32, 2)
                else:
                    ob = k.pool("ob", [128, 512], BF16, 4)
                k.copy(ob[:], ps[:], eng=('act' if half else 'dve'))
                k.dma(V(outd[b * 128:(b + 1) * 128, half * 512:(half + 1) * 512], uk("o")), ob[:])
    wt = load_w(k, w_in, POFF["c_i"], 8)
    for half in range(NH):
        ps = proj_block(k, wt, 0, 8, hT, half)
        ob = k.pool("obf", [128, 512], F32, 2)
        k.copy(ob[0:8, :], ps[0:8, :])
        k.dma(V(cif[:, half * 512:(half + 1) * 512], uk("o")), ob[0:8, :])

    if upto <= 5:
        return k.finish()
    wtq = load_w(k, w_in, POFF["b_cq"], 448)
    wtk = load_w(k, w_in, POFF["b_ckv"], 192)
    wsw = k.sb("wsw", [128, 16, 64], BF16)
    k.dma(V(wsw[:, :, 0:32], "wsw_a"), w_in[:, POFF["b_kr"] + 32:POFF["b_kr"] + 64].rearrange("(kt p) c -> p kt c", p=128), q='pool')
    k.dma(V(wsw[:, :, 32:64], "wsw_b"), w_in[:, POFF["b_kr"]:POFF["b_kr"] + 32].rearrange("(kt p) c -> p kt c", p=128), q='pool')
    WSW = [V(wsw[:, kt, :], "wsw_a") for kt in range(16)]
    wuq = k.sb("wuq", [128, 4, 768], BF16)
    for blk in range(4):
        r = 128 if blk < 3 else 64
        k.dma(V(wuq[0:r, blk, :], "wuq%d" % blk), b_w_uq[blk * 128:blk * 128 + r, :], q='pool')
    wuqs = k.sb("wuqs", [128, 4, 4, 64], BF16)
    b_w_uq_h = b_w_uq.rearrange("k (h d) -> k h d", h=4)
    for blk in range(4):
        r = 128 if blk < 3 else 64
        k.dma(V(wuqs[0:r, blk, :, 0:32], "wuqs%da" % blk), b_w_uq_h[blk * 128:blk * 128 + r, :, 160:192], q='pool')
        k.dma(V(wuqs[0:r, blk, :, 32:64], "wuqs%db" % blk), b_w_uq_h[blk * 128:blk * 128 + r, :, 128:160], q='pool')
    wukv = k.sb("wukv", [128, 1024], BF16)
    k.dma(wukv[:], b_w_ukv, q='pool')
    gcq = k.sb("gcq", [128, 4], F32)
    for blk in range(4):
        r = 128 if blk < 3 else 64
        k.dma(V(gcq[0:r, blk:blk + 1], "gcq%d" % blk), b_cq_g[:, blk * 128:blk * 128 + r].rearrange("o f -> f o"))
    gckv = colvec(k, "gckv", b_ckv_g, 128)
    gqn = colvec(k, "gqn", b_qn_g, 128)
    gkn = colvec(k, "gkn", b_kn_g, 128)
    gqr = colvec(k, "gqr", b_qn_g, 64, 128)
    gkr = colvec(k, "gkr", b_kn_g, 64, 128)
    gqrs = k.sb("gqrs", [128, 1], F32)
    k.dma(V(gqrs[0:32, :], "gqrs_a"), b_qn_g[:, 160:192].rearrange("o f -> f o"))
    k.dma(V(gqrs[32:64, :], "gqrs_b"), b_qn_g[:, 128:160].rearrange("o f -> f o"))
    gkrs = k.sb("gkrs", [128, 1], F32)
    k.dma(V(gkrs[0:32, :], "gkrs_a"), b_kn_g[:, 160:192].rearrange("o f -> f o"))
    k.dma(V(gkrs[32:64, :], "gkrs_b"), b_kn_g[:, 128:160].rearrange("o f -> f o"))
    GQRS = V(gqrs[0:64, 0:1], ("gqrs_a", "gqrs_b"))
    GKRS = V(gkrs[0:64, 0:1], ("gkrs_a", "gkrs_b"))

    if upto <= 6:
        return k.finish()
    for half in range(NH):
        hs = slice(half * 512, (half + 1) * 512)
        cqraw = k.pool("cqraw", [128, 4, 512], F32, 1)
        sqa = k.pool("sqa", [128, 512], F32, 1)
        if not hasattr(k, "_sq3"):
            k._sq3 = k.sb("sq3", [128, 512], F32)
            k.memset(k._sq3[:], 0.0)
        for blk in range(4):
            r = 128 if blk < 3 else 64
            ps = proj_block(k, wtq, blk * 128, r, hT, half)
            k.copy(V(cqraw[0:r, blk, :], "cqraw%d" % blk), ps[0:r, :], eng='dve')
            craw = V(cqraw[0:r, blk, :], "cqraw%d" % blk)
            if blk == 0:
                k.act(sqa[:], craw, AF.Square)
            elif blk < 3:
                sqt = k.pool("sqt", [128, 512], F32, 2)
                k.act(sqt[:], craw, AF.Square)
                k.tt(sqa[:], sqa[:], sqt[:], ALU.add)
            else:
                k.act(k._sq3[0:64, :], craw, AF.Square)
                sq = k.pool("sq", [128, 512], BF16, 3)
                k.tt(sq[:], sqa[:], k._sq3[:], ALU.add)
        ps_ss = k.psum()
        k.mm(ps_ss[:], ones[:], sq[:])
        if upto <= 6.6:
            continue
        rs = rstd_from_ss(k, ps_ss, 448)
        cqn = k.pool("cqn", [128, 4, 512], BF16, 1)
        for blk in range(4):
            r = 128 if blk < 3 else 64
            k.stt(V(cqn[0:r, blk, :], "cqn%d" % blk), V(cqraw[0:r, blk, :], "cqraw%d" % blk),
                  V(gcq[0:r, blk:blk + 1], "gcq%d" % blk), rs[0:r, :], ALU.mult, ALU.mult)
        if upto <= 7:
            continue
        for hh in range(4):
            psn = k.psum()
            psr = k.psum()
            pss = k.psum()
            for blk in range(4):
                r = 128 if blk < 3 else 64
                rhs = V(cqn[0:r, blk, :], "cqn%d" % blk)
                k.mm(psn[:], V(wuq[0:r, blk, hh * 192:hh * 192 + 128], "wuq%d" % blk), rhs, start=(blk == 0), stop=(blk == 3))
            for blk in range(4):
                r = 128 if blk < 3 else 64
                rhs = V(cqn[0:r, blk, :], "cqn%d" % blk)
                k.mm(psr[0:64, :], V(wuq[0:r, blk, hh * 192 + 128:hh * 192 + 192], "wuq%d" % blk), rhs, start=(blk == 0), stop=(blk == 3))
            for blk in range(4):
                r = 128 if blk < 3 else 64
                rhs = V(cqn[0:r, blk, :], "cqn%d" % blk)
                k.S.op('pe', (lambda e, o=pss[0:64, :], l=wuqs[0:r, blk, hh, :], rr=_ap(rhs), s=(blk == 0), t=(blk == 3):
                              e.matmul(o, lhsT=l, rhs=rr, start=s, stop=t)),
                       reads=["wuqs%da" % blk, "wuqs%db" % blk, rhs.key] + ([] if blk == 0 else [pss.name]), writes=[pss.name])
            sq = k.pool("sq", [128, 512], BF16, 3)
            k.act(sq[:], psn[:], AF.Square)
            sq2 = k.pool("sq", [128, 512], BF16, 3)
            k.act(sq2[0:64, :], psr[0:64, :], AF.Square)
            ps2 = k.psum()
            k.mm(ps2[:], ones[:], sq[:], start=True, stop=False)
            k.mm(ps2[:], ones[0:64, :], sq2[0:64, :], start=False, stop=True)
            rq = rstd_from_ss(k, ps2, 192)
            ob = k.pool("ob", [128, 512], BF16, 4)
            k.stt(ob[:], psn[:], gqn[:, 0:1], rq[:], ALU.mult, ALU.mult)
            k.dma(V(qb[hh, 0:128, hs], uk("o")), ob[:])
            t1 = k.pool("t1", [64, 512], F32, 1)
            k.stt(t1[:], psr[0:64, :], gqr[0:64, 0:1], cosT[:, hs], ALU.mult, ALU.mult)
            t2 = k.pool("t2", [64, 512], F32, 1)
            k.stt(t2[:], pss[0:64, :], GQRS, sinS[:, hs], ALU.mult, ALU.mult)
            k.tt(t1[:], t1[:], t2[:], ALU.add)
            ob2 = k.pool("ob", [128, 512], BF16, 4)
            k.tt(ob2[0:64, :], t1[:], rq[0:64, :], ALU.mult)
            k.dma(V(qb[hh, 128:192, hs], uk("o")), ob2[0:64, :])
        if upto <= 8:
            continue
        ps = proj_block(k, wtk, 0, 128, hT, half)
        ckraw = k.pool("ckraw", [128, 512], F32, 1)
        k.copy(ckraw[:], ps[:], eng='dve')
        sq = k.pool("sq", [128, 512], BF16, 3)
        k.act(sq[:], ps[:], AF.Square)
        ps2 = k.psum()
        k.mm(ps2[:], ones[:], sq[:])
        rs = rstd_from_ss(k, ps2, 128)
        ckvn = k.pool("ckvn", [128, 512], BF16, 1)
        k.stt(ckvn[:], ckraw[:], gckv[:, 0:1], rs[:], ALU.mult, ALU.mult)
        if upto <= 9:
            continue
        wv = wukv[:].rearrange("p (h d) -> p h d", h=4)[:, :, 128:256]
        for t4 in range(4):
            psv = k.psum()
            k.mm(psv[:].rearrange("p (h d) -> p h d", h=4), ckvn[:, t4 * 128:(t4 + 1) * 128], wv)
            ob = k.pool("ob", [128, 512], BF16, 4)
            k.copy(ob[:], psv[:], eng='act')
            tok0 = half * 512 + t4 * 128
            k.dma(V(vb[tok0:tok0 + 128, :], uk("o")), ob[:])
        if upto <= 10:
            continue
        pkr = proj_block(k, wtk, 128, 64, hT, half)
        pks = k.psum()
        for kt in range(16):
            k.S.op('pe', (lambda e, o=pks[0:64, :], l=wsw[:, kt, :], rr=hT[:, kt, hs], s=(kt == 0), t=(kt == 15):
                          e.matmul(o, lhsT=l, rhs=rr, start=s, stop=t)),
                   reads=["wsw_a", "wsw_b"] + list(hk(half)) + ([] if kt == 0 else [pks.name]), writes=[pks.name])
        sqr = k.pool("sqr", [64, 512], BF16, 2)
        k.act(sqr[:], pkr[0:64, :], AF.Square)
        kro = k.pool("kro", [64, 512], F32, 1)
        k.stt(kro[:], pkr[0:64, :], gkr[0:64, 0:1], cosT[:, hs], ALU.mult, ALU.mult)
        t2 = k.pool("t2", [64, 512], F32, 1)
        k.stt(t2[:], pks[0:64, :], GKRS, sinS[:, hs], ALU.mult, ALU.mult)
        k.tt(kro[:], kro[:], t2[:], ALU.add)
        if upto <= 11:
            continue
        for hh in range(4):
            pkn = k.psum()
            k.mm(pkn[:], wukv[:, hh * 256:hh * 256 + 128], ckvn[:])
            sq = k.pool("sq", [128, 512], BF16, 3)
            k.act(sq[:], pkn[:], AF.Square)
            ps2 = k.psum()
            k.mm(ps2[:], ones[:], sq[:], start=True, stop=False)
            k.mm(ps2[:], ones[0:64, :], sqr[:], start=False, stop=True)
            rk = rstd_from_ss(k, ps2, 192)
            ob = k.pool("ob", [128, 512], BF16, 4)
            k.stt(ob[:], pkn[:], gkn[:, 0:1], rk[:], ALU.mult, ALU.mult)
            k.dma(V(kb[hh, 0:128, hs], uk("o")), ob[:])
            ob2 = k.pool("ob", [128, 512], BF16, 4)
            k.tt(ob2[0:64, :], kro[:], rk[0:64, :], ALU.mult)
            k.dma(V(kb[hh, 128:192, hs], uk("o")), ob2[0:64, :])
    return k.finish()


def trig(k, out, ang, P, N, shift, a, ki):
    TWO_PI = 2.0 * math.pi
    HI = 6.28125
    LO = TWO_PI - HI
    k.ts(a[:], ang[:], shift, ALU.add)
    kf = k.pool("tg_k", [P, N], F32, 1)
    k.ts(kf[:], a[:], 1.0 / TWO_PI, ALU.mult)
    k.copy(ki[:], kf[:])
    k.copy(kf[:], ki[:])
    k.stt(a[:], kf[:], -HI, a[:], ALU.mult, ALU.add)
    k.stt(a[:], kf[:], -LO, a[:], ALU.mult, ALU.add)
    m = k.pool("tg_m", [P, N], F32, 1)
    k.ts(m[:], a[:], math.pi, ALU.is_gt, -TWO_PI, ALU.mult)
    k.tt(a[:], a[:], m[:], ALU.add)
    k.ts(m[:], a[:], -math.pi, ALU.is_lt, TWO_PI, ALU.mult)
    k.tt(a[:], a[:], m[:], ALU.add)
    k.ts(a[:], a[:], math.pi, ALU.min, -math.pi, ALU.max)
    k.act(out[:], a[:], AF.Sin)


import math

S = 8192
A_DIL = (1, 4, 16)
A_SCALE = 128 ** -0.5
B_SCALE = 192 ** -0.5
NEG = -30000.0


def sst(start, count, step):
    return slice(start, start + step * (count - 1) + 1, step)


def host_biasA(core):
    slopes = 2.0 ** (-8.0 * np.arange(1, 13, dtype=np.float64) / 12)
    kk = np.arange(128)[:, None]
    qq = np.arange(128)[None, :]
    out = np.zeros((12, 3, 128, 128), np.float32)
    for gh in range(12):
        d = A_DIL[gh // 4]
        sl = slopes[gh]
        delta0 = qq - kk + 128
        b0 = np.where(kk >= qq, -sl * d * delta0, NEG * A_SCALE)
        delta1 = qq - kk
        b1 = np.where(kk <= qq, -sl * d * delta1, NEG * A_SCALE)
        out[gh, 0] = b0 / A_SCALE
        out[gh, 1] = (b0 / A_SCALE) if core > 0 else NEG
        out[gh, 2] = b1 / A_SCALE
    return np.maximum(out, NEG).astype(np.float32)


def host_maskB(par):
    out = np.zeros((8, 128, 512), np.float32)
    kk = np.arange(128)[:, None]
    qq = np.arange(512)[None, :]
    for i in range(8):
        o = i - 4 * par
        if o < 0:
            out[i] = 0.0
        elif o > 3:
            out[i] = NEG
        else:
            out[i] = np.where(128 * o + kk <= qq, 0.0, NEG)
    return out


def mixer_A(k, ident, ones):
    qa = k.din("qa", [12, 128, T], BF16)
    kax = k.din("ka_ext", [12, 128, 2048 + T], BF16)
    vax = k.din("va_ext", [2048 + T, 1536], BF16)
    biasd = k.din("c_biasA", [12, 3, 128, 128], F32)
    ya = k.dout("ya", [512, T], F32)

    bias = k.sb("biasA", [128, 36, 128], BF16)
    for gh in range(12):
        k.dma(V(bias[:, gh * 3:(gh + 1) * 3, :], "biasA%d" % gh), biasd[gh].rearrange("t k q -> k t q"), q='pool')
    accN = [k.sb("accN%d" % h, [128, T], F32) for h in range(4)]
    accD = [k.sb("accD%d" % h, [128, T], F32) for h in range(4)]
    pend = []

    def part2(ctx):
        vt, vkeys, t0, t1, P, Bq, g, h, c0, d = ctx
        psO = k.psum(4, 8)
        k.mm(psO[:, 0:Bq], V(vt[:, t0, :], vkeys), P[:, 0:Bq], start=True, stop=False)
        k.mm(psO[:, 0:Bq], V(vt[0:Bq, t1, :], vkeys), P[0:Bq, 128:128 + Bq], start=False, stop=True)
        k.mm(psO[:, 128:128 + Bq], ones[:], P[:, 0:Bq], start=True, stop=False)
        k.mm(psO[:, 128:128 + Bq], ones[0:Bq, :], P[0:Bq, 128:128 + Bq], start=False, stop=True)
        cs = sst(c0, Bq, d)
        if g == 0:
            k.copy(accN[h][:, cs], psO[:, 0:Bq], eng='dve')
            k.copy(accD[h][:, cs], psO[:, 128:128 + Bq], eng='dve')
        else:
            k.tt(accN[h][:, cs], accN[h][:, cs], psO[:, 0:Bq], ALU.add)
            k.tt(accD[h][:, cs], accD[h][:, cs], psO[:, 128:128 + Bq], ALU.add)
    for gh in range(12):
        g, h = gh // 4, gh % 4
        d = A_DIL[g]
        Bq = 128 if d < 16 else 64
        nq = T // d
        nblk = nq // Bq
        qs = k.pool("qA", [128, T], BF16, 2)
        k.dma(qs[:], qa[gh])
        kx = k.pool("kA", [128, 2048 + T], BF16, 2)
        k.dma(kx[:], kax[gh])
        for r in range(d):
            nkeys = 128 + nq
            nt_full = nkeys // 128
            rem = nkeys - nt_full * 128
            vt = k.pool("vA", [128, 9, 128], BF16, 4)
            e0 = 2048 + r - 128 * d
            src = vax[sst(e0, 128 * nt_full, d), gh * 128:(gh + 1) * 128].rearrange("(t k) c -> k t c", k=128)
            vkeys = ["%s_a" % vt.name]
            k.dma(V(vt[:, 0:nt_full, :], vkeys[0]), src)
            if rem:
                e1 = e0 + d * 128 * nt_full
                vkeys.append("%s_b" % vt.name)
                k.dma(V(vt[0:rem, nt_full, :], vkeys[1]), vax[sst(e1, rem, d), gh * 128:(gh + 1) * 128])
            for blk in range(nblk):
                i0 = blk * Bq
                q_ap = qs[:, sst(r + d * i0, Bq, d)]
                ek0 = 2048 + r + d * (i0 - 128)
                k0 = kx[:, sst(ek0, 128, d)]
                ek1 = 2048 + r + d * i0
                k1 = kx[:, sst(ek1, Bq, d)]
                btype = 1 if i0 == 0 else 0
                bkey = "biasA%d" % gh
                psS = k.psum(0, 4)
                k.mm(psS[:, 0:Bq], k0, q_ap, start=True, stop=False)
                k.mm(psS[:, 0:Bq], ident[:], V(bias[:, gh * 3 + btype, 0:Bq], bkey), start=False, stop=True)
                k.mm(psS[0:Bq, 128:128 + Bq], k1, q_ap, start=True, stop=False)
                k.mm(psS[0:Bq, 128:128 + Bq], ident[0:Bq, 0:Bq], V(bias[0:Bq, gh * 3 + 2, 0:Bq], bkey), start=False, stop=True)
                P = k.pool("PA", [128, 256], BF16, 4)
                if Bq == 128:
                    k.act(P[:], psS[:, 0:256], AF.Exp, scale=A_SCALE)
                else:
                    k.act(P[:, 0:Bq], psS[:, 0:Bq], AF.Exp, scale=A_SCALE)
                    k.act(P[0:Bq, 128:128 + Bq], psS[0:Bq, 128:128 + Bq], AF.Exp, scale=A_SCALE)
                t0, t1 = blk, blk + 1
                pend.append((vt, vkeys, t0, t1, P, Bq, g, h, r + d * i0, d))
                if len(pend) > 1:
                    part2(pend.pop(0))
    while pend:
        part2(pend.pop(0))
    for h in range(4):
        k.recip(accD[h][:], accD[h][:])
        k.tt(accN[h][:], accN[h][:], accD[h][:], ALU.mult)
        k.dma(V(ya[h * 128:(h + 1) * 128, :], uk("o")), accN[h][:])


def mixer_B(k, ident, ones):
    qb = k.din("qb_h", [192, 4096], BF16)
    kb = k.din("kb_h", [192, S], BF16)
    vb = k.din("vb_h", [S, 128], BF16)
    maskd = k.din("c_maskB", [8, 128, 512], F32)
    yb = k.dout("yb", [128, 4096], F32)

    mask = k.sb("maskB", [128, 8, 512], BF16)
    k.dma(mask[:], maskd.rearrange("i k q -> k i q"), q='pool')
    kn = k.sb("kbn", [128, S], BF16)
    kr = k.sb("kbr", [64, S], BF16)
    for c4 in range(4):
        cs = slice(c4 * 2048, (c4 + 1) * 2048)
        k.dma(V(kn[:, cs], "kbn%d" % c4), kb[0:128, cs])
        k.dma(V(kr[:, cs], "kbr%d" % c4), kb[128:192, cs])
    vt = k.sb("vbt", [128, 64, 128], BF16)
    for c4 in range(4):
        k.dma(V(vt[:, c4 * 16:(c4 + 1) * 16, :], "vbt%d" % c4),
              vb[c4 * 2048:(c4 + 1) * 2048, :].rearrange("(t k) c -> k t c", k=128))
    for j in range(8):
        qn = k.pool("qbn", [128, 512], BF16, 2)
        qr = k.pool("qbr", [64, 512], BF16, 2)
        k.dma(qn[:], qb[0:128, j * 512:(j + 1) * 512])
        k.dma(qr[:], qb[128:192, j * 512:(j + 1) * 512])
        nkb = 8 * j + 8
        psO = k.bank(j % 2)
        psD = k.bank(2 + j % 2)

        def pv(b, P):
            c4_ = b // 16
            k.mm(psO[:], V(vt[:, b, :], "vbt%d" % c4_), P[:], start=(b == 0), stop=(b == nkb - 1))
            k.mm(psD[:], ones[:], P[:], start=(b == 0), stop=(b == nkb - 1))
        prev = None
        for b in range(nkb):
            c4 = b // 16
            psS = k.psum(4, 8)
            last8 = b >= nkb - 8
            k.mm(psS[:], V(kn[:, b * 128:(b + 1) * 128], "kbn%d" % c4), qn[:], start=True, stop=False)
            k.mm(psS[:], V(kr[:, b * 128:(b + 1) * 128], "kbr%d" % c4), qr[:], start=False, stop=(not last8))
            if last8:
                k.mm(psS[:], ident[:], mask[:, b - (nkb - 8), :], start=False, stop=True)
            P = k.pool("PB", [128, 512], BF16, 4)
            k.act(P[:], psS[:], AF.Exp, scale=B_SCALE)
            if prev is not None:
                pv(*prev)
            prev = (b, P)
        pv(*prev)
        rd = k.pool("rdB", [128, 512], F32, 2)
        k.recip(rd[:], psD[:])
        ob = k.pool("obB", [128, 512], F32, 2)
        k.tt(ob[:], psO[:], rd[:], ALU.mult)
        k.dma(V(yb[:, j * 512:(j + 1) * 512], uk("o")), ob[:])


def build_M(parts="ABCD"):
    k = K()
    idf = k.din("c_ident", [128, 128], F32)
    ident = k.sb("ident", [128, 128], BF16)
    k.dma(ident[:], idf, q='pool')
    ones = k.sb("ones", [128, 128], BF16)
    k.memset(ones[:], 1.0)
    if "A" in parts:
        mixer_A(k, ident, ones)
    if "B" in parts:
        mixer_B(k, ident, ones)
    if "C" in parts:
        mixer_C(k, ident, ones)
    if "D" in parts:
        mixer_D(k, ident, ones)
    return k.finish()


S = 8192
NCH = 128
L = 64


def host_constsC():
    tri = (np.arange(128)[:, None] < np.arange(128)[None, :]).astype(np.float32)
    mask = (np.arange(64)[:, None] <= np.arange(64)[None, :]).astype(np.float32)
    return {"c_tri": tri, "c_maskC": mask, "c_identf": np.eye(128, dtype=np.float32)}


def mixer_C(k, ident, ones):
    cq = k.din("cq_h", [64, 3 + S], F32)
    ck = k.din("ck_h", [64, 3 + S], F32)
    cwq = k.din("cw_q", [64, 4], F32)
    cbq = k.din("cb_q", [64, 1], F32)
    cwk = k.din("cw_k", [64, 4], F32)
    cbk = k.din("cb_k", [64, 1], F32)
    cv = k.din("cv_h", [S, 64], BF16)
    gi = k.din("gi", [128, 64], F32)
    gf = k.din("gf", [128, 64], F32)
    gb = k.din("gb", [1, 2], F32)
    trid = k.din("c_tri", [128, 128], F32)
    maskd = k.din("c_maskC", [64, 64], F32)
    identd = k.din("c_identf", [128, 128], F32)
    hc = k.dout("hc", [S, 64], F32)

    identf = k.sb("identf", [128, 128], F32)
    k.dma(identf[:], identd)
    tri = k.sb("tri", [128, 128], F32)
    k.dma(tri[:], trid)
    maskC = k.sb("maskC", [64, 64], F32)
    k.dma(maskC[:], maskd)
    onesf = k.sb("onesf", [128, 64], F32)
    k.memset(onesf[:], 1.0)

    qT = k.sb("qT", [64, S], BF16)
    kT = k.sb("kT", [64, S], BF16)
    CH = 2048
    for (src, wd, bd, dst, scl, nm) in ((cq, cwq, cbq, qT, 1.0, "q"), (ck, cwk, cbk, kT, 0.125, "k")):
        w = k.sb("cw" + nm, [64, 4], F32)
        k.dma(w[:], wd)
        b = k.sb("cb" + nm, [64, 1], F32)
        k.dma(b[:], bd)
        for ci in range(S // CH):
            xt = k.pool("cx", [64, CH + 3], F32, 2)
            k.dma(xt[:], src[:, ci * CH:ci * CH + CH + 3])
            acc = k.pool("cacc", [64, CH], F32, 2)
            k.ts(acc[:], xt[:, 0:CH], w[:, 0:1], ALU.mult)
            for j in range(1, 4):
                k.stt(acc[:], xt[:, j:j + CH], w[:, j:j + 1], acc[:], ALU.mult, ALU.add)
            dkey = V(dst[:, ci * CH:(ci + 1) * CH], "%sT%d" % (nm, ci))
            if scl == 1.0:
                k.act(dkey, acc[:], AF.Silu, bias=b[:, 0:1])
            else:
                k.act(acc[:], acc[:], AF.Silu, bias=b[:, 0:1])
                k.ts(dkey, acc[:], scl, ALU.mult, eng='pool')

    def tk(nm, t0):
        return "%sT%d" % (nm, t0 // CH)

    vx = k.sb("vx", [64, NCH, 65], BF16)
    for c4 in range(4):
        k.dma(V(vx[:, c4 * 32:(c4 + 1) * 32, 0:64], "vx%d" % c4),
              cv[c4 * 2048:(c4 + 1) * 2048, :].rearrange("(c s) d -> s c d", s=64))
    k.memset(V(vx[:, :, 64:65], "vx1"), 1.0, eng='pool')

    def vxk(c):
        return ("vx%d" % (c // 32), "vx1")

    def g(name, cols=64):
        return k.sb(name, [128, cols], F32)
    gbb = g("gbb", 2)
    k.dma(gbb[:], gb.partition_broadcast(128))
    ngb = g("ngb", 2)
    k.ts(ngb[:], gbb[:], -1.0, ALU.mult)
    gi_s = g("gi_s")
    gf_s = g("gf_s")
    k.dma(gi_s[:], gi)
    k.dma(gf_s[:], gf)
    e1 = g("e1")
    k.act(e1[:], gf_s[:], AF.Exp, bias=ngb[:, 1:2], scale=-1.0)
    k.act(e1[:], e1[:], AF.Ln, bias=1.0)
    lf = g("lf")
    k.ts(lf[:], e1[:], -1.0, ALU.mult)
    bloc = g("bloc")
    k.scan(bloc[:], onesf[:], lf[:], 0.0, ALU.mult, ALU.add)
    psb = k.psum()
    k.mm(psb[:, 0:1], tri[:], bloc[:, 63:64])
    bst = g("bst", 1)
    k.copy(bst[:], psb[:, 0:1])
    Bg = g("Bg")
    k.ts(Bg[:], bloc[:], bst[:, 0:1], ALU.add)
    a = g("a")
    k.ts(a[:], gi_s[:], gbb[:, 0:1], ALU.add)
    k.tt(a[:], a[:], Bg[:], ALU.subtract)
    cm = g("cm")
    k.scan(cm[:], a[:], a[:], -1.0e30, ALU.max, ALU.max)
    pst = k.psum()
    k.tr(pst[0:1, 0:128], cm[:, 63:64], identf[:])
    crow = k.sb("crow", [1, 128], F32)
    k.copy(crow[:], pst[0:1, 0:128])
    zrow = k.sb("zrow", [1, 128], F32)
    k.memset(zrow[:], 0.0)
    rinc = k.sb("rinc", [1, 128], F32)
    k.scan(rinc[:], crow[:], zrow[:], 0.0, ALU.max, ALU.max)
    rexc = k.sb("rexc", [1, 128], F32)
    k.memset(V(rexc[:, 0:1], "rexc"), 0.0)
    k.copy(V(rexc[:, 1:128], "rexc"), rinc[:, 0:127])
    Rc = g("Rc", 1)
    Rn = g("Rn", 1)
    for (row, col) in ((rexc, Rc), (rinc, Rn)):
        p2 = k.psum()
        k.tr(p2[:, 0:1], V(row[0:1, :], row.name), identf[0:1, 0:1])
        k.copy(col[:], p2[:, 0:1])
    nRn = g("nRn", 1)
    k.ts(nRn[:], Rn[:], -1.0, ALU.mult)
    M = g("M")
    k.ts(M[:], cm[:], Rc[:, 0:1], ALU.max)
    fI = g("fI")
    k.act(fI[:], M[:], AF.Exp, bias=Rn[:, 0:1], scale=-1.0)
    fE = g("fE")
    k.act(fE[:], M[:], AF.Exp, bias=Rc[:, 0:1], scale=-1.0)
    wk = g("wk")
    k.act(wk[:], a[:], AF.Exp, bias=nRn[:, 0:1], scale=1.0)
    thr = g("thr")
    k.tt(thr[:], Bg[:], M[:], ALU.add)
    k.act(thr[:], thr[:], AF.Exp, scale=-1.0)
    dec = g("dec", 1)
    k.tt(dec[:], Rc[:], Rn[:], ALU.subtract)
    k.act(dec[:], dec[:], AF.Exp)
    tabs = {}
    for nm, src in (("fI", fI), ("fE", fE), ("wk", wk), ("thr", thr)):
        p2 = k.psum()
        k.tr(p2[0:64, 0:128], src[:], identf[:])
        t = k.sb(nm + "_T", [64, 128], F32)
        k.copy(t[:], p2[0:64, 0:128])
        tabs[nm] = t
    p2 = k.psum()
    k.tr(p2[0:1, 0:128], dec[:], identf[:])
    drow = k.sb("drow", [1, 128], F32)
    k.copy(drow[:], p2[0:1, 0:128])
    p3 = k.psum()
    k.mm(p3[0:64, 0:128], onesf[0:1, 0:64], drow[:])
    dec_rep = k.sb("dec_rep", [64, 128], F32)
    k.copy(dec_rep[:], p3[0:64, 0:128])

    from_bc = lambda ap, shape, axis: ap.unsqueeze(axis).broadcast_to(list(shape))
    G = 8
    Ub = k.sb("Ub", [64, NCH, 65], F32)
    Sball = k.sb("Sball", [64, NCH, 65], BF16)
    for c0 in range(0, NCH, G):
        bT = k.psum(0, 2)
        pkb = bT[:].bitcast(BF16)
        for i in range(G):
            c = c0 + i
            t0 = c * L
            k.tr(V(pkb[0:64, i * 64:(i + 1) * 64], bT.name), V(kT[:, t0:t0 + L], tk("k", t0)), ident[0:64, 0:64])
        Kt = k.pool("Kt8", [64, G, 64], BF16, 2)
        k.tt(Kt[:], V(pkb[0:64, 0:G * 64].rearrange("p (g d) -> p g d", g=G), bT.name),
             V(from_bc(tabs["wk"][:, c0:c0 + G], [64, G, 64], 2), "wk_T"), ALU.mult)
        for hgrp in range(2):
            bU = k.psum(2, 6)
            for i4 in range(4):
                i = hgrp * 4 + i4
                c = c0 + i
                k.mm(bU[0:64, i4 * 65:(i4 + 1) * 65], V(Kt[:, i, :], Kt.name), V(vx[:, c, :], vxk(c)))
            cc = c0 + hgrp * 4
            k.copy(V(Ub[:, cc:cc + 4, :], "Ub%d" % (cc // 16)), bU[0:64, 0:260].rearrange("p (g d) -> p g d", g=4), eng='act')
    Sf = [k.sb("Sf0", [64, 65], F32), k.sb("Sf1", [64, 65], F32)]
    k.memset(Sf[0][:], 0.0)
    for c in range(NCH):
        cur, nxt = Sf[c % 2], Sf[(c + 1) % 2]
        k.copy(V(Sball[:, c, :], "Sb%d" % (c // 16)), cur[:], eng='pool')
        if c < NCH - 1:
            k.stt(nxt[:], cur[:], dec_rep[:, c:c + 1], V(Ub[:, c, :], "Ub%d" % (c // 16)), ALU.mult, ALU.add)
    hout = None
    for c0 in range(0, NCH, G):
        cs = list(range(c0, c0 + G))
        bk = {c: k.bank(c % 8) for c in cs}
        for c in cs:
            t0 = c * L
            k.mm(bk[c][0:64, 0:64], V(kT[:, t0:t0 + L], tk("k", t0)), V(qT[:, t0:t0 + L], tk("q", t0)))
        Pt = k.pool("Pt8", [64, G, 64], BF16, 2)
        for i, c in enumerate(cs):
            k.stt(V(Pt[:, i, :], "%s_%d" % (Pt.name, i)), bk[c][0:64, 0:64], tabs["wk"][:, c:c + 1], maskC[:], ALU.mult, ALU.mult)
        for i, c in enumerate(cs):
            t0 = c * L
            k.mm(bk[c][0:64, 128:193], V(Pt[:, i, :], "%s_%d" % (Pt.name, i)), V(vx[:, c, :], vxk(c)))
            k.mm(bk[c][0:64, 256:321], V(qT[:, t0:t0 + L], tk("q", t0)), V(Sball[:, c, :], "Sb%d" % (c // 16)))
        tmp = k.pool("ctmp8", [64, G, 65], F32, 2)
        for i, c in enumerate(cs):
            k.ts(V(tmp[:, i, :], "%s_%d" % (tmp.name, i)), bk[c][0:64, 256:321], tabs["fE"][:, c:c + 1], ALU.mult)
        tot = k.pool("ctot8", [64, G, 65], F32, 2)
        for i, c in enumerate(cs):
            k.stt(V(tot[:, i, :], "%s_%d" % (tot.name, i)), bk[c][0:64, 128:193], tabs["fI"][:, c:c + 1],
                  V(tmp[:, i, :], "%s_%d" % (tmp.name, i)), ALU.mult, ALU.add)
        totk = tuple("%s_%d" % (tot.name, i) for i in range(G))
        den8 = V(tot[:, :, 64:65].rearrange("p g o -> p (g o)"), totk)
        dm = k.pool("cdm8", [64, G], F32, 2)
        k.stt(dm[:], den8, -1.0, den8, ALU.mult, ALU.max)
        k.tt(dm[:], dm[:], tabs["thr"][:, c0:c0 + G], ALU.max)
        k.recip(dm[:], dm[:])
        if c0 % 16 == 0:
            hout = k.pool("hout", [64, 16, 64], F32, 2)
        k.tt(V(hout[:, c0 % 16:c0 % 16 + G, :], hout.name), V(tot[:, :, 0:64], totk),
             V(from_bc(dm[:], [64, G, 64], 2), dm.name), ALU.mult, eng='pool')
        if c0 % 16 == 16 - G:
            cc = c0 + G - 16
            k.dma(V(hc[cc * 64:(cc + 16) * 64, :].rearrange("(c s) d -> s c d", s=64), uk("o")), hout[:])


import math

S = 8192
TC = 64
NC_ = S // TC


def host_constsD():
    seg = np.ones((128, TC), np.float32)
    seg[:, 0] = 0.0
    return {"c_jvec": np.broadcast_to(np.arange(65, dtype=np.float32)[None, :], (128, 65)).copy(),
            "c_seg": seg, "c_identf": np.eye(128, dtype=np.float32)}


def bc(ap, shape, axis):
    return ap.unsqueeze(axis).broadcast_to(list(shape))


def mixer_D(k, ident, ones):
    du = k.din("du_g", [2, 32, S], BF16)
    lam = k.din("d_lam", [128, 2, 3], F32)
    bmat = k.din("d_b", [128, 2, 2, 16], F32)
    cmat = k.din("d_c", [128, 2, 2, 16], F32)
    dsk = k.din("d_sk", [32, 2], F32)
    jvd = k.din("c_jvec", [128, 65], F32)
    segd = k.din("c_seg", [128, TC], F32)
    identd = k.din("c_identf", [128, 128], F32)
    ys = k.dout("ys5", [2, 32, S], F32)

    identf = k.sb("identf", [128, 128], F32)
    k.dma(identf[:], identd)
    jv = k.sb("jv", [128, 65], F32)
    k.dma(jv[:], jvd)
    seg1 = k.sb("seg1", [128, TC], BF16)
    k.dma(seg1[:], segd, q='pool')
    seg = k.sb("seg", [128, NC_, TC], BF16)
    k.copy(seg[:], bc(seg1[:], [128, NC_, TC], 1), eng='pool')
    lam_s = k.sb("lam_s", [128, 2, 3], F32)
    k.dma(lam_s[:], lam)
    b_s = k.sb("b_s", [128, 2, 2, 16], F32)
    k.dma(b_s[:], bmat)
    c_s = k.sb("c_s", [128, 2, 2, 16], F32)
    k.dma(c_s[:], cmat)
    dsk_s = k.sb("dsk_s", [32, 2], F32)
    k.dma(dsk_s[:], dsk)

    arena = k.sb("arena", [128, 2 * S], BF16)
    Zre = V(arena[:, 0:S], "arena")
    Zim = V(arena[:, S:2 * S], "arena")
    ybuf = V(arena[0:32, :].bitcast(F32), "arena")
    Cre = k.sb("Cre", [128, S], BF16)
    Cim = k.sb("Cim", [128, S], BF16)
    Wtab = k.sb("Wtab", [128, TC, 2, 32], F32)
    WT = k.sb("WT", [32, TC, 2, 128], BF16)
    G = k.sb("G", [128, TC + 1, 2, 32], BF16)
    u = k.sb("u", [32, S], BF16)
    dskd = k.sb("dskd", [32, 32], BF16)

    _cols = {}

    def col(name):
        if name not in _cols:
            _cols[name] = k.sb(name, [128, 1], F32)
        return _cols[name]

    for t in range(2):
        k.dma(u[:], du[t])
        lr = lam_s[:, t, 0:1]
        li = lam_s[:, t, 1:2]
        dt = col("dt")
        k.act(dt[:], lam_s[:, t, 2:3], AF.Exp)
        ldt = col("ldt")
        k.tt(ldt[:], lr, dt[:], ALU.mult)
        nldt = col("nldt")
        k.ts(nldt[:], ldt[:], -1.0, ALU.mult)
        wdt = col("wdt")
        k.tt(wdt[:], li, dt[:], ALU.mult)
        magp = k.sb("magp", [128, 65], F32) if t == 0 else magp
        magn = k.sb("magn", [128, 65], F32) if t == 0 else magn
        k.act(magp[:], jv[:], AF.Exp, scale=ldt[:, 0:1])
        k.act(magn[:], jv[:], AF.Exp, scale=nldt[:, 0:1])
        ang = k.sb("angd", [128, 65], F32) if t == 0 else ang
        k.ts(ang[:], jv[:], wdt[:, 0:1], ALU.mult)
        cosj = k.sb("cosj", [128, 65], F32) if t == 0 else cosj
        sinj = k.sb("sinj", [128, 65], F32) if t == 0 else sinj
        sa = k.sb("tga", [128, 65], F32) if t == 0 else sa
        si = k.sb("tgi", [128, 65], I32) if t == 0 else si
        trig(k, cosj, ang, 128, 65, math.pi / 2, sa, si)
        trig(k, sinj, ang, 128, 65, 0.0, sa, si)
        are = k.sb("are", [128, 65], F32) if t == 0 else are
        aim = k.sb("aim", [128, 65], F32) if t == 0 else aim
        nre = k.sb("nre", [128, 65], F32) if t == 0 else nre
        nim = k.sb("nim", [128, 65], F32) if t == 0 else nim
        k.tt(are[:], magp[:], cosj[:], ALU.mult)
        k.tt(aim[:], magp[:], sinj[:], ALU.mult)
        k.tt(nre[:], magn[:], cosj[:], ALU.mult)
        k.stt(nim[:], magn[:], -1.0, sinj[:], ALU.mult, ALU.mult)
        den = col("den")
        k.tt(den[:], lr, lr, ALU.mult)
        k.stt(den[:], li, li, den[:], ALU.mult, ALU.add)
        k.recip(den[:], den[:])
        ar1 = col("ar1")
        k.ts(ar1[:], are[:, 1:2], -1.0, ALU.add)
        fre = col("fre")
        k.tt(fre[:], ar1[:], lr, ALU.mult)
        k.stt(fre[:], aim[:, 1:2], li, fre[:], ALU.mult, ALU.add)
        k.tt(fre[:], fre[:], den[:], ALU.mult)
        fim = col("fim")
        k.tt(fim[:], aim[:, 1:2], lr, ALU.mult)
        tmpc = col("tmpc")
        k.tt(tmpc[:], ar1[:], li, ALU.mult)
        k.tt(fim[:], fim[:], tmpc[:], ALU.subtract)
        k.tt(fim[:], fim[:], den[:], ALU.mult)
        nfim = col("nfim")
        k.ts(nfim[:], fim[:], -1.0, ALU.mult)
        Bb = k.sb("Bb", [128, 2, 16], F32) if t == 0 else Bb
        bre, bim = b_s[:, t, 0, :], b_s[:, t, 1, :]
        k.ts(V(Bb[:, 0, :], "Bb"), bre, fre[:, 0:1], ALU.mult)
        k.stt(V(Bb[:, 0, :], "Bb"), bim, nfim[:, 0:1], V(Bb[:, 0, :], "Bb"), ALU.mult, ALU.add)
        k.ts(V(Bb[:, 1, :], "Bb"), bim, fre[:, 0:1], ALU.mult)
        k.stt(V(Bb[:, 1, :], "Bb"), bre, fim[:, 0:1], V(Bb[:, 1, :], "Bb"), ALU.mult, ALU.add)
        k.memset(Wtab[:], 0.0, eng='pool')
        k.memset(G[:], 0.0, eng='pool')
        tA = k.sb("tA", [128, TC + 1, 16], F32) if t == 0 else tA
        tB = k.sb("tB", [128, TC + 1, 16], F32) if t == 0 else tB
        for gi in range(2):
            ps_ = slice(gi * 64, (gi + 1) * 64)
            cs_ = slice(gi * 16, (gi + 1) * 16)
            shp = [64, TC, 16]
            n_re = bc(nre[ps_, 0:TC], shp, 2)
            n_im = bc(nim[ps_, 0:TC], shp, 2)
            B_re = bc(V(Bb[ps_, 0, :], "Bb").ap, shp, 1)
            B_im = bc(V(Bb[ps_, 1, :], "Bb").ap, shp, 1)
            tAv = V(tA[ps_, 0:TC, :], "tA")
            tBv = V(tB[ps_, 0:TC, :], "tB")
            k.tt(tAv, V(n_re, "nre"), V(B_re, "Bb"), ALU.mult)
            k.tt(tBv, V(n_im, "nim"), V(B_im, "Bb"), ALU.mult)
            k.tt(V(Wtab[ps_, :, 0, cs_], "Wtab"), tAv, tBv, ALU.subtract)
            k.tt(tAv, V(n_re, "nre"), V(B_im, "Bb"), ALU.mult)
            k.tt(tBv, V(n_im, "nim"), V(B_re, "Bb"), ALU.mult)
            k.tt(V(Wtab[ps_, :, 1, cs_], "Wtab"), tAv, tBv, ALU.add)
            shp2 = [64, TC + 1, 16]
            a_re = bc(are[ps_, :], shp2, 2)
            a_im = bc(aim[ps_, :], shp2, 2)
            C_re = bc(c_s[ps_, t, 0, :], shp2, 1)
            C_im = bc(c_s[ps_, t, 1, :], shp2, 1)
            tAv2 = V(tA[ps_, :, :], "tA")
            tBv2 = V(tB[ps_, :, :], "tB")
            k.tt(tAv2, V(a_re, "are"), V(C_re, "c_s"), ALU.mult)
            k.tt(tBv2, V(a_im, "aim"), V(C_im, "c_s"), ALU.mult)
            k.tt(V(G[ps_, :, 0, cs_], "G"), tAv2, tBv2, ALU.subtract)
            k.tt(tAv2, V(a_re, "are"), V(C_im, "c_s"), ALU.mult)
            k.tt(tBv2, V(a_im, "aim"), V(C_re, "c_s"), ALU.mult)
            k.stt(V(G[ps_, :, 1, cs_], "G"), tAv2, -1.0, tBv2, ALU.mult, ALU.subtract)
        for i4 in range(TC // 2):
            pt = k.psum()
            for q_ in range(2):
                i = i4 * 2 + q_
                for ri in range(2):
                    k.tr(pt[0:32, (q_ * 2 + ri) * 128:(q_ * 2 + ri + 1) * 128], V(Wtab[:, i, ri, :], "Wtab"), identf[:])
            k.copy(V(WT[:, i4 * 2:i4 * 2 + 2, :, :], "WT"), pt[0:32, :].rearrange("p (i r m) -> p i r m", i=2, r=2),
                   eng=('act' if i4 % 2 else 'dve'))
        idb = k.sb("idb", [32, 32], F32) if t == 0 else idb
        k.ts(idb[:], identf[0:32, 0:32], dsk_s[:, t:t + 1], ALU.mult)
        k.copy(dskd[:], idb[:])
        for i4 in range(TC // 4):
            pzr = k.psum()
            pzi = k.psum()
            for q_ in range(4):
                i = i4 * 4 + q_
                rhs = u[:, i:i + TC * (NC_ - 1) + 1:TC]
                k.mm(pzr[:, q_ * 128:(q_ + 1) * 128], V(WT[:, i, 0, :], "WT"), rhs)
                k.mm(pzi[:, q_ * 128:(q_ + 1) * 128], V(WT[:, i, 1, :], "WT"), rhs)
            for (pz, Z, e_) in ((pzr, Zre, 'dve'), (pzi, Zim, 'act')):
                dst = V(Z.ap.rearrange("p (c i) -> p c i", i=TC)[:, :, i4 * 4:i4 * 4 + 4], "arena")
                k.copy(dst, pz[:].rearrange("p (i c) -> p c i", i=4), eng=e_)
        segf = V(seg[:].rearrange("p c i -> p (c i)"), "seg")
        k.scan(Cre[:], segf, Zre, 0.0, ALU.mult, ALU.add)
        k.scan(Cim[:], segf, Zim, 0.0, ALU.mult, ALU.add)
        def pl(name):
            return k.sb(name, [128, NC_], F32) if t == 0 else getattr(k, "_pl_" + name)
        Ere, Eim = pl("Ere"), pl("Eim")
        X0r, X0i, X1r, X1i = pl("X0r"), pl("X0i"), pl("X1r"), pl("X1i")
        for nm_, o_ in (("Ere", Ere), ("Eim", Eim), ("X0r", X0r), ("X0i", X0i), ("X1r", X1r), ("X1i", X1i)):
            setattr(k, "_pl_" + nm_, o_)
        k.copy(Ere[:], Cre[:, TC - 1:TC - 1 + TC * (NC_ - 1) + 1:TC])
        k.copy(Eim[:], Cim[:, TC - 1:TC - 1 + TC * (NC_ - 1) + 1:TC])
        a63r, a63i = are[:, 63:64], aim[:, 63:64]
        na63i = col("na63i")
        k.ts(na63i[:], a63i, -1.0, ALU.mult)
        k.ts(X0r[:], Ere[:], a63r, ALU.mult)
        k.stt(X0r[:], Eim[:], na63i[:, 0:1], X0r[:], ALU.mult, ALU.add)
        k.ts(X0i[:], Ere[:], a63i, ALU.mult)
        k.stt(X0i[:], Eim[:], a63r, X0i[:], ALU.mult, ALU.add)
        Akr, Aki, nAki, Ak2 = col("Akr"), col("Aki"), col("nAki"), col("Ak2")
        k.copy(Akr[:], are[:, 64:65])
        k.copy(Aki[:], aim[:, 64:65])
        cur = (X0r, X0i)
        nxt = (X1r, X1i)
        for st_ in range(7):
            s = 1 << st_
            k.ts(nAki[:], Aki[:], -1.0, ALU.mult)
            cr, ci_ = cur
            nr, ni = nxt
            k.copy(nr[:, 0:s], cr[:, 0:s])
            k.copy(ni[:, 0:s], ci_[:, 0:s])
            n_ = NC_ - s
            k.stt(nr[:, s:], cr[:, 0:n_], Akr[:, 0:1], cr[:, s:], ALU.mult, ALU.add)
            k.stt(nr[:, s:], ci_[:, 0:n_], nAki[:, 0:1], nr[:, s:], ALU.mult, ALU.add)
            k.stt(ni[:, s:], ci_[:, 0:n_], Akr[:, 0:1], ci_[:, s:], ALU.mult, ALU.add)
            k.stt(ni[:, s:], cr[:, 0:n_], Aki[:, 0:1], ni[:, s:], ALU.mult, ALU.add)
            cur, nxt = nxt, cur
            if st_ < 6:
                k.tt(Ak2[:], Aki[:], Aki[:], ALU.mult)
                k.tt(Aki[:], Akr[:], Aki[:], ALU.mult)
                k.ts(Aki[:], Aki[:], 2.0, ALU.mult)
                k.stt(Akr[:], Akr[:], Akr[:, 0:1], Ak2[:], ALU.mult, ALU.subtract)
        Xpr = k.sb("Xpr", [128, NC_], BF16) if t == 0 else Xpr
        Xpi = k.sb("Xpi", [128, NC_], BF16) if t == 0 else Xpi
        k.memset(V(Xpr[:, 0:1], "Xpr"), 0.0)
        k.memset(V(Xpi[:, 0:1], "Xpi"), 0.0)
        k.copy(V(Xpr[:, 1:NC_], "Xpr"), cur[0][:, 0:NC_ - 1])
        k.copy(V(Xpi[:, 1:NC_], "Xpi"), cur[1][:, 0:NC_ - 1])
        for j4 in range(TC // 4):
            py = k.psum()
            for q_ in range(4):
                j = j4 * 4 + q_
                o_ = py[0:32, q_ * 128:(q_ + 1) * 128]
                sl = slice(j, j + TC * (NC_ - 1) + 1, TC)
                k.mm(o_, V(G[:, j, 0, :], "G"), Cre[:, sl], start=True, stop=False)
                k.mm(o_, V(G[:, j, 1, :], "G"), Cim[:, sl], start=False, stop=False)
                k.mm(o_, V(G[:, j + 1, 0, :], "G"), Xpr[:], start=False, stop=False)
                k.mm(o_, V(G[:, j + 1, 1, :], "G"), Xpi[:], start=False, stop=False)
                k.mm(o_, dskd[:], u[:, sl], start=False, stop=True)
            dst = V(ybuf.ap.rearrange("p (c i) -> p c i", i=TC)[:, :, j4 * 4:j4 * 4 + 4], "arena")
            k.copy(dst, py[0:32, :].rearrange("p (i c) -> p c i", i=4), eng=('act' if j4 % 2 else 'dve'))
        k.dma(V(ys[t], uk("o")), ybuf)


FGROUPS = [("a_z", 4608, 512), ("b_z", 5760, 512), ("c_o", 7304, 512), ("c_z", 7816, 512), ("d_z", 8840, 512), ("gate", 9352, 8192)]
FCOLS = []
FOFF = {}
for (_n, _o, _w) in FGROUPS:
    FOFF[_n] = len(FCOLS)
    FCOLS.extend(range(_o, _o + _w))
FW = len(FCOLS)
GELU_C = 1.5957691216057308


def build_F():
    k = K()
    x = k.din("x", [T, 2048])
    w_f = k.din("w_f", [2048, FW])
    norm_g = k.din("norm_g", [1, 2048])
    mbd = k.din("merge_b", [128, 64])
    gbd = k.din("glu_b", [128, 4])
    glu_w = k.din("glu_w", [512, 512])
    w_up = k.din("w_up", [4, 512, 2048])
    w_out = k.din("w_out", [2048, 2048])
    yaT = k.din("yaT", [512, T])
    ybT = k.din("ybT", [512, T])
    ycT = k.din("ycT", [512, T])
    y5T = k.din("y5T", [512, T])
    xo = k.dout("xo", [T, 2048])

    ident, ones = consts(k)
    mT = k.sb("mT", [128, 16, T], BF16)
    mk = ("mT_0", "mT_1")
    ext = (V(mT[:, 0:4, :].rearrange("p a b -> p (a b)").bitcast(F32), mk),
           V(mT[:, 8:10, :].rearrange("p a b -> p (a b)"), mk),
           V(mT[:, 4:8, :].rearrange("p a b -> p (a b)").bitcast(F32), mk))
    hT = norm_hT(k, x, norm_g, ident, ext=ext)
    mb = k.sb("mb", [128, 64], F32)
    k.dma(mb[:], mbd)
    gb = k.sb("gb", [128, 4], F32)
    k.dma(gb[:], gbd)
    ysT = k.sb("ysT", [128, 16, T], BF16)

    def wload(c0, ncols):
        wt = k.pool("wt", [128, 16, 512], BF16, 2)
        k.dma(wt[:, :, 0:ncols], w_f[:, c0:c0 + ncols].rearrange("(kt p) c -> p kt c", p=128), q='pool')
        return wt

    def proj(wt, wc0, half):
        ps = k.psum()
        for kt in range(16):
            k.mm(ps[:], wt[:, kt, wc0:wc0 + 128], V(hT[:, kt, half * 512:(half + 1) * 512], hk(half)),
                 start=(kt == 0), stop=(kt == 15))
        return ps

    def ytile(src, wt_, half):
        t = k.pool("yt", [128, 512], F32, 2)
        k.dma(t[:], src[wt_ * 128:(wt_ + 1) * 128, half * 512:(half + 1) * 512])
        return t

    def yk(n, wt_, half):
        return "ysT_%d_%d" % (n * 4 + wt_, half)

    wz = {nm: wload(FOFF[nm], 512) for nm in ("a_z", "b_z")}
    for n, (nm, src) in enumerate((("a_z", yaT), ("b_z", ybT))):
        for wt_ in range(4):
            for half in range(NH):
                ps = proj(wz[nm], wt_ * 128, half)
                s = k.pool("sz", [128, 512], F32, 3)
                k.act(s[:], ps[:], AF.Silu)
                y = ytile(src, wt_, half)
                k.tt(V(ysT[:, n * 4 + wt_, half * 512:(half + 1) * 512], yk(n, wt_, half)), s[:], y[:], ALU.mult)
    wo = wload(FOFF["c_o"], 512)
    wc = wload(FOFF["c_z"], 512)
    for wt_ in range(4):
        for half in range(NH):
            ps = proj(wo, wt_ * 128, half)
            so = k.pool("sz", [128, 512], F32, 3)
            k.act(so[:], ps[:], AF.Sigmoid)
            ps2 = proj(wc, wt_ * 128, half)
            s = k.pool("sz", [128, 512], F32, 3)
            k.act(s[:], ps2[:], AF.Silu)
            y = ytile(ycT, wt_, half)
            k.tt(s[:], s[:], so[:], ALU.mult)
            k.tt(V(ysT[:, 8 + wt_, half * 512:(half + 1) * 512], yk(2, wt_, half)), s[:], y[:], ALU.mult)
    gdb = k.sb("gdb", [128, 4, T], BF16)
    for wt_ in range(4):
        for half in range(NH):
            y = ytile(y5T, wt_, half)
            t1 = k.pool("sz", [128, 512], F32, 3)
            k.tt(t1[:], y[:], y[:], ALU.mult)
            k.ts(t1[:], t1[:], 0.044715, ALU.mult, 1.0, ALU.add)
            k.tt(t1[:], t1[:], y[:], ALU.mult)
            k.act(t1[:], t1[:], AF.Sigmoid, scale=GELU_C)
            k.tt(V(gdb[:, wt_, half * 512:(half + 1) * 512], "gdb%d_%d" % (wt_, half)), t1[:], y[:], ALU.mult)
    wg = k.sb("wglu", [128, 4, 512], BF16)
    k.dma(wg[:], glu_w.rearrange("(kt p) c -> p kt c", p=128), q='pool')
    wd = wload(FOFF["d_z"], 512)
    for ob in range(4):
        for half in range(NH):
            ps = k.psum()
            for kt in range(4):
                k.mm(ps[:], wg[:, kt, ob * 128:(ob + 1) * 128],
                     V(gdb[:, kt, half * 512:(half + 1) * 512], "gdb%d_%d" % (kt, half)), start=(kt == 0), stop=(kt == 3))
            sg = k.pool("sz", [128, 512], F32, 3)
            k.act(sg[:], ps[:], AF.Sigmoid, bias=gb[:, ob:ob + 1])
            ps2 = proj(wd, ob * 128, half)
            s = k.pool("sz", [128, 512], F32, 3)
            k.act(s[:], ps2[:], AF.Silu)
            k.tt(s[:], s[:], sg[:], ALU.mult)
            k.tt(V(ysT[:, 12 + ob, half * 512:(half + 1) * 512], yk(3, ob, half)), s[:],
                 V(gdb[:, ob, half * 512:(half + 1) * 512], "gdb%d_%d" % (ob, half)), ALU.mult)

    g0 = FOFF["gate"]
    w_up_r = w_up.rearrange("n (wt p) d -> n p wt d", p=128)
    for db in range(16):
        wt = k.pool("wt", [128, 16, 512], BF16, 2)
        wu = k.pool("wup", [128, 4, 4, 128], BF16, 2)
        for n in range(4):
            c0 = g0 + n * 2048 + db * 128
            k.dma(wt[:, :, n * 128:(n + 1) * 128],
                  w_f[:, c0:c0 + 128].rearrange("(kt p) c -> p kt c", p=128), q='pool')
            k.dma(wu[:, n, :, :], w_up_r[n, :, :, db * 128:(db + 1) * 128], q='pool')
        for half in range(NH):
            acc = k.pool("acc", [128, 512], F32, 2)
            for n in range(4):
                pg = k.psum()
                for kt in range(16):
                    k.mm(pg[:], wt[:, kt, n * 128:(n + 1) * 128],
                         V(hT[:, kt, half * 512:(half + 1) * 512], hk(half)), start=(kt == 0), stop=(kt == 15))
                pu = k.psum()
                for wt_ in range(4):
                    k.mm(pu[:], wu[:, n, wt_, :],
                         V(ysT[:, n * 4 + wt_, half * 512:(half + 1) * 512], yk(n, wt_, half)), start=(wt_ == 0), stop=(wt_ == 3))
                sg = k.pool("sz", [128, 512], F32, 3)
                k.act(sg[:], pg[:], AF.Sigmoid, bias=mb[:, n * 16 + db:n * 16 + db + 1])
                if n == 0:
                    k.tt(acc[:], sg[:], pu[:], ALU.mult)
                elif n < 3:
                    k.tt(sg[:], sg[:], pu[:], ALU.mult)
                    k.tt(acc[:], acc[:], sg[:], ALU.add)
                else:
                    k.tt(sg[:], sg[:], pu[:], ALU.mult)
                    k.tt(V(mT[:, db, half * 512:(half + 1) * 512], "mT_%d" % half), acc[:], sg[:], ALU.add)

    for cg in range(4):
        wt = k.pool("wt", [128, 16, 512], BF16, 2)
        k.dma(wt[:], w_out[:, cg * 512:(cg + 1) * 512].rearrange("(kt p) c -> p kt c", p=128), q='pool')
        for tt in range(T // 128):
            ps = k.psum()
            for kt in range(16):
                k.mm(ps[:], V(mT[:, kt, tt * 128:(tt + 1) * 128], "mT_%d" % (tt // 4)), wt[:, kt, :], start=(kt == 0), stop=(kt == 15))
            xt = k.pool("yt", [128, 512], F32, 2)
            k.dma(xt[:], x[tt * 128:(tt + 1) * 128, cg * 512:(cg + 1) * 512])
            o = k.pool("sz", [128, 512], F32, 3)
            k.tt(o[:], xt[:], ps[:], ALU.add)
            k.dma(V(xo[tt * 128:(tt + 1) * 128, cg * 512:(cg + 1) * 512], uk("o")), o[:])
    return k.finish()


_PROGS = {}
import sys as _sys
import time as _time
_T0 = [_time.time()]


def _log(msg):
    print("[kernel %.1fs] %s" % (_time.time() - _T0[0], msg), file=_sys.stderr, flush=True)


def _prog(name):
    if name not in _PROGS:
        if name == "P":
            _PROGS[name] = build_P()
        elif name == "AB":
            _PROGS[name] = build_M("AB")
        elif name == "C":
            _PROGS[name] = build_M("C")
        elif name == "D":
            _PROGS[name] = build_M("D")
        elif name == "F":
            _PROGS[name] = build_F()
    return _PROGS[name]


def _c(a, dt=None):
    return np.ascontiguousarray(a) if dt is None else np.ascontiguousarray(a, dtype=dt)


def _prep_C(core, cqk_full, cv_full, cif_full, conv_w, conv_b, i_b, f_b):
    hc, dh = core // 2, core % 2
    qch = slice(hc * 64, hc * 64 + 64)
    kch = slice(256 + hc * 64, 256 + hc * 64 + 64)
    z3 = np.zeros((64, 3), np.float32)
    m = dict(cq_h=_c(np.concatenate([z3, cqk_full[qch]], 1)), ck_h=_c(np.concatenate([z3, cqk_full[kch]], 1)),
             cw_q=_c(conv_w[:, qch].T), cb_q=_c(conv_b[qch][:, None]),
             cw_k=_c(conv_w[:, kch].T), cb_k=_c(conv_b[kch][:, None]),
             cv_h=_c(cv_full[:, hc * 128 + dh * 64: hc * 128 + dh * 64 + 64]),
             gi=_c(cif_full[hc].reshape(128, 64)), gf=_c(cif_full[4 + hc].reshape(128, 64)),
             gb=_c(np.stack([i_b[hc:hc + 1], f_b[hc:hc + 1]], axis=1), np.float32))
    m.update(host_constsC())
    return m


def _prep_D(core, du_full, W):
    g0 = 4 * core
    lam = np.zeros((128, 2, 3), np.float32)
    bm = np.zeros((128, 2, 2, 16), np.float32)
    cm = np.zeros((128, 2, 2, 16), np.float32)
    dsk = np.zeros((32, 2), np.float32)
    du = np.zeros((2, 32, 8192), du_full.dtype)
    for t in range(2):
        for gi in range(2):
            g = g0 + 2 * t + gi
            ps = slice(gi * 64, gi * 64 + 64)
            lam[ps, t, 0] = W['d_lam_re'][g]
            lam[ps, t, 1] = W['d_lam_im'][g]
            lam[ps, t, 2] = W['d_log_dt'][g]
            bm[ps, t, 0] = W['d_b_re'][g]
            bm[ps, t, 1] = W['d_b_im'][g]
            cm[ps, t, 0] = W['d_c_re'][g].T
            cm[ps, t, 1] = W['d_c_im'][g].T
            dsk[gi * 16:(gi + 1) * 16, t] = W['d_skip'][g * 16:(g + 1) * 16]
            du[t, gi * 16:(gi + 1) * 16] = du_full[g * 16:(g + 1) * 16]
    m = dict(du_g=du, d_lam=lam, d_b=bm, d_c=cm, d_sk=dsk)
    m.update(host_constsD())
    return m


def kernel(x, positions, norm_g, w_in, a_qn_g, a_kn_g, b_cq_g, b_ckv_g, b_w_uq, b_w_ukv, b_qn_g, b_kn_g,
           c_conv_w, c_conv_b, c_i_b, c_f_b, d_lam_re, d_lam_im, d_log_dt, d_b_re, d_b_im, d_c_re, d_c_im,
           d_skip, d_glu_w, d_glu_b, w_up, merge_b, w_out):
    f32 = np.float32
    cur = np.asarray(x, f32)[0]
    pos = np.asarray(positions).astype(np.int32)
    eye = np.eye(128, dtype=f32)
    invf = (10000.0 ** (-(np.arange(0, 64, 2, dtype=f32)) / 64)).astype(f32)
    c_invf = np.concatenate([invf, invf])[:, None].astype(f32)
    c_sgn = np.concatenate([-np.ones(32), np.ones(32)])[:, None].astype(f32)
    biasA = [host_biasA(c) for c in range(NCORES)]
    maskB = [host_maskB(p) for p in range(2)]
    TS = [slice(c * T, (c + 1) * T) for c in range(NCORES)]
    for l in range(4):
        g = lambda a: np.asarray(a[l], f32)
        w_p = _c(g(w_in)[:, PCOLS])
        maps = []
        for c in range(NCORES):
            m = dict(c_ident=eye, c_invf=c_invf, c_sgn=c_sgn, x=_c(cur[TS[c]]), w_p=w_p, pos=_c(pos[:, TS[c]]),
                     norm_g=_c(g(norm_g)[None]), a_qn_g=_c(g(a_qn_g)[None]), a_kn_g=_c(g(a_kn_g)[None]),
                     b_cq_g=_c(g(b_cq_g)[None]), b_ckv_g=_c(g(b_ckv_g)[None]), b_qn_g=_c(g(b_qn_g)[None]),
                     b_kn_g=_c(g(b_kn_g)[None]), b_w_uq=_c(g(b_w_uq)), b_w_ukv=_c(g(b_w_ukv)))
            maps.append(m)
        _log("layer %d P" % l)
        rp = run(_prog("P"), maps).results
        del w_p, maps
        qfull = np.concatenate([rp[c]["qb"] for c in range(NCORES)], axis=2)
        kfull = np.concatenate([rp[c]["kb"] for c in range(NCORES)], axis=2)
        vfull = np.concatenate([rp[c]["vb"] for c in range(NCORES)], axis=0)
        maps = []
        toks = []
        for c in range(NCORES):
            def ext(name, axis):
                parts = []
                for cc in (c - 2, c - 1, c):
                    a = rp[max(cc, 0)][name]
                    parts.append(a if cc >= 0 else np.zeros_like(a))
                return _c(np.concatenate(parts, axis=axis))
            h, par = c // 2, c % 2
            tk_ = np.concatenate([np.arange(512 * (2 * j + par), 512 * (2 * j + par) + 512) for j in range(8)])
            toks.append(tk_)
            m = dict(c_ident=eye, qa=_c(rp[c]["qa"]), ka_ext=ext("ka", 2), va_ext=ext("va", 0), c_biasA=biasA[c],
                     qb_h=_c(qfull[h][:, tk_]), kb_h=_c(kfull[h]), vb_h=_c(vfull[:, h * 128:(h + 1) * 128]),
                     c_maskB=maskB[par])
            maps.append(m)
        _log("layer %d AB" % l)
        rab = run(_prog("AB"), maps).results
        del maps, qfull, kfull, vfull
        yaT = [rab[c]["ya"] for c in range(NCORES)]
        ybT = np.zeros((512, 8192), f32)
        for c in range(NCORES):
            h = c // 2
            ybT[h * 128:(h + 1) * 128, toks[c]] = rab[c]["yb"]
        cqk_full = np.concatenate([rp[c]["cqk"] for c in range(NCORES)], 1)
        cv_full = np.concatenate([rp[c]["cv"] for c in range(NCORES)], 0)
        cif_full = np.concatenate([rp[c]["cif"] for c in range(NCORES)], 1)
        maps = []
        for c in range(NCORES):
            m = _prep_C(c, cqk_full, cv_full, cif_full, g(c_conv_w), g(c_conv_b), g(c_i_b), g(c_f_b))
            m["c_ident"] = eye
            maps.append(m)
        _log("layer %d C" % l)
        rc = run(_prog("C"), maps).results
        ycT = np.zeros((512, 8192), f32)
        for c in range(NCORES):
            hc_, dh = c // 2, c % 2
            ycT[hc_ * 128 + dh * 64: hc_ * 128 + dh * 64 + 64, :] = rc[c]["hc"].T
        du_full = np.concatenate([rp[c]["du"] for c in range(NCORES)], 1)
        Wd = dict(d_lam_re=g(d_lam_re), d_lam_im=g(d_lam_im), d_log_dt=g(d_log_dt), d_b_re=g(d_b_re), d_b_im=g(d_b_im),
                  d_c_re=g(d_c_re), d_c_im=g(d_c_im), d_skip=g(d_skip))
        maps = []
        for c in range(NCORES):
            m = _prep_D(c, du_full, Wd)
            m["c_ident"] = eye
            maps.append(m)
        _log("layer %d D" % l)
        rd = run(_prog("D"), maps).results
        y5T = np.zeros((512, 8192), f32)
        for c in range(NCORES):
            for t in range(2):
                r0 = (4 * c + 2 * t) * 16
                y5T[r0:r0 + 32, :] = rd[c]["ys5"][t]
        del rp, maps
        w_f = _c(g(w_in)[:, FCOLS])
        mb = _c(g(merge_b).reshape(64, 128).T)
        gbv = _c(g(d_glu_b).reshape(4, 128).T)
        maps = []
        for c in range(NCORES):
            m = dict(c_ident=eye, x=_c(cur[TS[c]]), w_f=w_f, norm_g=_c(g(norm_g)[None]), merge_b=mb, glu_b=gbv,
                     glu_w=_c(g(d_glu_w)), w_up=_c(g(w_up)), w_out=_c(g(w_out)),
                     yaT=_c(yaT[c]), ybT=_c(ybT[:, TS[c]]), ycT=_c(ycT[:, TS[c]]), y5T=_c(y5T[:, TS[c]]))
            maps.append(m)
        _log("layer %d F" % l)
        rf = run(_prog("F"), maps).results
        cur = np.concatenate([rf[c]["xo"] for c in range(NCORES)], axis=0).astype(f32)
        del w_f, maps, rf
    return cur[None].astype(np.float32)
```

```python
import contextlib
import numpy as np
import concourse.bass as bass
import concourse.mybir as mybir
from concourse.bass_utils import run_bass_kernel_spmd

F32 = mybir.dt.float32
BF16 = mybir.dt.bfloat16
I32 = mybir.dt.int32
AF = mybir.ActivationFunctionType
ALU = mybir.AluOpType
AX = mybir.AxisListType
EPS = 1e-6
NCORES = 8


class Sched:
    CE = ('pe', 'act', 'dve', 'pool', 'sp')
    KDMA = 8

    def __init__(self):
        self.ops = {e: [] for e in self.CE}
        self.last_w = {}
        self.readers = {}
        self.waited = {e: {} for e in self.CE}
        self.ndma = {e: 0 for e in self.CE}
        self.fragile = set()

    def _deps(self, E, reads, writes):
        toks = []
        for b in reads:
            t = self.last_w.get(b)
            if t is not None:
                toks.append(t)
            if b.startswith("ps"):
                toks.extend(v for kk, v in self.readers.get(b, {}).items() if kk != E)
        for b in writes:
            t = self.last_w.get(b)
            if t is not None:
                toks.append(t)
            toks.extend(self.readers.get(b, {}).values())
        out = []
        for (key, val) in toks:
            if key == E and E == 'pe':
                continue
            if self.waited[E].get(key, -1) >= val:
                continue
            self.waited[E][key] = val
            out.append((key, val))
        return out

    def _commit(self, tok, reads, writes):
        for b in reads:
            d = self.readers.setdefault(b, {})
            k = tok[0]
            if k not in d or d[k][1] < tok[1]:
                d[k] = tok
        for b in writes:
            self.last_w[b] = tok
            self.readers[b] = {}

    def op(self, E, fn, reads=(), writes=(), fragile=False):
        waits = self._deps(E, reads, writes)
        idx = len(self.ops[E])
        self.ops[E].append(dict(fn=fn, waits=waits, dma=None, signal=False))
        if fragile and E != 'pe':
            self.fragile.add((E, idx))
        self._commit((E, idx), reads, writes)

    def dma(self, Q, fn, reads=(), writes=()):
        n = self.ndma[Q]
        self.ndma[Q] += 1
        slot = n % self.KDMA
        key = ('dma', Q, slot)
        waits = self._deps(Q, reads, writes)
        if n >= self.KDMA:
            v = 16 * (n // self.KDMA)
            if self.waited[Q].get(key, -1) < v:
                self.waited[Q][key] = v
                waits.append((key, v))
        self.ops[Q].append(dict(fn=fn, waits=waits, dma=key, signal=False))
        self._commit((key, 16 * (n // self.KDMA + 1)), reads, writes)

    def final_wait(self, E, bufs=()):
        waits = []
        for q in self.CE:
            n = self.ndma[q]
            for s in range(min(n, self.KDMA)):
                cnt = (n - s + self.KDMA - 1) // self.KDMA
                waits.append((('dma', q, s), 16 * cnt))
        self.ops[E].append(dict(fn=None, waits=waits, dma=None, signal=False))

    def emit(self, nc, st):
        for e in self.CE:
            for o in self.ops[e]:
                for (key, val) in o['waits']:
                    if isinstance(key, str):
                        self.ops[key][val]['signal'] = True
        rank = {}
        for e in self.CE:
            c = 0
            r = []
            for o in self.ops[e]:
                if o['signal'] and o['dma'] is None:
                    c += 1
                r.append(c)
            rank[e] = r
        sems = {e: st.enter_context(nc.semaphore("s_" + e)) for e in self.CE}
        for q in self.CE:
            if self.ndma[q]:
                for s in range(self.KDMA):
                    sems[('dma', q, s)] = st.enter_context(nc.semaphore("d_%s%d" % (q, s)))
        block = st.enter_context(nc.Block())

        def run(e):
            def body(eng):
                for o in self.ops[e]:
                    for (key, val) in o['waits']:
                        if isinstance(key, str):
                            eng.wait_ge(sems[key], rank[key][val])
                        else:
                            eng.wait_ge(sems[key], val)
                    if o['fn'] is None:
                        continue
                    inst = o['fn'](eng)
                    if o['dma'] is not None:
                        inst.then_inc(sems[o['dma']], 16)
                    elif o['signal']:
                        inst.then_inc(sems[e], 1)
            return body
        if self.ops['sp']:
            block.sync(run('sp'))
        if self.ops['pe']:
            block.tensor(run('pe'))
        if self.ops['act']:
            block.scalar(run('act'))
        if self.ops['dve']:
            block.vector(run('dve'))
        if self.ops['pool']:
            block.gpsimd(run('pool'))


class V:
    def __init__(self, ap, key):
        self.ap = ap
        self.key = key


def _ap(x):
    return x.ap if isinstance(x, V) else x


def _key(x):
    if isinstance(x, V):
        return list(x.key) if isinstance(x.key, (tuple, list)) else [x.key]
    return [x.tensor.name]


def _frag(out, accum=None):
    try:
        n = _ap(out).free_size()
    except Exception:
        n = 0
    return (n < 256) or (accum is not None)


class K:
    def __init__(self):
        self.nc = bass.Bass("TRN2", target_bir_lowering=False)
        self.S = Sched()
        self.st = contextlib.ExitStack()
        self.pools = {}
        self.outs = []
        self.psb = None
        self.psi = 0

    def din(self, name, shape, dt=F32):
        return self.nc.dram_tensor(name, list(shape), dt, kind="ExternalInput").ap()

    def dout(self, name, shape, dt=F32):
        ap = self.nc.dram_tensor(name, list(shape), dt, kind="ExternalOutput").ap()
        self.outs.append(name)
        return ap

    def sb(self, name, shape, dt=F32):
        return self.st.enter_context(self.nc.sbuf_tensor(name, list(shape), dt))

    def pool(self, name, shape, dt, n):
        if name not in self.pools:
            self.pools[name] = [[self.sb("%s_%d" % (name, i), shape, dt) for i in range(n)], 0]
        p = self.pools[name]
        t = p[0][p[1] % len(p[0])]
        p[1] += 1
        return t

    def psum(self, lo=0, hi=8):
        if self.psb is None:
            self.psb = [self.st.enter_context(self.nc.psum_tensor("ps%d" % i, [128, 512], F32)) for i in range(8)]
        t = self.psb[lo + self.psi % (hi - lo)]
        self.psi += 1
        return t

    def bank(self, i):
        self.psum(0, 8) if self.psb is None else None
        return self.psb[i]

    def dma(self, out, in_, q='sp', **kw):
        o, i = _ap(out), _ap(in_)
        self.S.dma(q, lambda e: e.dma_start(out=o, in_=i, **kw), reads=_key(in_), writes=_key(out))

    def mm(self, out, lhsT, rhs, start=True, stop=True, **kw):
        o, l, r = _ap(out), _ap(lhsT), _ap(rhs)
        self.S.op('pe', lambda e: e.matmul(o, lhsT=l, rhs=r, start=start, stop=stop, **kw),
                  reads=_key(lhsT) + _key(rhs) + ([] if start else _key(out)), writes=_key(out))

    def tr(self, out, in_, ident):
        o, i, d = _ap(out), _ap(in_), _ap(ident)
        self.S.op('pe', lambda e: e.transpose(o, i, d), reads=_key(in_) + _key(ident), writes=_key(out))

    def act(self, out, in_, func, bias=0.0, scale=1.0, accum_out=None, eng='act'):
        o, i = _ap(out), _ap(in_)
        rd = _key(in_)
        wr = _key(out)
        b = bias
        s = scale
        if not isinstance(bias, (int, float)):
            rd += _key(bias)
            b = _ap(bias)
        if not isinstance(scale, (int, float)):
            rd += _key(scale)
            s = _ap(scale)
        kw = {}
        if accum_out is not None:
            kw['accum_out'] = _ap(accum_out)
            wr += _key(accum_out)
        if func == AF.Copy:
            self.S.op(eng, lambda e: e.activation(out=o, in_=i, func=func, scale=s, **kw), reads=rd, writes=wr, fragile=_frag(out, accum_out))
        else:
            self.S.op(eng, lambda e: e.activation(out=o, in_=i, func=func, bias=b, scale=s, **kw), reads=rd, writes=wr, fragile=_frag(out, accum_out))

    def tt(self, out, in0, in1, op, eng='dve'):
        o, a, b = _ap(out), _ap(in0), _ap(in1)
        self.S.op(eng, lambda e: e.tensor_tensor(out=o, in0=a, in1=b, op=op),
                  reads=_key(in0) + _key(in1), writes=_key(out), fragile=_frag(out))

    def ts(self, out, in0, s1, op0, s2=None, op1=None, eng='dve', accum_out=None):
        o, a = _ap(out), _ap(in0)
        rd = _key(in0)
        wr = _key(out)
        x1, x2 = s1, s2
        if not isinstance(s1, (int, float)):
            rd += _key(s1)
            x1 = _ap(s1)
        if s2 is not None and not isinstance(s2, (int, float)):
            rd += _key(s2)
            x2 = _ap(s2)
        kw = {}
        if op1 is not None:
            kw['op1'] = op1
        if accum_out is not None:
            kw['accum_out'] = _ap(accum_out)
            wr += _key(accum_out)
        self.S.op(eng, lambda e: e.tensor_scalar(out=o, in0=a, scalar1=x1, scalar2=x2, op0=op0, **kw),
                  reads=rd, writes=wr, fragile=_frag(out, accum_out))

    def stt(self, out, in0, scalar, in1, op0, op1, eng='dve'):
        eng = 'dve'
        o, a, b = _ap(out), _ap(in0), _ap(in1)
        rd = _key(in0) + _key(in1)
        s = scalar
        if not isinstance(scalar, (int, float)):
            rd += _key(scalar)
            s = _ap(scalar)
        self.S.op(eng, lambda e: e.scalar_tensor_tensor(out=o, in0=a, scalar=s, in1=b, op0=op0, op1=op1),
                  reads=rd, writes=_key(out), fragile=_frag(out))

    def copy(self, out, in_, eng='dve'):
        o, i = _ap(out), _ap(in_)
        if eng == 'act':
            self.S.op('act', lambda e: e.copy(out=o, in_=i), reads=_key(in_), writes=_key(out), fragile=_frag(out))
        else:
            self.S.op(eng, lambda e: e.tensor_copy(out=o, in_=i), reads=_key(in_), writes=_key(out), fragile=_frag(out))

    def memset(self, out, val, eng='dve'):
        o = _ap(out)
        self.S.op(eng, lambda e: e.memset(o, val), reads=[], writes=_key(out), fragile=_frag(out))

    def recip(self, out, in_):
        o, i = _ap(out), _ap(in_)
        self.S.op('dve', lambda e: e.reciprocal(out=o, in_=i), reads=_key(in_), writes=_key(out), fragile=_frag(out))

    def scan(self, out, d0, d1, initial, op0, op1):
        o, a, b = _ap(out), _ap(d0), _ap(d1)
        rd = _key(d0) + _key(d1)
        ini = initial
        if not isinstance(initial, (int, float)):
            rd += _key(initial)
            ini = _ap(initial)
        self.S.op('dve', lambda e: e.tensor_tensor_scan(out=o, data0=a, data1=b, initial=ini, op0=op0, op1=op1),
                  reads=rd, writes=_key(out), fragile=_frag(out))

    def finish(self):
        rem = self.nc.sbuf_bytes_remaining
        rem = rem() if callable(rem) else rem
        used = 229376 - rem
        print("SBUF used per partition: %.1f KB" % (used / 1024.0))
        assert used <= 190 * 1024, "SBUF over budget (top of SBUF is reserved: DMA rings)"
        self.S.final_wait('sp')
        self.S.emit(self.nc, self.st)
        self.st.close()
        return self.nc


def run(nc, in_maps, trace=False):
    return run_bass_kernel_spmd(nc, in_maps, core_ids=list(range(len(in_maps))), trace=trace)


import math

T = 1024
NH = T // 512
OFF = dict(a_q=0, a_k=1536, a_v=3072, a_z=4608, b_cq=5120, b_ckv=5568, b_kr=5696, b_z=5760,
           c_qk=6272, c_v=6784, c_i=7296, c_f=7300, c_o=7304, c_z=7816, d_u=8328, d_z=8840, gate=9352)
IN_W = 17544
PGROUPS = [("a_q", 0, 1536), ("a_k", 1536, 1536), ("a_v", 3072, 1536), ("b_cq", 5120, 448), ("b_ckv", 5568, 128),
           ("b_kr", 5696, 64), ("c_qk", 6272, 512), ("c_v", 6784, 512), ("c_i", 7296, 4), ("c_f", 7300, 4), ("d_u", 8328, 512)]
PCOLS = []
POFF = {}
for (_n, _o, _w) in PGROUPS:
    POFF[_n] = len(PCOLS)
    PCOLS.extend(range(_o, _o + _w))
PW = len(PCOLS)
_uid = [0]


def uk(p):
    _uid[0] += 1
    return "%s#%d" % (p, _uid[0])


def load_w(k, w_ap, c0, ncols, nk=16, rows=None):
    wt = k.pool("wt", [128, 16, 512], BF16, 2)
    if rows is None:
        src = w_ap[:, c0:c0 + ncols].rearrange("(kt p) c -> p kt c", p=128)
        k.dma(wt[:, 0:nk, 0:ncols], src, q='pool')
    return wt


def hk(half):
    return tuple("hT%d" % tt for tt in range(half * 4, half * 4 + 4))


def norm_hT(k, x_ap, norm_g, ident, keep_x=None, ext=None):
    NT = T // 128
    eb = epsb(k)
    if ext is None or ext[2] is None:
        grep = k.sb("grep", [128, 2048], F32)[:]
    else:
        grep = ext[2]
    k.dma(grep, norm_g.partition_broadcast(128))
    hT = k.sb("hT", [128, 16, T], BF16)
    def stats(tt):
        if keep_x is not None:
            xt = V(keep_x[:, tt, :], "xs%d" % tt)
        elif ext is not None:
            xt = ext[0][tt % len(ext[0])] if isinstance(ext[0], list) else ext[0]
        else:
            xt = k.pool("xt", [128, 2048], F32, 2)[:]
        k.dma(xt, x_ap[tt * 128:(tt + 1) * 128, :])
        ss = k.pool("ss", [128, 1], F32, 2)
        if ext is None:
            xn = k.pool("xn", [128, 2048], BF16, 2)[:]
        else:
            xn = ext[1][tt % len(ext[1])] if isinstance(ext[1], list) else ext[1]
        k.act(xn, xt, AF.Square, accum_out=ss[:])
        rstd = k.pool("rstd", [128, 1], F32, 2)
        ms = k.pool("ms", [128, 1], F32, 2)
        k.ts(ms[:], ss[:], 1.0 / 2048, ALU.mult, EPS, ALU.add)
        k.act(rstd[:], ms[:], AF.Ln)
        k.act(rstd[:], rstd[:], AF.Exp, scale=-0.5)
        k.stt(xn, xt, rstd[:, 0:1], grep, ALU.mult, ALU.mult)
        return xn

    def trans(tt, xn):
        xnk = _key(xn)
        for f4 in range(4):
            ps = k.psum()
            pb = ps[:].bitcast(BF16)
            for j in range(4):
                ft = f4 * 4 + j
                k.tr(V(pb[:, j * 128:(j + 1) * 128], ps.name), V(_ap(xn)[:, ft * 128:(ft + 1) * 128], xnk), ident[:])
            src = V(pb[:, 0:512].rearrange("p (j t) -> p j t", j=4), ps.name)
            dst = V(hT[:, f4 * 4:(f4 + 1) * 4, tt * 128:(tt + 1) * 128], "hT%d" % tt)
            k.copy(dst, src, eng=('act' if f4 % 2 == 0 else 'dve'))

    prev = None
    for tt in range(NT):
        xn = stats(tt)
        if prev is not None:
            trans(*prev)
        prev = (tt, xn)
    trans(*prev)
    return hT


def epsb(k):
    if not hasattr(k, "_epsb"):
        k._epsb = k.sb("epsb", [128, 1], F32)
        k.memset(k._epsb[:], EPS)
    return k._epsb


def consts(k):
    idf = k.din("c_ident", [128, 128], F32)
    ident = k.sb("ident", [128, 128], BF16)
    k.dma(ident[:], idf, q='pool')
    ones = k.sb("ones", [128, 128], BF16)
    k.memset(ones[:], 1.0)
    return ident, ones


def rstd_from_ss(k, ps_ss, n, rows=128):
    r = k.pool("rs", [128, 512], F32, 2)
    k.act(r[0:rows, :], ps_ss[0:rows, :], AF.Ln, bias=epsb(k)[0:rows, 0:1], scale=1.0 / n)
    k.act(r[0:rows, :], r[0:rows, :], AF.Exp, scale=-0.5)
    return r


def proj_block(k, wt, wc0, nb, hT, half, nk=16):
    ps = k.psum()
    for kt in range(nk):
        k.mm(ps[0:nb, :], wt[:, kt, wc0:wc0 + nb], V(hT[:, kt, half * 512:(half + 1) * 512], hk(half)),
             start=(kt == 0), stop=(kt == nk - 1))
    return ps


def colvec(k, name, dram_row, n, col0=0):
    t = k.sb(name, [128, 1], F32)
    k.dma(t[0:n, :], dram_row[:, col0:col0 + n].rearrange("o f -> f o"))
    return t


def build_P(upto=99.0):
    k = K()
    x = k.din("x", [T, 2048])
    w_in = k.din("w_p", [2048, PW])
    norm_g = k.din("norm_g", [1, 2048])
    a_qn_g = k.din("a_qn_g", [1, 128])
    a_kn_g = k.din("a_kn_g", [1, 128])
    b_cq_g = k.din("b_cq_g", [1, 448])
    b_ckv_g = k.din("b_ckv_g", [1, 128])
    b_w_uq = k.din("b_w_uq", [448, 768])
    b_w_ukv = k.din("b_w_ukv", [128, 1024])
    b_qn_g = k.din("b_qn_g", [1, 192])
    b_kn_g = k.din("b_kn_g", [1, 192])
    pos = k.din("pos", [1, T], I32)
    invf = k.din("c_invf", [64, 1])
    sgn = k.din("c_sgn", [64, 1])

    qa = k.dout("qa", [12, 128, T], BF16)
    ka = k.dout("ka", [12, 128, T], BF16)
    va = k.dout("va", [T, 1536], BF16)
    qb = k.dout("qb", [4, 192, T], BF16)
    kb = k.dout("kb", [4, 192, T], BF16)
    vb = k.dout("vb", [T, 512], BF16)
    cqk = k.dout("cqk", [512, T], F32)
    cv = k.dout("cv", [T, 512], BF16)
    cif = k.dout("cif", [8, T], F32)
    du = k.dout("du", [512, T], BF16)

    ident, ones = consts(k)
    if upto <= 1:
        k.dbg = k.dout("dbg", [128, 16])
        k.dbg2 = k.dout("dbg2", [128, 2048])
    cq_t = k.pool("cqraw", [128, 4, 512], F32, 1)
    xt_a = k.pool("xt", [128, 2048], F32, 1)[:]
    xt_b = V(cq_t[:].rearrange("p a b -> p (a b)"), ("cqraw0", "cqraw1", "cqraw2", "cqraw3"))
    xn_l = [k.pool("xn", [128, 2048], BF16, 2)[:], k.pool("xn", [128, 2048], BF16, 2)[:]]
    hT = norm_hT(k, x, norm_g, ident, ext=([xt_a, xt_b], xn_l, None))
    if upto <= 1:
        hf = k.sb("hf", [128, 2048], F32)
        k.copy(hf[:].rearrange("p (a b) -> p a b", a=16), V(hT[:, :, 0:128], "hT0"))
        k.dma(k.dbg2, hf[:])

    if upto <= 1:
        return k.finish()
    invf_sb = k.sb("invf_sb", [64, 1], F32)
    k.dma(invf_sb[:], invf)
    sgn_sb = k.sb("sgn_sb", [64, 1], F32)
    k.dma(sgn_sb[:], sgn)
    cosT = k.sb("cosT", [64, T], F32)
    sinS = k.sb("sinS", [64, T], F32)
    posi = k.sb("posi", [64, 512], I32)
    posf = k.sb("posf", [64, 512], F32)
    ang = k.sb("ang", [64, 512], F32)
    for half in range(NH):
        hs_ = slice(half * 512, (half + 1) * 512)
        k.dma(posi[:], pos[:, hs_].partition_broadcast(64))
        k.copy(posf[:], posi[:])
        k.ts(ang[:], posf[:], invf_sb[:, 0:1], ALU.mult)
        trig(k, V(cosT[:, hs_], "cosT%d" % half), ang, 64, 512, math.pi / 2, posf, posi)
        trig(k, V(sinS[:, hs_], "sinS%d" % half), ang, 64, 512, 0.0, posf, posi)
        k.ts(V(sinS[:, hs_], "sinS%d" % half), V(sinS[:, hs_], "sinS%d" % half), sgn_sb[:, 0:1], ALU.mult)

    if upto <= 2:
        return k.finish()
    gq = colvec(k, "gq", a_qn_g, 128)
    gk = colvec(k, "gk", a_kn_g, 128)
    for (nm, outd, g) in (("a_q", qa, gq), ("a_k", ka, gk)):
        for cg in range(3):
            wt = load_w(k, w_in, POFF[nm] + cg * 512, 512)
            for b in range(4):
                hd = cg * 4 + b
                for half in range(NH):
                    ps = proj_block(k, wt, b * 128, 128, hT, half)
                    sq = k.pool("sq", [128, 512], BF16, 3)
                    k.act(sq[:], ps[:], AF.Square)
                    ps2 = k.psum()
                    k.mm(ps2[:], ones[:], sq[:])
                    rs = rstd_from_ss(k, ps2, 128)
                    ob = k.pool("ob", [128, 512], BF16, 4)
                    k.stt(ob[:], ps[:], g[:, 0:1], rs[:], ALU.mult, ALU.mult)
                    k.dma(V(outd[hd, :, half * 512:(half + 1) * 512], uk("o")), ob[:])

    if upto <= 3:
        return k.finish()
    for (nm, outd, ncg) in (("a_v", va, 3), ("c_v", cv, 1)):
        for cg in range(ncg):
            wt = load_w(k, w_in, POFF[nm] + cg * 512, 512)
            for tt in range(T // 128):
                ps = k.psum()
                for kt in range(16):
                    k.mm(ps[:], V(hT[:, kt, tt * 128:(tt + 1) * 128], "hT%d" % tt), wt[:, kt, 0:512], start=(kt == 0), stop=(kt == 15))
                ob = k.pool("ob", [128, 512], BF16, 4)
                k.copy(ob[:], ps[:], eng=('act' if tt % 2 else 'dve'))
                k.dma(V(outd[tt * 128:(tt + 1) * 128, cg * 512:(cg + 1) * 512], uk("o")), ob[:])

    if upto <= 4:
        return k.finish()
    for (nm, outd, dt_) in (("c_qk", cqk, F32), ("d_u", du, BF16)):
        wt = load_w(k, w_in, POFF[nm], 512)
        for b in range(4):
            for half in range(NH):
                ps = proj_block(k, wt, b * 128, 128, hT, half)
                if dt_ == F32:
                    ob = k.pool("obf", [128, 512], F32, 2)
                else:
                    ob = k.pool("ob", [128, 512], BF16, 4)
                k.copy(ob[:], ps[:], eng=('act' if half else 'dve'))
                k.dma(V(outd[b * 128:(b + 1) * 128, half * 512:(half + 1) * 512], uk("o")), ob[:])
    wt = load_w(k, w_in, POFF["c_i"], 8)
    for half in range(NH):
        ps = proj_block(k, wt, 0, 8, hT, half)
        ob = k.pool("obf", [128, 512], F32, 2)
        k.copy(ob[0:8, :], ps[0:8, :])
        k.dma(V(cif[:, half * 512:(half + 1) * 512], uk("o")), ob[0:8, :])

    if upto <= 5:
        return k.finish()
    wtq = load_w(k, w_in, POFF["b_cq"], 448)
    wtk = load_w(k, w_in, POFF["b_ckv"], 192)
    wsw = k.sb("wsw", [128, 16, 64], BF16)
    k.dma(V(wsw[:, :, 0:32], "wsw_a"), w_in[:, POFF["b_kr"] + 32:POFF["b_kr"] + 64].rearrange("(kt p) c -> p kt c", p=128), q='pool')
    k.dma(V(wsw[:, :, 32:64], "wsw_b"), w_in[:, POFF["b_kr"]:POFF["b_kr"] + 32].rearrange("(kt p) c -> p kt c", p=128), q='pool')
    WSW = [V(wsw[:, kt, :], "wsw_a") for kt in range(16)]
    wuq = k.sb("wuq", [128, 4, 768], BF16)
    for blk in range(4):
        r = 128 if blk < 3 else 64
        k.dma(V(wuq[0:r, blk, :], "wuq%d" % blk), b_w_uq[blk * 128:blk * 128 + r, :], q='pool')
    wuqs = k.sb("wuqs", [128, 4, 4, 64], BF16)
    b_w_uq_h = b_w_uq.rearrange("k (h d) -> k h d", h=4)
    for blk in range(4):
        r = 128 if blk < 3 else 64
        k.dma(V(wuqs[0:r, blk, :, 0:32], "wuqs%da" % blk), b_w_uq_h[blk * 128:blk * 128 + r, :, 160:192], q='pool')
        k.dma(V(wuqs[0:r, blk, :, 32:64], "wuqs%db" % blk), b_w_uq_h[blk * 128:blk * 128 + r, :, 128:160], q='pool')
    wukv = k.sb("wukv", [128, 1024], BF16)
    k.dma(wukv[:], b_w_ukv, q='pool')
    gcq = k.sb("gcq", [128, 4], F32)
    for blk in range(4):
        r = 128 if blk < 3 else 64
        k.dma(V(gcq[0:r, blk:blk + 1], "gcq%d" % blk), b_cq_g[:, blk * 128:blk * 128 + r].rearrange("o f -> f o"))
    gckv = colvec(k, "gckv", b_ckv_g, 128)
    gqn = colvec(k, "gqn", b_qn_g, 128)
    gkn = colvec(k, "gkn", b_kn_g, 128)
    gqr = colvec(k, "gqr", b_qn_g, 64, 128)
    gkr = colvec(k, "gkr", b_kn_g, 64, 128)
    gqrs = k.sb("gqrs", [128, 1], F32)
    k.dma(V(gqrs[0:32, :], "gqrs_a"), b_qn_g[:, 160:192].rearrange("o f -> f o"))
    k.dma(V(gqrs[32:64, :], "gqrs_b"), b_qn_g[:, 128:160].rearrange("o f -> f o"))
    gkrs = k.sb("gkrs", [128, 1], F32)
    k.dma(V(gkrs[0:32, :], "gkrs_a"), b_kn_g[:, 160:192].rearrange("o f -> f o"))
    k.dma(V(gkrs[32:64, :], "gkrs_b"), b_kn_g[:, 128:160].rearrange("o f -> f o"))
    GQRS = V(gqrs[0:64, 0:1], ("gqrs_a", "gqrs_b"))
    GKRS = V(gkrs[0:64, 0:1], ("gkrs_a", "gkrs_b"))

    if upto <= 6:
        return k.finish()
    for half in range(NH):
        hs = slice(half * 512, (half + 1) * 512)
        cqraw = k.pool("cqraw", [128, 4, 512], F32, 1)
        sqa = k.pool("sqa", [128, 512], F32, 1)
        if not hasattr(k, "_sq3"):
            k._sq3 = k.sb("sq3", [128, 512], F32)
            k.memset(k._sq3[:], 0.0)
        for blk in range(4):
            r = 128 if blk < 3 else 64
            ps = proj_block(k, wtq, blk * 128, r, hT, half)
            k.copy(V(cqraw[0:r, blk, :], "cqraw%d" % blk), ps[0:r, :], eng='dve')
            craw = V(cqraw[0:r, blk, :], "cqraw%d" % blk)
            if blk == 0:
                k.act(sqa[:], craw, AF.Square)
            elif blk < 3:
                sqt = k.pool("sqt", [128, 512], F32, 2)
                k.act(sqt[:], craw, AF.Square)
                k.tt(sqa[:], sqa[:], sqt[:], ALU.add)
            else:
                k.act(k._sq3[0:64, :], craw, AF.Square)
                sq = k.pool("sq", [128, 512], BF16, 3)
                k.tt(sq[:], sqa[:], k._sq3[:], ALU.add)
        ps_ss = k.psum()
        k.mm(ps_ss[:], ones[:], sq[:])
        if upto <= 6.6:
            continue
        rs = rstd_from_ss(k, ps_ss, 448)
        cqn = k.pool("cqn", [128, 4, 512], BF16, 1)
        for blk in range(4):
            r = 128 if blk < 3 else 64
            k.stt(V(cqn[0:r, blk, :], "cqn%d" % blk), V(cqraw[0:r, blk, :], "cqraw%d" % blk),
                  V(gcq[0:r, blk:blk + 1], "gcq%d" % blk), rs[0:r, :], ALU.mult, ALU.mult)
        if upto <= 7:
            continue
        for hh in range(4):
            psn = k.psum()
            psr = k.psum()
            pss = k.psum()
            for blk in range(4):
                r = 128 if blk < 3 else 64
                rhs = V(cqn[0:r, blk, :], "cqn%d" % blk)
                k.mm(psn[:], V(wuq[0:r, blk, hh * 192:hh * 192 + 128], "wuq%d" % blk), rhs, start=(blk == 0), stop=(blk == 3))
            for blk in range(4):
                r = 128 if blk < 3 else 64
                rhs = V(cqn[0:r, blk, :], "cqn%d" % blk)
                k.mm(psr[0:64, :], V(wuq[0:r, blk, hh * 192 + 128:hh * 192 + 192], "wuq%d" % blk), rhs, start=(blk == 0), stop=(blk == 3))
            for blk in range(4):
                r = 128 if blk < 3 else 64
                rhs = V(cqn[0:r, blk, :], "cqn%d" % blk)
                k.S.op('pe', (lambda e, o=pss[0:64, :], l=wuqs[0:r, blk, hh, :], rr=_ap(rhs), s=(blk == 0), t=(blk == 3):
                              e.matmul(o, lhsT=l, rhs=rr, start=s, stop=t)),
                       reads=["wuqs%da" % blk, "wuqs%db" % blk, rhs.key] + ([] if blk == 0 else [pss.name]), writes=[pss.name])
            sq = k.pool("sq", [128, 512], BF16, 3)
            k.act(sq[:], psn[:], AF.Square)
            sq2 = k.pool("sq", [128, 512], BF16, 3)
            k.act(sq2[0:64, :], psr[0:64, :], AF.Square)
            ps2 = k.psum()
            k.mm(ps2[:], ones[:], sq[:], start=True, stop=False)
            k.mm(ps2[:], ones[0:64, :], sq2[0:64, :], start=False, stop=True)
            rq = rstd_from_ss(k, ps2, 192)
            ob = k.pool("ob", [128, 512], BF16, 4)
            k.stt(ob[:], psn[:], gqn[:, 0:1], rq[:], ALU.mult, ALU.mult)
            k.dma(V(qb[hh, 0:128, hs], uk("o")), ob[:])
            t1 = k.pool("t1", [64, 512], F32, 1)
            k.stt(t1[:], psr[0:64, :], gqr[0:64, 0:1], V(cosT[:, hs], "cosT%d" % half), ALU.mult, ALU.mult)
            t2 = k.pool("t2", [64, 512], F32, 1)
            k.stt(t2[:], pss[0:64, :], GQRS, V(sinS[:, hs], "sinS%d" % half), ALU.mult, ALU.mult)
            k.tt(t1[:], t1[:], t2[:], ALU.add)
            ob2 = k.pool("ob", [128, 512], BF16, 4)
            k.tt(ob2[0:64, :], t1[:], rq[0:64, :], ALU.mult)
            k.dma(V(qb[hh, 128:192, hs], uk("o")), ob2[0:64, :])
        if upto <= 8:
            continue
        ps = proj_block(k, wtk, 0, 128, hT, half)
        ckraw = k.pool("ckraw", [128, 512], F32, 1)
        k.copy(ckraw[:], ps[:], eng='dve')
        sq = k.pool("sq", [128, 512], BF16, 3)
        k.act(sq[:], ps[:], AF.Square)
        ps2 = k.psum()
        k.mm(ps2[:], ones[:], sq[:])
        rs = rstd_from_ss(k, ps2, 128)
        ckvn = k.pool("ckvn", [128, 512], BF16, 1)
        k.stt(ckvn[:], ckraw[:], gckv[:, 0:1], rs[:], ALU.mult, ALU.mult)
        if upto <= 9:
            continue
        wv = wukv[:].rearrange("p (h d) -> p h d", h=4)[:, :, 128:256]
        for t4 in range(4):
            psv = k.psum()
            k.mm(psv[:].rearrange("p (h d) -> p h d", h=4), ckvn[:, t4 * 128:(t4 + 1) * 128], wv)
            ob = k.pool("ob", [128, 512], BF16, 4)
            k.copy(ob[:], psv[:], eng='act')
            tok0 = half * 512 + t4 * 128
            k.dma(V(vb[tok0:tok0 + 128, :], uk("o")), ob[:])
        if upto <= 10:
            continue
        pkr = proj_block(k, wtk, 128, 64, hT, half)
        pks = k.psum()
        for kt in range(16):
            k.S.op('pe', (lambda e, o=pks[0:64, :], l=wsw[:, kt, :], rr=hT[:, kt, hs], s=(kt == 0), t=(kt == 15):
                          e.matmul(o, lhsT=l, rhs=rr, start=s, stop=t)),
                   reads=["wsw_a", "wsw_b"] + list(hk(half)) + ([] if kt == 0 else [pks.name]), writes=[pks.name])
        sqr = k.pool("sqr", [64, 512], BF16, 2)
        k.act(sqr[:], pkr[0:64, :], AF.Square)
        kro = k.pool("kro", [64, 512], F32, 1)
        k.stt(kro[:], pkr[0:64, :], gkr[0:64, 0:1], V(cosT[:, hs], "cosT%d" % half), ALU.mult, ALU.mult)
        t2 = k.pool("t2", [64, 512], F32, 1)
        k.stt(t2[:], pks[0:64, :], GKRS, V(sinS[:, hs], "sinS%d" % half), ALU.mult, ALU.mult)
        k.tt(kro[:], kro[:], t2[:], ALU.add)
        if upto <= 11:
            continue
        for hh in range(4):
            pkn = k.psum()
            k.mm(pkn[:], wukv[:, hh * 256:hh * 256 + 128], ckvn[:])
            sq = k.pool("sq", [128, 512], BF16, 3)
            k.act(sq[:], pkn[:], AF.Square)
            ps2 = k.psum()
            k.mm(ps2[:], ones[:], sq[:], start=True, stop=False)
            k.mm(ps2[:], ones[0:64, :], sqr[:], start=False, stop=True)
            rk = rstd_from_ss(k, ps2, 192)
            ob = k.pool("ob", [128, 512], BF16, 4)
            k.stt(ob[:], pkn[:], gkn[:, 0:1], rk[:], ALU.mult, ALU.mult)
            k.dma(V(kb[hh, 0:128, hs], uk("o")), ob[:])
            ob2 = k.pool("ob", [128, 512], BF16, 4)
            k.tt(ob2[0:64, :], kro[:], rk[0:64, :], ALU.mult)
            k.dma(V(kb[hh, 128:192, hs], uk("o")), ob2[0:64, :])
    return k.finish()


def trig(k, out, ang, P, N, shift, a, ki):
    TWO_PI = 2.0 * math.pi
    HI = 6.28125
    LO = TWO_PI - HI
    k.ts(a[:], ang[:], shift, ALU.add)
    kf = k.pool("tg_k", [P, N], F32, 1)
    k.ts(kf[:], a[:], 1.0 / TWO_PI, ALU.mult)
    k.copy(ki[:], kf[:])
    k.copy(kf[:], ki[:])
    k.stt(a[:], kf[:], -HI, a[:], ALU.mult, ALU.add)
    k.stt(a[:], kf[:], -LO, a[:], ALU.mult, ALU.add)
    m = k.pool("tg_m", [P, N], F32, 1)
    k.ts(m[:], a[:], math.pi, ALU.is_gt, -TWO_PI, ALU.mult)
    k.tt(a[:], a[:], m[:], ALU.add)
    k.ts(m[:], a[:], -math.pi, ALU.is_lt, TWO_PI, ALU.mult)
    k.tt(a[:], a[:], m[:], ALU.add)
    k.ts(a[:], a[:], math.pi, ALU.min, -math.pi, ALU.max)
    k.act(out if isinstance(out, V) else out[:], a[:], AF.Sin)


import math

S = 8192
A_DIL = (1, 4, 16)
A_SCALE = 128 ** -0.5
B_SCALE = 192 ** -0.5
NEG = -30000.0
A_BANKS_S = (0, 4)
A_BANKS_O = (4, 8)
B_BANKS_S = (4, 8)


def sst(start, count, step):
    return slice(start, start + step * (count - 1) + 1, step)


def host_biasA(core):
    slopes = 2.0 ** (-8.0 * np.arange(1, 13, dtype=np.float64) / 12)
    kk = np.arange(128)[:, None]
    qq = np.arange(128)[None, :]
    out = np.zeros((12, 3, 128, 128), np.float32)
    for gh in range(12):
        d = A_DIL[gh // 4]
        sl = slopes[gh]
        delta0 = qq - kk + 128
        b0 = np.where(kk >= qq, -sl * d * delta0, NEG * A_SCALE)
        delta1 = qq - kk
        b1 = np.where(kk <= qq, -sl * d * delta1, NEG * A_SCALE)
        out[gh, 0] = b0 / A_SCALE
        out[gh, 1] = (b0 / A_SCALE) if core > 0 else NEG
        out[gh, 2] = b1 / A_SCALE
    return np.maximum(out, NEG).astype(np.float32)


def host_maskB(par):
    out = np.zeros((8, 128, 512), np.float32)
    kk = np.arange(128)[:, None]
    qq = np.arange(512)[None, :]
    for i in range(8):
        o = i - 4 * par
        if o < 0:
            out[i] = 0.0
        elif o > 3:
            out[i] = NEG
        else:
            out[i] = np.where(128 * o + kk <= qq, 0.0, NEG)
    return out


def mixer_A(k, ident, ones):
    qa = k.din("qa", [12, 128, T], BF16)
    kax = k.din("ka_ext", [12, 128, 2048 + T], BF16)
    vax = k.din("va_ext", [2048 + T, 1536], BF16)
    biasd = k.din("c_biasA", [12, 3, 128, 128], F32)
    ya = k.dout("ya", [512, T], F32)

    bias = k.sb("biasA", [128, 36, 128], BF16)
    for gh in range(12):
        k.dma(V(bias[:, gh * 3:(gh + 1) * 3, :], "biasA%d" % gh), biasd[gh].rearrange("t k q -> k t q"), q='pool')
    accN = [k.sb("accN%d" % h, [128, T], F32) for h in range(4)]
    accD = [k.sb("accD%d" % h, [128, T], F32) for h in range(4)]
    pend = []

    def part2(ctx):
        vt, vkeys, t0, t1, P, Bq, g, h, c0, d = ctx
        psO = k.psum(*A_BANKS_O)
        k.mm(psO[:, 0:Bq], V(vt[:, t0, :], vkeys), P[:, 0:Bq], start=True, stop=False)
        k.mm(psO[:, 0:Bq], V(vt[0:Bq, t1, :], vkeys), P[0:Bq, 128:128 + Bq], start=False, stop=True)
        k.mm(psO[:, 128:128 + Bq], ones[:], P[:, 0:Bq], start=True, stop=False)
        k.mm(psO[:, 128:128 + Bq], ones[0:Bq, :], P[0:Bq, 128:128 + Bq], start=False, stop=True)
        cs = sst(c0, Bq, d)
        if g == 0:
            k.copy(accN[h][:, cs], psO[:, 0:Bq], eng='dve')
            k.copy(accD[h][:, cs], psO[:, 128:128 + Bq], eng='dve')
        else:
            k.tt(accN[h][:, cs], accN[h][:, cs], psO[:, 0:Bq], ALU.add)
            k.tt(accD[h][:, cs], accD[h][:, cs], psO[:, 128:128 + Bq], ALU.add)
    for gh in range(12):
        g, h = gh // 4, gh % 4
        d = A_DIL[g]
        Bq = 128 if d < 16 else 64
        nq = T // d
        nblk = nq // Bq
        qs = k.pool("qA", [128, T], BF16, 2)
        k.dma(qs[:], qa[gh])
        kx = k.pool("kA", [128, 2048 + T], BF16, 2)
        k.dma(kx[:], kax[gh])
        for r in range(d):
            nkeys = 128 + nq
            nt_full = nkeys // 128
            rem = nkeys - nt_full * 128
            vt = k.pool("vA", [128, 9, 128], BF16, 4)
            e0 = 2048 + r - 128 * d
            src = vax[sst(e0, 128 * nt_full, d), gh * 128:(gh + 1) * 128].rearrange("(t k) c -> k t c", k=128)
            vkeys = ["%s_a" % vt.name]
            k.dma(V(vt[:, 0:nt_full, :], vkeys[0]), src)
            if rem:
                e1 = e0 + d * 128 * nt_full
                vkeys.append("%s_b" % vt.name)
                k.dma(V(vt[0:rem, nt_full, :], vkeys[1]), vax[sst(e1, rem, d), gh * 128:(gh + 1) * 128])
            for blk in range(nblk):
                i0 = blk * Bq
                q_ap = qs[:, sst(r + d * i0, Bq, d)]
                ek0 = 2048 + r + d * (i0 - 128)
                k0 = kx[:, sst(ek0, 128, d)]
                ek1 = 2048 + r + d * i0
                k1 = kx[:, sst(ek1, Bq, d)]
                btype = 1 if i0 == 0 else 0
                bkey = "biasA%d" % gh
                psS = k.psum(*A_BANKS_S)
                k.mm(psS[:, 0:Bq], k0, q_ap, start=True, stop=False)
                k.mm(psS[:, 0:Bq], ident[:], V(bias[:, gh * 3 + btype, 0:Bq], bkey), start=False, stop=True)
                k.mm(psS[0:Bq, 128:128 + Bq], k1, q_ap, start=True, stop=False)
                k.mm(psS[0:Bq, 128:128 + Bq], ident[0:Bq, 0:Bq], V(bias[0:Bq, gh * 3 + 2, 0:Bq], bkey), start=False, stop=True)
                P = k.pool("PA", [128, 256], BF16, 4)
                if Bq == 128:
                    k.act(P[:], psS[:, 0:256], AF.Exp, scale=A_SCALE)
                else:
                    k.act(P[:, 0:Bq], psS[:, 0:Bq], AF.Exp, scale=A_SCALE)
                    k.act(P[0:Bq, 128:128 + Bq], psS[0:Bq, 128:128 + Bq], AF.Exp, scale=A_SCALE)
                t0, t1 = blk, blk + 1
                pend.append((vt, vkeys, t0, t1, P, Bq, g, h, r + d * i0, d))
                if len(pend) > 1:
                    part2(pend.pop(0))
                yield
    while pend:
        part2(pend.pop(0))
    for h in range(4):
        k.recip(accD[h][:], accD[h][:])
        k.tt(accN[h][:], accN[h][:], accD[h][:], ALU.mult)
        k.dma(V(ya[h * 128:(h + 1) * 128, :], uk("o")), accN[h][:])


def mixer_B(k, ident, ones):
    qb = k.din("qb_h", [192, 4096], BF16)
    kb = k.din("kb_h", [192, S], BF16)
    vb = k.din("vb_h", [S, 128], BF16)
    maskd = k.din("c_maskB", [8, 128, 512], F32)
    yb = k.dout("yb", [128, 4096], F32)

    mask = k.sb("maskB", [128, 8, 512], BF16)
    k.dma(mask[:], maskd.rearrange("i k q -> k i q"), q='pool')
    kn = k.sb("kbn", [128, S], BF16)
    kr = k.sb("kbr", [64, S], BF16)
    for c4 in range(4):
        cs = slice(c4 * 2048, (c4 + 1) * 2048)
        k.dma(V(kn[:, cs], "kbn%d" % c4), kb[0:128, cs])
        k.dma(V(kr[:, cs], "kbr%d" % c4), kb[128:192, cs])
    vt = k.sb("vbt", [128, 64, 128], BF16)
    for c4 in range(4):
        k.dma(V(vt[:, c4 * 16:(c4 + 1) * 16, :], "vbt%d" % c4),
              vb[c4 * 2048:(c4 + 1) * 2048, :].rearrange("(t k) c -> k t c", k=128))
    for j in range(8):
        qn = k.pool("qbn", [128, 512], BF16, 2)
        qr = k.pool("qbr", [64, 512], BF16, 2)
        k.dma(qn[:], qb[0:128, j * 512:(j + 1) * 512])
        k.dma(qr[:], qb[128:192, j * 512:(j + 1) * 512])
        nkb = 8 * j + 8
        psO = k.bank(j % 2)
        psD = k.bank(2 + j % 2)

        def pv(b, P):
            c4_ = b // 16
            k.mm(psO[:], V(vt[:, b, :], "vbt%d" % c4_), P[:], start=(b == 0), stop=(b == nkb - 1))
            k.mm(psD[:], ones[:], P[:], start=(b == 0), stop=(b == nkb - 1))
        prev = None
        for b in range(nkb):
            c4 = b // 16
            psS = k.psum(*B_BANKS_S)
            last8 = b >= nkb - 8
            k.mm(psS[:], V(kn[:, b * 128:(b + 1) * 128], "kbn%d" % c4), qn[:], start=True, stop=False)
            k.mm(psS[:], V(kr[:, b * 128:(b + 1) * 128], "kbr%d" % c4), qr[:], start=False, stop=(not last8))
            if last8:
                k.mm(psS[:], ident[:], mask[:, b - (nkb - 8), :], start=False, stop=True)
            P = k.pool("PB", [128, 512], BF16, 4)
            k.act(P[:], psS[:], AF.Exp, scale=B_SCALE)
            if prev is not None:
                pv(*prev)
            prev = (b, P)
            yield
        pv(*prev)
        rd = k.pool("rdB", [128, 512], F32, 2)
        k.recip(rd[:], psD[:])
        ob = k.pool("obB", [128, 512], F32, 2)
        k.tt(ob[:], psO[:], rd[:], ALU.mult)
        k.dma(V(yb[:, j * 512:(j + 1) * 512], uk("o")), ob[:])


def build_M(parts="ABCD"):
    k = K()
    idf = k.din("c_ident", [128, 128], F32)
    ident = k.sb("ident", [128, 128], BF16)
    k.dma(ident[:], idf, q='pool')
    ones = k.sb("ones", [128, 128], BF16)
    k.memset(ones[:], 1.0)
    global A_BANKS_S, A_BANKS_O, B_BANKS_S
    if "A" in parts and "B" in parts:
        A_BANKS_S, A_BANKS_O, B_BANKS_S = (6, 7), (7, 8), (4, 6)
        ga = mixer_A(k, ident, ones)
        gb = mixer_B(k, ident, ones)
        done_a = done_b = False
        while not (done_a and done_b):
            for _ in range(2):
                if not done_b:
                    try:
                        next(gb)
                    except StopIteration:
                        done_b = True
            if not done_a:
                try:
                    next(ga)
                except StopIteration:
                    done_a = True
    else:
        if "A" in parts:
            A_BANKS_S, A_BANKS_O = (0, 4), (4, 8)
            for _ in mixer_A(k, ident, ones):
                pass
        if "B" in parts:
            B_BANKS_S = (4, 8)
            for _ in mixer_B(k, ident, ones):
                pass
    if "C" in parts:
        mixer_C(k, ident, ones)
    if "D" in parts:
        mixer_D(k, ident, ones)
    return k.finish()


S = 8192
NCH = 128
L = 64


def host_constsC():
    tri = (np.arange(128)[:, None] < np.arange(128)[None, :]).astype(np.float32)
    mask = (np.arange(64)[:, None] <= np.arange(64)[None, :]).astype(np.float32)
    return {"c_tri": tri, "c_maskC": mask, "c_identf": np.eye(128, dtype=np.float32)}


def mixer_C(k, ident, ones):
    cq = k.din("cq_h", [64, 3 + S], F32)
    ck = k.din("ck_h", [64, 3 + S], F32)
    cwq = k.din("cw_q", [64, 4], F32)
    cbq = k.din("cb_q", [64, 1], F32)
    cwk = k.din("cw_k", [64, 4], F32)
    cbk = k.din("cb_k", [64, 1], F32)
    cv = k.din("cv_h", [S, 64], BF16)
    gi = k.din("gi", [128, 64], F32)
    gf = k.din("gf", [128, 64], F32)
    gb = k.din("gb", [1, 2], F32)
    trid = k.din("c_tri", [128, 128], F32)
    maskd = k.din("c_maskC", [64, 64], F32)
    identd = k.din("c_identf", [128, 128], F32)
    hc = k.dout("hc", [S, 64], F32)

    identf = k.sb("identf", [128, 128], F32)
    k.dma(identf[:], identd)
    tri = k.sb("tri", [128, 128], F32)
    k.dma(tri[:], trid)
    maskC = k.sb("maskC", [64, 64], F32)
    k.dma(maskC[:], maskd)
    onesf = k.sb("onesf", [128, 64], F32)
    k.memset(onesf[:], 1.0)

    qT = k.sb("qT", [64, S], BF16)
    kT = k.sb("kT", [64, S], BF16)
    CH = 2048
    for (src, wd, bd, dst, scl, nm) in ((cq, cwq, cbq, qT, 1.0, "q"), (ck, cwk, cbk, kT, 0.125, "k")):
        w = k.sb("cw" + nm, [64, 4], F32)
        k.dma(w[:], wd)
        b = k.sb("cb" + nm, [64, 1], F32)
        k.dma(b[:], bd)
        for ci in range(S // CH):
            xt = k.pool("cx", [64, CH + 3], F32, 2)
            k.dma(xt[:], src[:, ci * CH:ci * CH + CH + 3])
            acc = k.pool("cacc", [64, CH], F32, 2)
            k.ts(acc[:], xt[:, 0:CH], w[:, 0:1], ALU.mult)
            for j in range(1, 4):
                k.stt(acc[:], xt[:, j:j + CH], w[:, j:j + 1], acc[:], ALU.mult, ALU.add)
            dkey = V(dst[:, ci * CH:(ci + 1) * CH], "%sT%d" % (nm, ci))
            if scl == 1.0:
                k.act(dkey, acc[:], AF.Silu, bias=b[:, 0:1])
            else:
                k.act(acc[:], acc[:], AF.Silu, bias=b[:, 0:1])
                k.ts(dkey, acc[:], scl, ALU.mult, eng='pool')

    def tk(nm, t0):
        return "%sT%d" % (nm, t0 // CH)

    vx = k.sb("vx", [64, NCH, 65], BF16)
    for c4 in range(4):
        k.dma(V(vx[:, c4 * 32:(c4 + 1) * 32, 0:64], "vx%d" % c4),
              cv[c4 * 2048:(c4 + 1) * 2048, :].rearrange("(c s) d -> s c d", s=64))
    k.memset(V(vx[:, :, 64:65], "vx1"), 1.0, eng='pool')

    def vxk(c):
        return ("vx%d" % (c // 32), "vx1")

    def g(name, cols=64):
        return k.sb(name, [128, cols], F32)
    gbb = g("gbb", 2)
    k.dma(gbb[:], gb.partition_broadcast(128))
    ngb = g("ngb", 2)
    k.ts(ngb[:], gbb[:], -1.0, ALU.mult)
    gi_s = g("gi_s")
    gf_s = g("gf_s")
    k.dma(gi_s[:], gi)
    k.dma(gf_s[:], gf)
    e1 = g("e1")
    k.act(e1[:], gf_s[:], AF.Exp, bias=ngb[:, 1:2], scale=-1.0)
    k.act(e1[:], e1[:], AF.Ln, bias=1.0)
    lf = g("lf")
    k.ts(lf[:], e1[:], -1.0, ALU.mult)
    bloc = g("bloc")
    k.scan(bloc[:], onesf[:], lf[:], 0.0, ALU.mult, ALU.add)
    psb = k.psum()
    k.mm(psb[:, 0:1], tri[:], bloc[:, 63:64])
    bst = g("bst", 1)
    k.copy(bst[:], psb[:, 0:1])
    Bg = g("Bg")
    k.ts(Bg[:], bloc[:], bst[:, 0:1], ALU.add)
    a = g("a")
    k.ts(a[:], gi_s[:], gbb[:, 0:1], ALU.add)
    k.tt(a[:], a[:], Bg[:], ALU.subtract)
    cm = g("cm")
    k.scan(cm[:], a[:], a[:], -1.0e30, ALU.max, ALU.max)
    pst = k.psum()
    k.tr(pst[0:1, 0:128], cm[:, 63:64], identf[:])
    crow = k.sb("crow", [1, 128], F32)
    k.copy(crow[:], pst[0:1, 0:128])
    zrow = k.sb("zrow", [1, 128], F32)
    k.memset(zrow[:], 0.0)
    rinc = k.sb("rinc", [1, 128], F32)
    k.scan(rinc[:], crow[:], zrow[:], 0.0, ALU.max, ALU.max)
    rexc = k.sb("rexc", [1, 128], F32)
    k.memset(V(rexc[:, 0:1], "rexc"), 0.0)
    k.copy(V(rexc[:, 1:128], "rexc"), rinc[:, 0:127])
    Rc = g("Rc", 1)
    Rn = g("Rn", 1)
    for (row, col) in ((rexc, Rc), (rinc, Rn)):
        p2 = k.psum()
        k.tr(p2[:, 0:1], V(row[0:1, :], row.name), identf[0:1, 0:1])
        k.copy(col[:], p2[:, 0:1])
    nRn = g("nRn", 1)
    k.ts(nRn[:], Rn[:], -1.0, ALU.mult)
    M = g("M")
    k.ts(M[:], cm[:], Rc[:, 0:1], ALU.max)
    fI = g("fI")
    k.act(fI[:], M[:], AF.Exp, bias=Rn[:, 0:1], scale=-1.0)
    fE = g("fE")
    k.act(fE[:], M[:], AF.Exp, bias=Rc[:, 0:1], scale=-1.0)
    wk = g("wk")
    k.act(wk[:], a[:], AF.Exp, bias=nRn[:, 0:1], scale=1.0)
    thr = g("thr")
    k.tt(thr[:], Bg[:], M[:], ALU.add)
    k.act(thr[:], thr[:], AF.Exp, scale=-1.0)
    dec = g("dec", 1)
    k.tt(dec[:], Rc[:], Rn[:], ALU.subtract)
    k.act(dec[:], dec[:], AF.Exp)
    tabs = {}
    for nm, src in (("fI", fI), ("fE", fE), ("wk", wk), ("thr", thr)):
        p2 = k.psum()
        k.tr(p2[0:64, 0:128], src[:], identf[:])
        t = k.sb(nm + "_T", [64, 128], F32)
        k.copy(t[:], p2[0:64, 0:128])
        tabs[nm] = t
    p2 = k.psum()
    k.tr(p2[0:1, 0:128], dec[:], identf[:])
    drow = k.sb("drow", [1, 128], F32)
    k.copy(drow[:], p2[0:1, 0:128])
    p3 = k.psum()
    k.mm(p3[0:64, 0:128], onesf[0:1, 0:64], drow[:])
    dec_rep = k.sb("dec_rep", [64, 128], F32)
    k.copy(dec_rep[:], p3[0:64, 0:128])

    from_bc = lambda ap, shape, axis: ap.unsqueeze(axis).broadcast_to(list(shape))
    G = 8
    Ub = k.sb("Ub", [64, NCH, 65], F32)
    Sball = k.sb("Sball", [64, NCH, 65], BF16)
    for c0 in range(0, NCH, G):
        bT = k.psum(0, 2)
        pkb = bT[:].bitcast(BF16)
        for i in range(G):
            c = c0 + i
            t0 = c * L
            k.tr(V(pkb[0:64, i * 64:(i + 1) * 64], bT.name), V(kT[:, t0:t0 + L], tk("k", t0)), ident[0:64, 0:64])
        Kt = k.pool("Kt8", [64, G, 64], BF16, 2)
        k.tt(Kt[:], V(pkb[0:64, 0:G * 64].rearrange("p (g d) -> p g d", g=G), bT.name),
             V(from_bc(tabs["wk"][:, c0:c0 + G], [64, G, 64], 2), "wk_T"), ALU.mult)
        for hgrp in range(2):
            bU = k.psum(2, 6)
            for i4 in range(4):
                i = hgrp * 4 + i4
                c = c0 + i
                k.mm(bU[0:64, i4 * 65:(i4 + 1) * 65], V(Kt[:, i, :], Kt.name), V(vx[:, c, :], vxk(c)))
            cc = c0 + hgrp * 4
            k.copy(V(Ub[:, cc:cc + 4, :], "Ub%d" % (cc // 16)), bU[0:64, 0:260].rearrange("p (g d) -> p g d", g=4), eng='act')
    Sf = [k.sb("Sf0", [64, 65], F32), k.sb("Sf1", [64, 65], F32)]
    k.memset(Sf[0][:], 0.0)
    for c in range(NCH):
        cur, nxt = Sf[c % 2], Sf[(c + 1) % 2]
        k.copy(V(Sball[:, c, :], "Sb%d" % (c // 16)), cur[:], eng='pool')
        if c < NCH - 1:
            k.stt(nxt[:], cur[:], dec_rep[:, c:c + 1], V(Ub[:, c, :], "Ub%d" % (c // 16)), ALU.mult, ALU.add)
    hout = None
    for c0 in range(0, NCH, G):
        cs = list(range(c0, c0 + G))
        bk = {c: k.bank(c % 8) for c in cs}
        for c in cs:
            t0 = c * L
            k.mm(bk[c][0:64, 0:64], V(kT[:, t0:t0 + L], tk("k", t0)), V(qT[:, t0:t0 + L], tk("q", t0)))
        Pt = k.pool("Pt8", [64, G, 64], BF16, 2)
        for i, c in enumerate(cs):
            k.stt(V(Pt[:, i, :], "%s_%d" % (Pt.name, i)), bk[c][0:64, 0:64], tabs["wk"][:, c:c + 1], maskC[:], ALU.mult, ALU.mult)
        for i, c in enumerate(cs):
            t0 = c * L
            k.mm(bk[c][0:64, 128:193], V(Pt[:, i, :], "%s_%d" % (Pt.name, i)), V(vx[:, c, :], vxk(c)))
            k.mm(bk[c][0:64, 256:321], V(qT[:, t0:t0 + L], tk("q", t0)), V(Sball[:, c, :], "Sb%d" % (c // 16)))
        tmp = k.pool("ctmp8", [64, G, 65], F32, 2)
        for i, c in enumerate(cs):
            k.ts(V(tmp[:, i, :], "%s_%d" % (tmp.name, i)), bk[c][0:64, 256:321], tabs["fE"][:, c:c + 1], ALU.mult)
        tot = k.pool("ctot8", [64, G, 65], F32, 2)
        for i, c in enumerate(cs):
            k.stt(V(tot[:, i, :], "%s_%d" % (tot.name, i)), bk[c][0:64, 128:193], tabs["fI"][:, c:c + 1],
                  V(tmp[:, i, :], "%s_%d" % (tmp.name, i)), ALU.mult, ALU.add)
        totk = tuple("%s_%d" % (tot.name, i) for i in range(G))
        den8 = V(tot[:, :, 64:65].rearrange("p g o -> p (g o)"), totk)
        dm = k.pool("cdm8", [64, G], F32, 2)
        k.stt(dm[:], den8, -1.0, den8, ALU.mult, ALU.max)
        k.tt(dm[:], dm[:], tabs["thr"][:, c0:c0 + G], ALU.max)
        k.recip(dm[:], dm[:])
        if c0 % 16 == 0:
            hout = k.pool("hout", [64, 16, 64], F32, 2)
        k.tt(V(hout[:, c0 % 16:c0 % 16 + G, :], hout.name), V(tot[:, :, 0:64], totk),
             V(from_bc(dm[:], [64, G, 64], 2), dm.name), ALU.mult, eng='pool')
        if c0 % 16 == 16 - G:
            cc = c0 + G - 16
            k.dma(V(hc[cc * 64:(cc + 16) * 64, :].rearrange("(c s) d -> s c d", s=64), uk("o")), hout[:])


import math

S = 8192
TC = 64
NC_ = S // TC


def host_constsD():
    seg = np.ones((128, TC), np.float32)
    seg[:, 0] = 0.0
    return {"c_jvec": np.broadcast_to(np.arange(65, dtype=np.float32)[None, :], (128, 65)).copy(),
            "c_seg": seg, "c_identf": np.eye(128, dtype=np.float32)}


def bc(ap, shape, axis):
    return ap.unsqueeze(axis).broadcast_to(list(shape))


def mixer_D(k, ident, ones):
    du = k.din("du_g", [2, 32, S], BF16)
    lam = k.din("d_lam", [128, 2, 3], F32)
    bmat = k.din("d_b", [128, 2, 2, 16], F32)
    cmat = k.din("d_c", [128, 2, 2, 16], F32)
    dsk = k.din("d_sk", [32, 2], F32)
    jvd = k.din("c_jvec", [128, 65], F32)
    segd = k.din("c_seg", [128, TC], F32)
    identd = k.din("c_identf", [128, 128], F32)
    ys = k.dout("ys5", [2, 32, S], F32)

    identf = k.sb("identf", [128, 128], F32)
    k.dma(identf[:], identd)
    jv = k.sb("jv", [128, 65], F32)
    k.dma(jv[:], jvd)
    seg1 = k.sb("seg1", [128, TC], BF16)
    k.dma(seg1[:], segd, q='pool')
    seg = k.sb("seg", [128, NC_, TC], BF16)
    k.copy(seg[:], bc(seg1[:], [128, NC_, TC], 1), eng='pool')
    lam_s = k.sb("lam_s", [128, 2, 3], F32)
    k.dma(lam_s[:], lam)
    b_s = k.sb("b_s", [128, 2, 2, 16], F32)
    k.dma(b_s[:], bmat)
    c_s = k.sb("c_s", [128, 2, 2, 16], F32)
    k.dma(c_s[:], cmat)
    dsk_s = k.sb("dsk_s", [32, 2], F32)
    k.dma(dsk_s[:], dsk)

    arena = k.sb("arena", [128, 2 * S], BF16)
    Zre = V(arena[:, 0:S], "arena")
    Zim = V(arena[:, S:2 * S], "arena")
    ybuf = V(arena[0:32, :].bitcast(F32), "arena")
    Cre = k.sb("Cre", [128, S], BF16)
    Cim = k.sb("Cim", [128, S], BF16)
    Wtab = k.sb("Wtab", [128, TC, 2, 32], F32)
    WT = k.sb("WT", [32, TC, 2, 128], BF16)
    G = k.sb("G", [128, TC + 1, 2, 32], BF16)
    u = k.sb("u", [32, S], BF16)
    dskd = k.sb("dskd", [32, 32], BF16)

    _cols = {}

    def col(name):
        if name not in _cols:
            _cols[name] = k.sb(name, [128, 1], F32)
        return _cols[name]

    for t in range(2):
        k.dma(u[:], du[t])
        lr = lam_s[:, t, 0:1]
        li = lam_s[:, t, 1:2]
        dt = col("dt")
        k.act(dt[:], lam_s[:, t, 2:3], AF.Exp)
        ldt = col("ldt")
        k.tt(ldt[:], lr, dt[:], ALU.mult)
        nldt = col("nldt")
        k.ts(nldt[:], ldt[:], -1.0, ALU.mult)
        wdt = col("wdt")
        k.tt(wdt[:], li, dt[:], ALU.mult)
        magp = k.sb("magp", [128, 65], F32) if t == 0 else magp
        magn = k.sb("magn", [128, 65], F32) if t == 0 else magn
        k.act(magp[:], jv[:], AF.Exp, scale=ldt[:, 0:1])
        k.act(magn[:], jv[:], AF.Exp, scale=nldt[:, 0:1])
        ang = k.sb("angd", [128, 65], F32) if t == 0 else ang
        k.ts(ang[:], jv[:], wdt[:, 0:1], ALU.mult)
        cosj = k.sb("cosj", [128, 65], F32) if t == 0 else cosj
        sinj = k.sb("sinj", [128, 65], F32) if t == 0 else sinj
        sa = k.sb("tga", [128, 65], F32) if t == 0 else sa
        si = k.sb("tgi", [128, 65], I32) if t == 0 else si
        trig(k, cosj, ang, 128, 65, math.pi / 2, sa, si)
        trig(k, sinj, ang, 128, 65, 0.0, sa, si)
        are = k.sb("are", [128, 65], F32) if t == 0 else are
        aim = k.sb("aim", [128, 65], F32) if t == 0 else aim
        nre = k.sb("nre", [128, 65], F32) if t == 0 else nre
        nim = k.sb("nim", [128, 65], F32) if t == 0 else nim
        k.tt(are[:], magp[:], cosj[:], ALU.mult)
        k.tt(aim[:], magp[:], sinj[:], ALU.mult)
        k.tt(nre[:], magn[:], cosj[:], ALU.mult)
        k.stt(nim[:], magn[:], -1.0, sinj[:], ALU.mult, ALU.mult)
        den = col("den")
        k.tt(den[:], lr, lr, ALU.mult)
        k.stt(den[:], li, li, den[:], ALU.mult, ALU.add)
        k.recip(den[:], den[:])
        ar1 = col("ar1")
        k.ts(ar1[:], are[:, 1:2], -1.0, ALU.add)
        fre = col("fre")
        k.tt(fre[:], ar1[:], lr, ALU.mult)
        k.stt(fre[:], aim[:, 1:2], li, fre[:], ALU.mult, ALU.add)
        k.tt(fre[:], fre[:], den[:], ALU.mult)
        fim = col("fim")
        k.tt(fim[:], aim[:, 1:2], lr, ALU.mult)
        tmpc = col("tmpc")
        k.tt(tmpc[:], ar1[:], li, ALU.mult)
        k.tt(fim[:], fim[:], tmpc[:], ALU.subtract)
        k.tt(fim[:], fim[:], den[:], ALU.mult)
        nfim = col("nfim")
        k.ts(nfim[:], fim[:], -1.0, ALU.mult)
        Bb = k.sb("Bb", [128, 2, 16], F32) if t == 0 else Bb
        bre, bim = b_s[:, t, 0, :], b_s[:, t, 1, :]
        k.ts(V(Bb[:, 0, :], "Bb"), bre, fre[:, 0:1], ALU.mult)
        k.stt(V(Bb[:, 0, :], "Bb"), bim, nfim[:, 0:1], V(Bb[:, 0, :], "Bb"), ALU.mult, ALU.add)
        k.ts(V(Bb[:, 1, :], "Bb"), bim, fre[:, 0:1], ALU.mult)
        k.stt(V(Bb[:, 1, :], "Bb"), bre, fim[:, 0:1], V(Bb[:, 1, :], "Bb"), ALU.mult, ALU.add)
        k.memset(Wtab[:], 0.0, eng='pool')
        k.memset(G[:], 0.0, eng='pool')
        tA = k.sb("tA", [128, TC + 1, 16], F32) if t == 0 else tA
        tB = k.sb("tB", [128, TC + 1, 16], F32) if t == 0 else tB
        for gi in range(2):
            ps_ = slice(gi * 64, (gi + 1) * 64)
            cs_ = slice(gi * 16, (gi + 1) * 16)
            shp = [64, TC, 16]
            n_re = bc(nre[ps_, 0:TC], shp, 2)
            n_im = bc(nim[ps_, 0:TC], shp, 2)
            B_re = bc(V(Bb[ps_, 0, :], "Bb").ap, shp, 1)
            B_im = bc(V(Bb[ps_, 1, :], "Bb").ap, shp, 1)
            tAv = V(tA[ps_, 0:TC, :], "tA")
            tBv = V(tB[ps_, 0:TC, :], "tB")
            k.tt(tAv, V(n_re, "nre"), V(B_re, "Bb"), ALU.mult)
            k.tt(tBv, V(n_im, "nim"), V(B_im, "Bb"), ALU.mult)
            k.tt(V(Wtab[ps_, :, 0, cs_], "Wtab"), tAv, tBv, ALU.subtract)
            k.tt(tAv, V(n_re, "nre"), V(B_im, "Bb"), ALU.mult)
            k.tt(tBv, V(n_im, "nim"), V(B_re, "Bb"), ALU.mult)
            k.tt(V(Wtab[ps_, :, 1, cs_], "Wtab"), tAv, tBv, ALU.add)
            shp2 = [64, TC + 1, 16]
            a_re = bc(are[ps_, :], shp2, 2)
            a_im = bc(aim[ps_, :], shp2, 2)
            C_re = bc(c_s[ps_, t, 0, :], shp2, 1)
            C_im = bc(c_s[ps_, t, 1, :], shp2, 1)
            tAv2 = V(tA[ps_, :, :], "tA")
            tBv2 = V(tB[ps_, :, :], "tB")
            k.tt(tAv2, V(a_re, "are"), V(C_re, "c_s"), ALU.mult)
            k.tt(tBv2, V(a_im, "aim"), V(C_im, "c_s"), ALU.mult)
            k.tt(V(G[ps_, :, 0, cs_], "G"), tAv2, tBv2, ALU.subtract)
            k.tt(tAv2, V(a_re, "are"), V(C_im, "c_s"), ALU.mult)
            k.tt(tBv2, V(a_im, "aim"), V(C_re, "c_s"), ALU.mult)
            k.stt(V(G[ps_, :, 1, cs_], "G"), tAv2, -1.0, tBv2, ALU.mult, ALU.subtract)
        for i4 in range(TC // 2):
            pt = k.psum()
            for q_ in range(2):
                i = i4 * 2 + q_
                for ri in range(2):
                    k.tr(pt[0:32, (q_ * 2 + ri) * 128:(q_ * 2 + ri + 1) * 128], V(Wtab[:, i, ri, :], "Wtab"), identf[:])
            k.copy(V(WT[:, i4 * 2:i4 * 2 + 2, :, :], "WT"), pt[0:32, :].rearrange("p (i r m) -> p i r m", i=2, r=2),
                   eng=('act' if i4 % 2 else 'dve'))
        idb = k.sb("idb", [32, 32], F32) if t == 0 else idb
        k.ts(idb[:], identf[0:32, 0:32], dsk_s[:, t:t + 1], ALU.mult)
        k.copy(dskd[:], idb[:])
        for i4 in range(TC // 4):
            pzr = k.psum()
            pzi = k.psum()
            for q_ in range(4):
                i = i4 * 4 + q_
                rhs = u[:, i:i + TC * (NC_ - 1) + 1:TC]
                k.mm(pzr[:, q_ * 128:(q_ + 1) * 128], V(WT[:, i, 0, :], "WT"), rhs)
                k.mm(pzi[:, q_ * 128:(q_ + 1) * 128], V(WT[:, i, 1, :], "WT"), rhs)
            for (pz, Z, e_) in ((pzr, Zre, 'dve'), (pzi, Zim, 'act')):
                dst = V(Z.ap.rearrange("p (c i) -> p c i", i=TC)[:, :, i4 * 4:i4 * 4 + 4], "arena")
                k.copy(dst, pz[:].rearrange("p (i c) -> p c i", i=4), eng=e_)
        segf = V(seg[:].rearrange("p c i -> p (c i)"), "seg")
        k.scan(Cre[:], segf, Zre, 0.0, ALU.mult, ALU.add)
        k.scan(Cim[:], segf, Zim, 0.0, ALU.mult, ALU.add)
        def pl(name):
            return k.sb(name, [128, NC_], F32) if t == 0 else getattr(k, "_pl_" + name)
        Ere, Eim = pl("Ere"), pl("Eim")
        X0r, X0i, X1r, X1i = pl("X0r"), pl("X0i"), pl("X1r"), pl("X1i")
        for nm_, o_ in (("Ere", Ere), ("Eim", Eim), ("X0r", X0r), ("X0i", X0i), ("X1r", X1r), ("X1i", X1i)):
            setattr(k, "_pl_" + nm_, o_)
        k.copy(Ere[:], Cre[:, TC - 1:TC - 1 + TC * (NC_ - 1) + 1:TC])
        k.copy(Eim[:], Cim[:, TC - 1:TC - 1 + TC * (NC_ - 1) + 1:TC])
        a63r, a63i = are[:, 63:64], aim[:, 63:64]
        na63i = col("na63i")
        k.ts(na63i[:], a63i, -1.0, ALU.mult)
        k.ts(X0r[:], Ere[:], a63r, ALU.mult)
        k.stt(X0r[:], Eim[:], na63i[:, 0:1], X0r[:], ALU.mult, ALU.add)
        k.ts(X0i[:], Ere[:], a63i, ALU.mult)
        k.stt(X0i[:], Eim[:], a63r, X0i[:], ALU.mult, ALU.add)
        Akr, Aki, nAki, Ak2 = col("Akr"), col("Aki"), col("nAki"), col("Ak2")
        k.copy(Akr[:], are[:, 64:65])
        k.copy(Aki[:], aim[:, 64:65])
        cur = (X0r, X0i)
        nxt = (X1r, X1i)
        for st_ in range(7):
            s = 1 << st_
            k.ts(nAki[:], Aki[:], -1.0, ALU.mult)
            cr, ci_ = cur
            nr, ni = nxt
            k.copy(nr[:, 0:s], cr[:, 0:s])
            k.copy(ni[:, 0:s], ci_[:, 0:s])
            n_ = NC_ - s
            k.stt(nr[:, s:], cr[:, 0:n_], Akr[:, 0:1], cr[:, s:], ALU.mult, ALU.add)
            k.stt(nr[:, s:], ci_[:, 0:n_], nAki[:, 0:1], nr[:, s:], ALU.mult, ALU.add)
            k.stt(ni[:, s:], ci_[:, 0:n_], Akr[:, 0:1], ci_[:, s:], ALU.mult, ALU.add)
            k.stt(ni[:, s:], cr[:, 0:n_], Aki[:, 0:1], ni[:, s:], ALU.mult, ALU.add)
            cur, nxt = nxt, cur
            if st_ < 6:
                k.tt(Ak2[:], Aki[:], Aki[:], ALU.mult)
                k.tt(Aki[:], Akr[:], Aki[:], ALU.mult)
                k.ts(Aki[:], Aki[:], 2.0, ALU.mult)
                k.stt(Akr[:], Akr[:], Akr[:, 0:1], Ak2[:], ALU.mult, ALU.subtract)
        Xpr = k.sb("Xpr", [128, NC_], BF16) if t == 0 else Xpr
        Xpi = k.sb("Xpi", [128, NC_], BF16) if t == 0 else Xpi
        k.memset(V(Xpr[:, 0:1], "Xpr"), 0.0)
        k.memset(V(Xpi[:, 0:1], "Xpi"), 0.0)
        k.copy(V(Xpr[:, 1:NC_], "Xpr"), cur[0][:, 0:NC_ - 1])
        k.copy(V(Xpi[:, 1:NC_], "Xpi"), cur[1][:, 0:NC_ - 1])
        for j4 in range(TC // 4):
            py = k.psum()
            for q_ in range(4):
                j = j4 * 4 + q_
                o_ = py[0:32, q_ * 128:(q_ + 1) * 128]
                sl = slice(j, j + TC * (NC_ - 1) + 1, TC)
                k.mm(o_, V(G[:, j, 0, :], "G"), Cre[:, sl], start=True, stop=False)
                k.mm(o_, V(G[:, j, 1, :], "G"), Cim[:, sl], start=False, stop=False)
                k.mm(o_, V(G[:, j + 1, 0, :], "G"), Xpr[:], start=False, stop=False)
                k.mm(o_, V(G[:, j + 1, 1, :], "G"), Xpi[:], start=False, stop=False)
                k.mm(o_, dskd[:], u[:, sl], start=False, stop=True)
            dst = V(ybuf.ap.rearrange("p (c i) -> p c i", i=TC)[:, :, j4 * 4:j4 * 4 + 4], "arena")
            k.copy(dst, py[0:32, :].rearrange("p (i c) -> p c i", i=4), eng=('act' if j4 % 2 else 'dve'))
        k.dma(V(ys[t], uk("o")), ybuf)


FGROUPS = [("a_z", 4608, 512), ("b_z", 5760, 512), ("c_o", 7304, 512), ("c_z", 7816, 512), ("d_z", 8840, 512), ("gate", 9352, 8192)]
FCOLS = []
FOFF = {}
for (_n, _o, _w) in FGROUPS:
    FOFF[_n] = len(FCOLS)
    FCOLS.extend(range(_o, _o + _w))
FW = len(FCOLS)
GELU_C = 1.5957691216057308


def build_F():
    k = K()
    x = k.din("x", [T, 2048])
    w_f = k.din("w_f", [2048, FW])
    norm_g = k.din("norm_g", [1, 2048])
    mbd = k.din("merge_b", [128, 64])
    gbd = k.din("glu_b", [128, 4])
    glu_w = k.din("glu_w", [512, 512])
    w_up = k.din("w_up", [4, 512, 2048])
    w_out = k.din("w_out", [2048, 2048])
    yaT = k.din("yaT", [512, T])
    ybT = k.din("ybT", [512, T])
    ycT = k.din("ycT", [512, T])
    y5T = k.din("y5T", [512, T])
    xo = k.dout("xo", [T, 2048])

    ident, ones = consts(k)
    mT = k.sb("mT", [128, 16, T], BF16)
    mk = ("mT_0", "mT_1")
    ext = ([V(mT[:, 0:4, :].rearrange("p a b -> p (a b)").bitcast(F32), mk),
            V(mT[:, 10:14, :].rearrange("p a b -> p (a b)").bitcast(F32), mk)],
           [V(mT[:, 8:10, :].rearrange("p a b -> p (a b)"), mk),
            V(mT[:, 14:16, :].rearrange("p a b -> p (a b)"), mk)],
           V(mT[:, 4:8, :].rearrange("p a b -> p (a b)").bitcast(F32), mk))
    hT = norm_hT(k, x, norm_g, ident, ext=ext)
    mb = k.sb("mb", [128, 64], F32)
    k.dma(mb[:], mbd)
    gb = k.sb("gb", [128, 4], F32)
    k.dma(gb[:], gbd)
    ysT = k.sb("ysT", [128, 16, T], BF16)

    def wload(c0, ncols):
        wt = k.pool("wt", [128, 16, 512], BF16, 2)
        k.dma(wt[:, :, 0:ncols], w_f[:, c0:c0 + ncols].rearrange("(kt p) c -> p kt c", p=128), q='pool')
        return wt

    def proj(wt, wc0, half):
        ps = k.psum()
        for kt in range(16):
            k.mm(ps[:], wt[:, kt, wc0:wc0 + 128], V(hT[:, kt, half * 512:(half + 1) * 512], hk(half)),
                 start=(kt == 0), stop=(kt == 15))
        return ps

    def ytile(src, wt_, half):
        t = k.pool("yt", [128, 512], F32, 2)
        k.dma(t[:], src[wt_ * 128:(wt_ + 1) * 128, half * 512:(half + 1) * 512])
        return t

    def yk(n, wt_, half):
        return "ysT_%d_%d" % (n * 4 + wt_, half)

    wz = {nm: wload(FOFF[nm], 512) for nm in ("a_z", "b_z")}
    for n, (nm, src) in enumerate((("a_z", yaT), ("b_z", ybT))):
        for wt_ in range(4):
            for half in range(NH):
                ps = proj(wz[nm], wt_ * 128, half)
                s = k.pool("sz", [128, 512], F32, 3)
                k.act(s[:], ps[:], AF.Silu)
                y = ytile(src, wt_, half)
                k.tt(V(ysT[:, n * 4 + wt_, half * 512:(half + 1) * 512], yk(n, wt_, half)), s[:], y[:], ALU.mult)
    wo = wload(FOFF["c_o"], 512)
    wc = wload(FOFF["c_z"], 512)
    for wt_ in range(4):
        for half in range(NH):
            ps = proj(wo, wt_ * 128, half)
            so = k.pool("sz", [128, 512], F32, 3)
            k.act(so[:], ps[:], AF.Sigmoid)
            ps2 = proj(wc, wt_ * 128, half)
            s = k.pool("sz", [128, 512], F32, 3)
            k.act(s[:], ps2[:], AF.Silu)
            y = ytile(ycT, wt_, half)
            k.tt(s[:], s[:], so[:], ALU.mult)
            k.tt(V(ysT[:, 8 + wt_, half * 512:(half + 1) * 512], yk(2, wt_, half)), s[:], y[:], ALU.mult)
    gdb = k.sb("gdb", [128, 4, T], BF16)
    for wt_ in range(4):
        for half in range(NH):
            y = ytile(y5T, wt_, half)
            t1 = k.pool("sz", [128, 512], F32, 3)
            k.tt(t1[:], y[:], y[:], ALU.mult)
            k.ts(t1[:], t1[:], 0.044715, ALU.mult, 1.0, ALU.add)
            k.tt(t1[:], t1[:], y[:], ALU.mult)
            k.act(t1[:], t1[:], AF.Sigmoid, scale=GELU_C)
            k.tt(V(gdb[:, wt_, half * 512:(half + 1) * 512], "gdb%d_%d" % (wt_, half)), t1[:], y[:], ALU.mult)
    wg = k.sb("wglu", [128, 4, 512], BF16)
    k.dma(wg[:], glu_w.rearrange("(kt p) c -> p kt c", p=128), q='pool')
    wd = wload(FOFF["d_z"], 512)
    for ob in range(4):
        for half in range(NH):
            ps = k.psum()
            for kt in range(4):
                k.mm(ps[:], wg[:, kt, ob * 128:(ob + 1) * 128],
                     V(gdb[:, kt, half * 512:(half + 1) * 512], "gdb%d_%d" % (kt, half)), start=(kt == 0), stop=(kt == 3))
            sg = k.pool("sz", [128, 512], F32, 3)
            k.act(sg[:], ps[:], AF.Sigmoid, bias=gb[:, ob:ob + 1])
            ps2 = proj(wd, ob * 128, half)
            s = k.pool("sz", [128, 512], F32, 3)
            k.act(s[:], ps2[:], AF.Silu)
            k.tt(s[:], s[:], sg[:], ALU.mult)
            k.tt(V(ysT[:, 12 + ob, half * 512:(half + 1) * 512], yk(3, ob, half)), s[:],
                 V(gdb[:, ob, half * 512:(half + 1) * 512], "gdb%d_%d" % (ob, half)), ALU.mult)

    g0 = FOFF["gate"]
    w_up_r = w_up.rearrange("n (wt p) d -> n p wt d", p=128)
    for db in range(16):
        wt = k.pool("wt", [128, 16, 512], BF16, 2)
        wu = k.pool("wup", [128, 4, 4, 128], BF16, 2)
        for n in range(4):
            c0 = g0 + n * 2048 + db * 128
            k.dma(wt[:, :, n * 128:(n + 1) * 128],
                  w_f[:, c0:c0 + 128].rearrange("(kt p) c -> p kt c", p=128), q='pool')
            k.dma(wu[:, n, :, :], w_up_r[n, :, :, db * 128:(db + 1) * 128], q='pool')
        for half in range(NH):
            acc = k.pool("acc", [128, 512], F32, 2)
            for n in range(4):
                pg = k.psum()
                for kt in range(16):
                    k.mm(pg[:], wt[:, kt, n * 128:(n + 1) * 128],
                         V(hT[:, kt, half * 512:(half + 1) * 512], hk(half)), start=(kt == 0), stop=(kt == 15))
                pu = k.psum()
                for wt_ in range(4):
                    k.mm(pu[:], wu[:, n, wt_, :],
                         V(ysT[:, n * 4 + wt_, half * 512:(half + 1) * 512], yk(n, wt_, half)), start=(wt_ == 0), stop=(wt_ == 3))
                sg = k.pool("sz", [128, 512], F32, 3)
                k.act(sg[:], pg[:], AF.Sigmoid, bias=mb[:, n * 16 + db:n * 16 + db + 1])
                if n == 0:
                    k.tt(acc[:], sg[:], pu[:], ALU.mult)
                elif n < 3:
                    k.tt(sg[:], sg[:], pu[:], ALU.mult)
                    k.tt(acc[:], acc[:], sg[:], ALU.add)
                else:
                    k.tt(sg[:], sg[:], pu[:], ALU.mult)
                    k.tt(V(mT[:, db, half * 512:(half + 1) * 512], "mT_%d" % half), acc[:], sg[:], ALU.add)

    for cg in range(4):
        wt = k.pool("wt", [128, 16, 512], BF16, 2)
        k.dma(wt[:], w_out[:, cg * 512:(cg + 1) * 512].rearrange("(kt p) c -> p kt c", p=128), q='pool')
        for tt in range(T // 128):
            ps = k.psum()
            for kt in range(16):
                k.mm(ps[:], V(mT[:, kt, tt * 128:(tt + 1) * 128], "mT_%d" % (tt // 4)), wt[:, kt, :], start=(kt == 0), stop=(kt == 15))
            xt = k.pool("yt", [128, 512], F32, 2)
            k.dma(xt[:], x[tt * 128:(tt + 1) * 128, cg * 512:(cg + 1) * 512])
            o = k.pool("sz", [128, 512], F32, 3)
            k.tt(o[:], xt[:], ps[:], ALU.add)
            k.dma(V(xo[tt * 128:(tt + 1) * 128, cg * 512:(cg + 1) * 512], uk("o")), o[:])
    return k.finish()


_PROGS = {}
import sys as _sys
import time as _time
_T0 = [_time.time()]


def _log(msg):
    print("[kernel %.1fs] %s" % (_time.time() - _T0[0], msg), file=_sys.stderr, flush=True)


def _prog(name):
    if name not in _PROGS:
        if name == "P":
            _PROGS[name] = build_P()
        elif name == "AB":
            _PROGS[name] = build_M("AB")
        elif name == "C":
            _PROGS[name] = build_M("C")
        elif name == "D":
            _PROGS[name] = build_M("D")
        elif name == "F":
            _PROGS[name] = build_F()
    return _PROGS[name]


def _c(a, dt=None):
    return np.ascontiguousarray(a) if dt is None else np.ascontiguousarray(a, dtype=dt)


def _prep_C(core, cqk_full, cv_full, cif_full, conv_w, conv_b, i_b, f_b):
    hc, dh = core // 2, core % 2
    qch = slice(hc * 64, hc * 64 + 64)
    kch = slice(256 + hc * 64, 256 + hc * 64 + 64)
    z3 = np.zeros((64, 3), np.float32)
    m = dict(cq_h=_c(np.concatenate([z3, cqk_full[qch]], 1)), ck_h=_c(np.concatenate([z3, cqk_full[kch]], 1)),
             cw_q=_c(conv_w[:, qch].T), cb_q=_c(conv_b[qch][:, None]),
             cw_k=_c(conv_w[:, kch].T), cb_k=_c(conv_b[kch][:, None]),
             cv_h=_c(cv_full[:, hc * 128 + dh * 64: hc * 128 + dh * 64 + 64]),
             gi=_c(cif_full[hc].reshape(128, 64)), gf=_c(cif_full[4 + hc].reshape(128, 64)),
             gb=_c(np.stack([i_b[hc:hc + 1], f_b[hc:hc + 1]], axis=1), np.float32))
    m.update(host_constsC())
    return m


def _prep_D(core, du_full, W):
    g0 = 4 * core
    lam = np.zeros((128, 2, 3), np.float32)
    bm = np.zeros((128, 2, 2, 16), np.float32)
    cm = np.zeros((128, 2, 2, 16), np.float32)
    dsk = np.zeros((32, 2), np.float32)
    du = np.zeros((2, 32, 8192), du_full.dtype)
    for t in range(2):
        for gi in range(2):
            g = g0 + 2 * t + gi
            ps = slice(gi * 64, gi * 64 + 64)
            lam[ps, t, 0] = W['d_lam_re'][g]
            lam[ps, t, 1] = W['d_lam_im'][g]
            lam[ps, t, 2] = W['d_log_dt'][g]
            bm[ps, t, 0] = W['d_b_re'][g]
            bm[ps, t, 1] = W['d_b_im'][g]
            cm[ps, t, 0] = W['d_c_re'][g].T
            cm[ps, t, 1] = W['d_c_im'][g].T
            dsk[gi * 16:(gi + 1) * 16, t] = W['d_skip'][g * 16:(g + 1) * 16]
            du[t, gi * 16:(gi + 1) * 16] = du_full[g * 16:(g + 1) * 16]
    m = dict(du_g=du, d_lam=lam, d_b=bm, d_c=cm, d_sk=dsk)
    m.update(host_constsD())
    return m


def kernel(x, positions, norm_g, w_in, a_qn_g, a_kn_g, b_cq_g, b_ckv_g, b_w_uq, b_w_ukv, b_qn_g, b_kn_g,
           c_conv_w, c_conv_b, c_i_b, c_f_b, d_lam_re, d_lam_im, d_log_dt, d_b_re, d_b_im, d_c_re, d_c_im,
           d_skip, d_glu_w, d_glu_b, w_up, merge_b, w_out):
    f32 = np.float32
    cur = np.asarray(x, f32)[0]
    pos = np.asarray(positions).astype(np.int32)
    eye = np.eye(128, dtype=f32)
    invf = (10000.0 ** (-(np.arange(0, 64, 2, dtype=f32)) / 64)).astype(f32)
    c_invf = np.concatenate([invf, invf])[:, None].astype(f32)
    c_sgn = np.concatenate([-np.ones(32), np.ones(32)])[:, None].astype(f32)
    biasA = [host_biasA(c) for c in range(NCORES)]
    maskB = [host_maskB(p) for p in range(2)]
    TS = [slice(c * T, (c + 1) * T) for c in range(NCORES)]
    for l in range(4):
        g = lambda a: np.asarray(a[l], f32)
        w_p = _c(g(w_in)[:, PCOLS])
        maps = []
        for c in range(NCORES):
            m = dict(c_ident=eye, c_invf=c_invf, c_sgn=c_sgn, x=_c(cur[TS[c]]), w_p=w_p, pos=_c(pos[:, TS[c]]),
                     norm_g=_c(g(norm_g)[None]), a_qn_g=_c(g(a_qn_g)[None]), a_kn_g=_c(g(a_kn_g)[None]),
                     b_cq_g=_c(g(b_cq_g)[None]), b_ckv_g=_c(g(b_ckv_g)[None]), b_qn_g=_c(g(b_qn_g)[None]),
                     b_kn_g=_c(g(b_kn_g)[None]), b_w_uq=_c(g(b_w_uq)), b_w_ukv=_c(g(b_w_ukv)))
            maps.append(m)
        _log("layer %d P" % l)
        rp = run(_prog("P"), maps).results
        del w_p, maps
        qfull = np.concatenate([rp[c]["qb"] for c in range(NCORES)], axis=2)
        kfull = np.concatenate([rp[c]["kb"] for c in range(NCORES)], axis=2)
        vfull = np.concatenate([rp[c]["vb"] for c in range(NCORES)], axis=0)
        maps = []
        toks = []
        for c in range(NCORES):
            def ext(name, axis):
                parts = []
                for cc in (c - 2, c - 1, c):
                    a = rp[max(cc, 0)][name]
                    parts.append(a if cc >= 0 else np.zeros_like(a))
                return _c(np.concatenate(parts, axis=axis))
            h, par = c // 2, c % 2
            tk_ = np.concatenate([np.arange(512 * (2 * j + par), 512 * (2 * j + par) + 512) for j in range(8)])
            toks.append(tk_)
            m = dict(c_ident=eye, qa=_c(rp[c]["qa"]), ka_ext=ext("ka", 2), va_ext=ext("va", 0), c_biasA=biasA[c],
                     qb_h=_c(qfull[h][:, tk_]), kb_h=_c(kfull[h]), vb_h=_c(vfull[:, h * 128:(h + 1) * 128]),
                     c_maskB=maskB[par])
            maps.append(m)
        _log("layer %d AB" % l)
        rab = run(_prog("AB"), maps).results
        del maps, qfull, kfull, vfull
        yaT = [rab[c]["ya"] for c in range(NCORES)]
        ybT = np.zeros((512, 8192), f32)
        for c in range(NCORES):
            h = c // 2
            ybT[h * 128:(h + 1) * 128, toks[c]] = rab[c]["yb"]
        cqk_full = np.concatenate([rp[c]["cqk"] for c in range(NCORES)], 1)
        cv_full = np.concatenate([rp[c]["cv"] for c in range(NCORES)], 0)
        cif_full = np.concatenate([rp[c]["cif"] for c in range(NCORES)], 1)
        maps = []
        for c in range(NCORES):
            m = _prep_C(c, cqk_full, cv_full, cif_full, g(c_conv_w), g(c_conv_b), g(c_i_b), g(c_f_b))
            m["c_ident"] = eye
            maps.append(m)
        _log("layer %d C" % l)
        rc = run(_prog("C"), maps).results
        ycT = np.zeros((512, 8192), f32)
        for c in range(NCORES):
            hc_, dh = c // 2, c % 2
            ycT[hc_ * 128 + dh * 64: hc_ * 128 + dh * 64 + 64, :] = rc[c]["hc"].T
        du_full = np.concatenate([rp[c]["du"] for c in range(NCORES)], 1)
        Wd = dict(d_lam_re=g(d_lam_re), d_lam_im=g(d_lam_im), d_log_dt=g(d_log_dt), d_b_re=g(d_b_re), d_b_im=g(d_b_im),
                  d_c_re=g(d_c_re), d_c_im=g(d_c_im), d_skip=g(d_skip))
        maps = []
        for c in range(NCORES):
            m = _prep_D(c, du_full, Wd)
            m["c_ident"] = eye
            maps.append(m)
        _log("layer %d D" % l)
        rd = run(_prog("D"), maps).results
        y5T = np.zeros((512, 8192), f32)
        for c in range(NCORES):
            for t in range(2):
                r0 = (4 * c + 2 * t) * 16
                y5T[r0:r0 + 32, :] = rd[c]["ys5"][t]
        del rp, maps
        w_f = _c(g(w_in)[:, FCOLS])
        mb = _c(g(merge_b).reshape(64, 128).T)
        gbv = _c(g(d_glu_b).reshape(4, 128).T)
        maps = []
        for c in range(NCORES):
            m = dict(c_ident=eye, x=_c(cur[TS[c]]), w_f=w_f, norm_g=_c(g(norm_g)[None]), merge_b=mb, glu_b=gbv,
                     glu_w=_c(g(d_glu_w)), w_up=_c(g(w_up)), w_out=_c(g(w_out)),
                     yaT=_c(yaT[c]), ybT=_c(ybT[:, TS[c]]), ycT=_c(ycT[:, TS[c]]), y5T=_c(y5T[:, TS[c]]))
            maps.append(m)
        _log("layer %d F" % l)
        rf = run(_prog("F"), maps).results
        cur = np.concatenate([rf[c]["xo"] for c in range(NCORES)], axis=0).astype(f32)
        del w_f, maps, rf
    return cur[None].astype(np.float32)
```
